# Optimizing a Trainium2 kernel written in Bass

```python
import jax, jax.numpy as jnp
from jax import lax
import numpy as np

D_MODEL = 1024
BATCH = 8
SEQ = 4096
DEPTH = 2

HEAD_DIM = 64
ROPE_THETA = 10000.0
NORM_EPS = 1e-6
N_BRANCH = 4
BRANCH_W = 4 * HEAD_DIM

A_HEADS = 4
A_BLOCK = 256
A_TOPK = 3
B_HEADS = 4
B_CMP_LEN = 32
B_CMP_STRIDE = 16
B_CMP_HIDDEN = 256
B_SLC_LEN = 64
B_SLC_TOPN = 16
B_WINDOW = 512
C_HEADS = 4
D_HEADS = 4
D_PATTERNS = ((128, 1), (512, 4), (2048, 16))

GATHER_QBLOCK = 32
DENSE_QBLOCK = 128

IN_SIZES = (
    BRANCH_W, BRANCH_W, BRANCH_W, BRANCH_W,
    BRANCH_W, 6 * HEAD_DIM, BRANCH_W, 3 * B_HEADS,
    3 * BRANCH_W, BRANCH_W,
    3 * len(D_PATTERNS) * D_HEADS * HEAD_DIM, BRANCH_W,
    N_BRANCH * D_MODEL,
)
IN_WIDTH = sum(IN_SIZES)

kernel_name = 'hybrid_moba_nsa_stickbreak_dilated'


def _rmsnorm(x, w):
    xf = x.astype(jnp.float32)
    y = xf * lax.rsqrt(jnp.mean(xf * xf, axis=-1, keepdims=True) + NORM_EPS)
    return (y * w.astype(jnp.float32)).astype(x.dtype)


def _rope(x, pos):
    half = HEAD_DIM // 2
    inv = ROPE_THETA ** (-jnp.arange(half, dtype=jnp.float32) / half)
    ang = pos.astype(jnp.float32)[:, None] * inv[None, :]
    cos = jnp.cos(ang).astype(x.dtype)
    sin = jnp.sin(ang).astype(x.dtype)
    x1, x2 = x[..., :half], x[..., half:]
    return jnp.concatenate([x1 * cos - x2 * sin, x1 * sin + x2 * cos], axis=-1)


def _heads(x, n):
    b, s = x.shape[:2]
    return x.reshape(b, s, n, HEAD_DIM).transpose(0, 2, 1, 3)


def _merge_heads(x):
    b, h, s, d = x.shape
    return x.transpose(0, 2, 1, 3).reshape(b, s, h * d)


def _masked_softmax(s, mask):
    s = jnp.where(mask, s, -jnp.inf)
    mx = jnp.max(s, axis=-1, keepdims=True)
    mx = jnp.where(jnp.isfinite(mx), mx, 0.0)
    e = jnp.where(mask, jnp.exp(s - mx), 0.0)
    den = jnp.maximum(jnp.sum(e, axis=-1, keepdims=True), 1e-30)
    return e / den, (mx + jnp.log(den))[..., 0]


def _unblock(y):
    nb, b, h, qb, d = y.shape
    return jnp.moveaxis(y, 0, 2).reshape(b, h, nb * qb, d)


def _banded_attention(q, k, v, window):
    b, h, L, d = q.shape
    hk = k.shape[1]
    g = h // hk
    qb = min(DENSE_QBLOCK, L)
    nblk = -(-L // qb)
    lp = nblk * qb
    q = jnp.pad(q, ((0, 0), (0, 0), (0, lp - L), (0, 0))).reshape(b, hk, g, lp, d)
    kpad = ((0, 0), (0, 0), (window, lp - L), (0, 0))
    k = jnp.pad(k, kpad)
    v = jnp.pad(v, kpad)
    span = window + qb
    scale = d ** -0.5

    def block(q0):
        qs = lax.dynamic_slice_in_dim(q, q0, qb, axis=3)
        ks = lax.dynamic_slice_in_dim(k, q0, span, axis=2)
        vs = lax.dynamic_slice_in_dim(v, q0, span, axis=2)
        s = jnp.einsum('bhgqd,bhkd->bhgqk', qs, ks, preferred_element_type=jnp.float32) * scale
        tq = q0 + jnp.arange(qb)
        tk = q0 - window + jnp.arange(span)
        dist = tq[:, None] - tk[None, :]
        mask = (dist >= 0) & (dist <= window) & (tk[None, :] >= 0)
        p, lse = _masked_softmax(s, mask)
        return jnp.einsum('bhgqk,bhkd->bhgqd', p.astype(vs.dtype), vs), lse

    o, lse = lax.map(block, jnp.arange(nblk) * qb)
    o = jnp.moveaxis(o, 0, 3).reshape(b, h, lp, d)[:, :, :L]
    lse = jnp.moveaxis(lse, 0, 3).reshape(b, h, lp)[:, :, :L]
    return o, lse


def _moba_attention(q, k, v):
    b, h, S, d = q.shape
    nb = -(-S // A_BLOCK)
    sp = nb * A_BLOCK
    pad = ((0, 0), (0, 0), (0, sp - S), (0, 0))
    k = jnp.pad(k, pad)
    v = jnp.pad(v, pad)
    kb = k.reshape(b, h, nb, A_BLOCK, d)
    vb = v.reshape(b, h, nb, A_BLOCK, d)
    k_mean = jnp.mean(kb.astype(jnp.float32), axis=3)
    gate = jnp.einsum('bhsd,bhnd->bhsn', q.astype(jnp.float32), k_mean)
    past = jnp.arange(nb)[None, :] < (jnp.arange(S) // A_BLOCK)[:, None]
    gate = jnp.where(past, gate, -jnp.inf)
    top_score, top_idx = lax.top_k(gate, min(A_TOPK, nb))
    top_ok = top_score > -jnp.inf
    topk = top_idx.shape[-1]
    n_sel = topk * A_BLOCK
    scale = d ** -0.5
    pick = jax.vmap(jax.vmap(lambda blocks, i: blocks[i]))

    def block(q0):
        qs = lax.dynamic_slice_in_dim(q, q0, GATHER_QBLOCK, axis=2)
        ids = lax.dynamic_slice_in_dim(top_idx, q0, GATHER_QBLOCK, axis=2)
        ok = lax.dynamic_slice_in_dim(top_ok, q0, GATHER_QBLOCK, axis=2)
        k_sel = pick(kb, ids).reshape(b, h, GATHER_QBLOCK, n_sel, d)
        v_sel = pick(vb, ids).reshape(b, h, GATHER_QBLOCK, n_sel, d)
        own0 = (q0 // A_BLOCK) * A_BLOCK
        k_own = lax.dynamic_slice_in_dim(k, own0, A_BLOCK, axis=2)
        v_own = lax.dynamic_slice_in_dim(v, own0, A_BLOCK, axis=2)
        tq = q0 + jnp.arange(GATHER_QBLOCK)
        tk = own0 + jnp.arange(A_BLOCK)
        s_sel = jnp.einsum('bhqd,bhqkd->bhqk', qs, k_sel, preferred_element_type=jnp.float32) * scale
        s_own = jnp.einsum('bhqd,bhkd->bhqk', qs, k_own, preferred_element_type=jnp.float32) * scale
        s = jnp.concatenate([s_sel, s_own], axis=-1)
        mask = jnp.concatenate([
            jnp.repeat(ok, A_BLOCK, axis=-1),
            jnp.broadcast_to(tk[None, :] <= tq[:, None], s_own.shape)], axis=-1)
        p, _ = _masked_softmax(s, mask)
        p = p.astype(v.dtype)
        return (jnp.einsum('bhqk,bhqkd->bhqd', p[..., :n_sel], v_sel)
                + jnp.einsum('bhqk,bhkd->bhqd', p[..., n_sel:], v_own))

    return _unblock(lax.map(block, jnp.arange(S // GATHER_QBLOCK) * GATHER_QBLOCK))


def _nsa_attention(q, kv, gates, cmp_pos, cmp_w1, cmp_w2, pos):
    b, h, S, d = q.shape
    scale = d ** -0.5
    k_c, v_c, k_s, v_s, k_w, v_w = [kv[:, :, i] for i in range(6)]
    q_rot = _rope(q, pos)
    k_s = _rope(k_s, pos)
    k_w = _rope(k_w, pos)

    nc = (S - B_CMP_LEN) // B_CMP_STRIDE + 1
    starts = np.arange(nc) * B_CMP_STRIDE
    gidx = starts[:, None] + np.arange(B_CMP_LEN)[None, :]

    def compress(x, pe, w1, w2):
        blocks = (x[:, gidx] + pe).reshape(b, nc, B_CMP_LEN * d)
        return jax.nn.gelu(blocks @ w1) @ w2

    k_cmp = compress(k_c, cmp_pos[0], cmp_w1[0], cmp_w2[0])
    v_cmp = compress(v_c, cmp_pos[1], cmp_w1[1], cmp_w2[1])
    s_cmp = jnp.einsum('bhsd,bnd->bhsn', q, k_cmp, preferred_element_type=jnp.float32) * scale
    vis = jnp.asarray(starts + B_CMP_LEN - 1)[None, :] <= pos[:, None]
    p_cmp, _ = _masked_softmax(s_cmp, vis)
    o_cmp = jnp.einsum('bhsn,bnd->bhsd', p_cmp.astype(v_cmp.dtype), v_cmp)

    nsel = S // B_SLC_LEN
    j = np.arange(nsel)
    overlap = ((starts[:, None] < (j[None, :] + 1) * B_SLC_LEN)
               & (starts[:, None] + B_CMP_LEN > j[None, :] * B_SLC_LEN)).astype(np.float32)
    imp = jnp.einsum('bhsn,nj->bsj', p_cmp, jnp.asarray(overlap))
    cur = (pos // B_SLC_LEN)[:, None]
    jj = jnp.arange(nsel)[None, :]
    forced = (jj == 0) | (jj == cur) | (jj == cur - 1)
    imp = jnp.where(jj <= cur, jnp.where(forced, jnp.inf, imp), -jnp.inf)
    top_score, top_idx = lax.top_k(imp, min(B_SLC_TOPN, nsel))
    top_ok = top_score > -jnp.inf
    topn = top_idx.shape[-1]
    n_key = topn * B_SLC_LEN
    ksb = k_s.reshape(b, nsel, B_SLC_LEN, d)
    vsb = v_s.reshape(b, nsel, B_SLC_LEN, d)
    pick = jax.vmap(lambda blocks, i: blocks[i])

    def block(q0):
        qs = lax.dynamic_slice_in_dim(q_rot, q0, GATHER_QBLOCK, axis=2)
        ids = lax.dynamic_slice_in_dim(top_idx, q0, GATHER_QBLOCK, axis=1)
        ok = lax.dynamic_slice_in_dim(top_ok, q0, GATHER_QBLOCK, axis=1)
        kg = pick(ksb, ids).reshape(b, GATHER_QBLOCK, n_key, d)
        vg = pick(vsb, ids).reshape(b, GATHER_QBLOCK, n_key, d)
        tk = (ids[..., None] * B_SLC_LEN + jnp.arange(B_SLC_LEN)).reshape(b, GATHER_QBLOCK, n_key)
        tq = q0 + jnp.arange(GATHER_QBLOCK)
        mask = jnp.repeat(ok, B_SLC_LEN, axis=-1) & (tk <= tq[None, :, None])
        s = jnp.einsum('bhqd,bqkd->bhqk', qs, kg, preferred_element_type=jnp.float32) * scale
        p, _ = _masked_softmax(s, mask[:, None])
        return jnp.einsum('bhqk,bqkd->bhqd', p.astype(vg.dtype), vg)

    o_slc = _unblock(lax.map(block, jnp.arange(S // GATHER_QBLOCK) * GATHER_QBLOCK))

    o_win, _ = _banded_attention(q_rot, k_w[:, None], v_w[:, None], B_WINDOW - 1)
    return gates[0] * o_cmp + gates[1] * o_slc + gates[2] * o_win


def _stick_breaking_attention(q, k, v):
    b, h, S, d = q.shape
    scale = d ** -0.5
    outs = []
    for i in range(S // DENSE_QBLOCK):
        q0, end = i * DENSE_QBLOCK, (i + 1) * DENSE_QBLOCK
        z = jnp.einsum('bhqd,bhkd->bhqk', q[:, :, q0:end], k[:, :, :end],
                       preferred_element_type=jnp.float32) * scale
        tq = q0 + jnp.arange(DENSE_QBLOCK)
        causal = jnp.arange(end)[None, :] < tq[:, None]
        log_1m = jnp.where(causal, jax.nn.log_sigmoid(-z), 0.0)
        tail = lax.cumsum(log_1m, axis=3, reverse=True) - log_1m
        a = jnp.where(causal, jnp.exp(jax.nn.log_sigmoid(z) + tail), 0.0)
        outs.append(jnp.einsum('bhqk,bhkd->bhqd', a.astype(v.dtype), v[:, :, :end]))
    return jnp.concatenate(outs, axis=2)


def _to_sub(x, dil):
    b, h, S, d = x.shape
    return x.reshape(b, h, S // dil, dil, d).transpose(0, 1, 3, 2, 4).reshape(b, h * dil, S // dil, d)


def _dilated_attention(q, k, v):
    b, _, S, d = q.shape
    outs, lses = [], []
    for g, (window, dil) in enumerate(D_PATTERNS):
        sl = slice(g * D_HEADS, (g + 1) * D_HEADS)
        L = S // dil
        o, lse = _banded_attention(_to_sub(q[:, sl], dil), _to_sub(k[:, sl], dil),
                                   _to_sub(v[:, sl], dil), window // dil)
        outs.append(o.reshape(b, D_HEADS, dil, L, d).transpose(0, 1, 3, 2, 4).reshape(b, D_HEADS, S, d))
        lses.append(lse.reshape(b, D_HEADS, dil, L).transpose(0, 1, 3, 2).reshape(b, D_HEADS, S))
    w = jax.nn.softmax(jnp.stack(lses), axis=0)
    return jnp.einsum('gbhs,gbhsd->bhsd', w.astype(q.dtype), jnp.stack(outs))


def _hybrid_layer(x, norm_w, w_in, cmp_pos, cmp_w1, cmp_w2, w_up, w_out):
    b, S, _ = x.shape
    pos = jnp.arange(S)
    h = _rmsnorm(x, norm_w)
    proj = jnp.einsum('bsd,dc->bsc', h, w_in)
    (qa, ka, va, ga, qb, kvb, gb, nsa_g, qkvc, gc, qkvd, gd, merge) = jnp.split(
        proj, np.cumsum(IN_SIZES)[:-1].tolist(), axis=-1)

    o_a = _moba_attention(_rope(_heads(qa, A_HEADS), pos), _rope(_heads(ka, A_HEADS), pos),
                          _heads(va, A_HEADS))

    nsa_gates = jax.nn.sigmoid(nsa_g.astype(jnp.float32)).reshape(b, S, 3, B_HEADS)
    nsa_gates = nsa_gates.transpose(2, 0, 3, 1)[..., None].astype(x.dtype)
    o_b = _nsa_attention(_heads(qb, B_HEADS), kvb.reshape(b, S, 6, HEAD_DIM), nsa_gates,
                         cmp_pos, cmp_w1, cmp_w2, pos)

    qc, kc, vc = jnp.split(qkvc, 3, axis=-1)
    o_c = _stick_breaking_attention(_heads(qc, C_HEADS), _heads(kc, C_HEADS), _heads(vc, C_HEADS))

    n_d = len(D_PATTERNS) * D_HEADS
    qd, kd, vd = jnp.split(qkvd, 3, axis=-1)
    o_d = _dilated_attention(_rope(_heads(qd, n_d), pos), _rope(_heads(kd, n_d), pos), _heads(vd, n_d))

    widened = jnp.stack([_merge_heads(o_a) * jax.nn.silu(ga), _merge_heads(o_b) * jax.nn.silu(gb),
                         _merge_heads(o_c) * jax.nn.silu(gc), _merge_heads(o_d) * jax.nn.silu(gd)],
                        axis=2)
    u = jnp.einsum('bsiw,iwd->bsid', widened, w_up)
    merge_g = jax.nn.sigmoid(merge.astype(jnp.float32)).astype(x.dtype).reshape(b, S, N_BRANCH, D_MODEL)
    y = jnp.sum(merge_g * u, axis=2)
    return x + y @ w_out


def setup_inputs(seed: int = 0) -> dict:
    key = jax.random.key(seed)
    ks = jax.random.split(key, 9)
    f32 = jnp.float32
    x = jax.random.normal(ks[0], (BATCH, SEQ, D_MODEL), f32)
    norm_w = 1.0 + 0.02 * jax.random.normal(ks[1], (DEPTH, D_MODEL), f32)
    w_in = jax.random.normal(ks[2], (DEPTH, D_MODEL, IN_WIDTH), f32) * D_MODEL ** -0.5
    nsa_cmp_pos = 0.02 * jax.random.normal(ks[3], (DEPTH, 2, B_CMP_LEN, HEAD_DIM), f32)
    nsa_cmp_w1 = jax.random.normal(ks[4], (DEPTH, 2, B_CMP_LEN * HEAD_DIM, B_CMP_HIDDEN), f32) * (B_CMP_LEN * HEAD_DIM) ** -0.5
    nsa_cmp_w2 = jax.random.normal(ks[5], (DEPTH, 2, B_CMP_HIDDEN, HEAD_DIM), f32) * B_CMP_HIDDEN ** -0.5
    w_up = jax.random.normal(ks[6], (DEPTH, N_BRANCH, BRANCH_W, D_MODEL), f32) * BRANCH_W ** -0.5
    w_out = jax.random.normal(ks[7], (DEPTH, D_MODEL, D_MODEL), f32) * D_MODEL ** -0.5
    final_norm_w = 1.0 + 0.02 * jax.random.normal(ks[8], (D_MODEL,), f32)
    return {'x': x, 'norm_w': norm_w, 'w_in': w_in, 'nsa_cmp_pos': nsa_cmp_pos,
            'nsa_cmp_w1': nsa_cmp_w1, 'nsa_cmp_w2': nsa_cmp_w2, 'w_up': w_up,
            'w_out': w_out, 'final_norm_w': final_norm_w}


def reference(x, norm_w, w_in, nsa_cmp_pos, nsa_cmp_w1, nsa_cmp_w2, w_up, w_out, final_norm_w):
    for layer in range(DEPTH):
        x = _hybrid_layer(x, norm_w[layer], w_in[layer], nsa_cmp_pos[layer], nsa_cmp_w1[layer],
                          nsa_cmp_w2[layer], w_up[layer], w_out[layer])
    return _rmsnorm(x, final_norm_w)
```

```python
import numpy as np
from contextlib import ExitStack
import concourse.bass as bass
import concourse.mybir as mybir
from concourse.bass_utils import run_bass_kernel_spmd

F32 = mybir.dt.float32
BF16 = mybir.dt.bfloat16
AF = mybir.ActivationFunctionType
ALU = mybir.AluOpType
AX = mybir.AxisListType

S = 4096
D = 1024
NCORES = 8
DEPTH = 2
INW = 9612
NEG = -30000.0

C_QA, C_KA, C_VA, C_GA = 0, 256, 512, 768
C_QB, C_KVB, C_GB, C_NG = 1024, 1280, 1664, 1920
C_QC, C_KC, C_VC, C_GC = 1932, 2188, 2444, 2700
C_QD, C_KD, C_VD, C_GD = 2956, 3724, 4492, 5260
C_MG = 5516


class Buf:
    __slots__ = ("name", "w", "r", "psum")

    def __init__(self, name="", psum=False):
        self.name = name
        self.w = None
        self.r = {}
        self.psum = psum


class EngState:
    def __init__(self, name, eng, sem):
        self.name = name
        self.eng = eng
        self.sem = sem
        self.count = 0
        self.waited = {}


class Ctx:
    NSLOT = 24

    def __init__(self, nc):
        self.nc = nc
        self.es = ExitStack()
        self.E = {}
        for nm in ("tensor", "vector", "scalar", "gpsimd", "sync"):
            sem = self.es.enter_context(nc.semaphore("s_" + nm))
            self.E[nm] = EngState(nm, getattr(nc, nm), sem)
        self.slots = [self.es.enter_context(nc.semaphore("d_%d" % i)) for i in range(self.NSLOT)]
        self.slot_uses = [0] * self.NSLOT
        self.slot_next = 0
        self.n_inst = 0

    def _wait(self, E, ev):
        sem, val = ev
        key = id(sem)
        if E.waited.get(key, 0) >= val:
            return
        if sem is E.sem and False:
            return
        E.eng.wait_ge(sem, val)
        E.waited[key] = val

    def _deps(self, E, r, w, acc):
        for b in r:
            if b.w is not None:
                self._wait(E, b.w)
            if b.psum:
                for k, ev in b.r.items():
                    if ev[0] is not E.sem:
                        self._wait(E, ev)
        for b in w:
            if b.w is not None:
                self._wait(E, b.w)
            for k, ev in b.r.items():
                self._wait(E, ev)

    def _record(self, ev, r, w, acc):
        for b in r:
            b.r[id(ev[0])] = ev
        for b in w:
            b.w = ev
            b.r = {}
        for b in acc:
            b.w = ev

    def op(self, eng, fn, r=(), w=(), acc=()):
        E = self.E[eng]
        self._deps(E, r, w, acc)
        ins = fn(E.eng)
        E.count += 1
        ins.then_inc(E.sem, 1)
        self.n_inst += 1
        self._record((E.sem, E.count), r, w, acc)

    def dma(self, out, in_, r=(), w=(), q="sync"):
        E = self.E[q]
        self._deps(E, r, w, ())
        i = self.slot_next
        self.slot_next = (i + 1) % self.NSLOT
        sem = self.slots[i]
        if self.slot_uses[i] > 0:
            self._wait(E, (sem, 16 * self.slot_uses[i]))
        self.slot_uses[i] += 1
        E.eng.dma_start(out=out, in_=in_).then_inc(sem, 16)
        self.n_inst += 1
        self._record((sem, 16 * self.slot_uses[i]), r, w, ())

    def barrier(self):
        evs = [(E.sem, E.count) for E in self.E.values() if E.count > 0]
        evs += [(self.slots[i], 16 * self.slot_uses[i]) for i in range(self.NSLOT) if self.slot_uses[i] > 0]
        for E in self.E.values():
            for ev in evs:
                if ev[0] is E.sem:
                    continue
                self._wait(E, ev)


def build_program(debug=None):
    debug = debug or {}
    nc = bass.Bass("TRN2", target_bir_lowering=False)
    cx = Ctx(nc)
    es = cx.es
    scratch_kind = "ExternalOutput" if debug.get("dump") else "Internal"

    def din(name, shape, dt=F32):
        return nc.dram_tensor(name, list(shape), dt, kind="ExternalInput").ap()

    def dscr(name, shape, dt):
        kind = "ExternalOutput" if name in debug.get("dump_names", ()) else "Internal"
        return nc.dram_tensor(name, list(shape), dt, kind=kind).ap()

    xT_in = din("xT", [D, S])
    norm_w = din("norm_wT", [128, DEPTH * 8])
    fnorm_w = din("fnorm_wT", [128, 8])
    w_in = din("w_in", [DEPTH, D, INW])
    cmp_peT = din("cmp_peT", [DEPTH, 128, 32])
    cmp_w1 = din("cmp_w1", [DEPTH, 2, 2048, 256])
    cmp_w2 = din("cmp_w2", [DEPTH, 2, 256, 64])
    w_up = din("w_up", [DEPTH, 4, 256, D])
    w_out = din("w_out", [DEPTH, D, D])
    cs2 = din("cs2", [2, 128, S])
    c_f32 = din("c_f32", [128, 512])
    outT = nc.dram_tensor("outT", [D, S], F32, kind="ExternalOutput").ap()

    xr = [dscr("xr0", [D, S], F32), dscr("xr1", [D, S], F32)]
    hT_d = dscr("hT_d", [D, S], BF16)
    QK = dscr("QK", [3968, S], BF16)
    KVC = dscr("KVC", [128, S], F32)
    G_d = dscr("G_d", [1024, S], F32)
    NG_d = dscr("NG_d", [12, S], F32)
    R_QA, R_KA, R_QB, R_QBR, R_KSW, R_QC, R_KC, R_QD, R_KD = 0, 256, 512, 768, 1024, 1152, 1408, 1664, 2432

    VA_d = dscr("VA_d", [S, 4, 128], BF16)
    VB_d = dscr("VB_d", [S, 2, 128], BF16)
    VC_d = dscr("VC_d", [S, 4, 128], BF16)
    VD_d = [dscr("VD%d_d" % g, [S, 4, 128], BF16) for g in range(3)]
    WD_d = dscr("WD_d", [1024, S], BF16)
    b_VA, b_VB, b_VC, b_VD, b_WD = Buf("VA"), Buf("VB"), Buf("VC"), Buf("VD"), Buf("WD")
    c_bf = din("c_bf", [128, 1024], BF16)
    ind16 = din("ind16", [16, S], BF16)
    OB_d = [dscr("OB%d_d" % i, [256, S], F32) for i in range(3)]
    b_OB = [Buf(), Buf(), Buf()]
    ind64 = din("ind64", [64, S], BF16)
    vis_c = din("vis_c", [2, 128, S], BF16)
    ovb_c = din("ovb_c", [128, 256], BF16)
    fm_c = din("fm_c", [128, 32 * 64], F32)
    sel_c = din("sel_c", [12, 12 * 128], F32)
    b_xr = [Buf("xr0"), Buf("xr1")]
    b_hT_d, b_QK, b_KVC, b_G, b_NG = Buf("hT_d"), Buf("QK"), Buf("KVC"), Buf("G"), Buf("NG")

    uniq = [0]

    def sb(name, shape, dt, stack):
        uniq[0] += 1
        return stack.enter_context(nc.sbuf_tensor("%s_%d" % (name, uniq[0]), list(shape), dt))

    PS = [es.enter_context(nc.psum_tensor("ps%d" % i, [128, 512], F32)) for i in range(8)]
    bPS = [Buf("ps%d" % i, psum=True) for i in range(8)]

    cF = sb("cF", [128, 512], F32, es)
    ident = cF[:, 0:128]
    avg = cF[:, 128:256]
    nw_sb = sb("nw_sb", [128, DEPTH * 8], F32, es)
    fnw_sb = sb("fnw_sb", [128, 8], F32, es)
    cB = sb("cB", [128, 1024], BF16, es)
    tri_le = cB[:, 0:128]
    tri_lt = cB[:, 128:256]
    tri_gt = cB[:, 256:384]
    band = cB[:, 384:640]
    zeros_bf = cB[:, 640:768]
    UTneg = cB[:, 768:896]
    onesneg = cB[:, 896:1024]
    b_const = Buf("const")
    cx.dma(cB[:], c_bf[:, :], w=[b_const])
    cx.dma(cF[:], c_f32[:, :], w=[b_const])
    cx.dma(nw_sb[:], norm_w[:, :], w=[b_const])
    cx.dma(fnw_sb[:], fnorm_w[:, :], w=[b_const])
    cx.barrier()

    def phase1(l, x_src, b_x_src):
        with ExitStack() as ph:
            hT = sb("hT", [128, 8, S], BF16, ph)
            b_hT = [Buf("hT%d" % i) for i in range(8)]
            cs = sb("cs", [128, 2, S], F32, ph)
            b_cs = Buf("cs")
            cx.dma(cs[:, 0, :], cs2[0, :, :], w=[b_cs])
            cx.dma(cs[:, 1, :], cs2[1, :, :], w=[b_cs])
            with ExitStack() as pa:
                xt = [sb("xt%d" % i, [128, 8, 512], F32, pa) for i in range(2)]
                b_xt = [Buf(), Buf()]
                sq = sb("sq", [128, 8, 512], F32, pa)
                b_sq = Buf()
                rstd = sb("rstd", [128, 512], F32, pa)
                b_rstd = Buf()
                for tb in range(8):
                    t0 = tb * 512
                    X, bX = xt[tb % 2], b_xt[tb % 2]
                    cx.dma(X[:], x_src[:, t0:t0 + 512].rearrange("(kc p) t -> p kc t", p=128),
                           r=[b_x_src], w=[bX])
                    cx.op("scalar", lambda e: e.activation(out=sq[:], in_=X[:], func=AF.Square),
                          r=[bX], w=[b_sq])
                    pm, bpm = PS[tb % 2], bPS[tb % 2]
                    for kc in range(8):
                        if kc == 0:
                            cx.op("tensor", lambda e: e.matmul(pm[:, :], lhsT=avg, rhs=sq[:, kc, :],
                                                               start=True, stop=False),
                                  r=[b_sq], w=[bpm])
                        else:
                            cx.op("tensor", lambda e: e.matmul(pm[:, :], lhsT=avg, rhs=sq[:, kc, :],
                                                               start=False, stop=(kc == 7)),
                                  r=[b_sq], acc=[bpm])
                    cx.op("vector", lambda e: e.tensor_scalar(out=rstd[:], in0=pm[:, :], scalar1=1e-6,
                                                              scalar2=None, op0=ALU.add),
                          r=[bpm], w=[b_rstd])
                    cx.op("scalar", lambda e: e.activation(out=rstd[:], in_=rstd[:], func=AF.Sqrt),
                          r=[b_rstd], w=[b_rstd])
                    cx.op("vector", lambda e: e.reciprocal(out=rstd[:], in_=rstd[:]),
                          r=[b_rstd], w=[b_rstd])
                    for kc in range(8):
                        cx.op("vector", lambda e: e.scalar_tensor_tensor(
                            out=hT[:, kc, t0:t0 + 512], in0=X[:, kc, :],
                            scalar=nw_sb[:, l * 8 + kc:l * 8 + kc + 1], in1=rstd[:],
                            op0=ALU.mult, op1=ALU.mult),
                            r=[bX, b_rstd], w=[] if kc else [b_hT[tb]],
                            acc=[b_hT[tb]] if kc else [])
                    cx.dma(hT_d[:, t0:t0 + 512].rearrange("(kc p) t -> p kc t", p=128),
                           hT[:, :, t0:t0 + 512], r=[b_hT[tb]], w=[b_hT_d])
                cx.barrier()
            specs = []
            for i in range(2):
                specs.append(([(C_QA + 128 * i, 128)], "rope", QK[R_QA + 128 * i:R_QA + 128 * (i + 1), :], b_QK))
            for i in range(2):
                specs.append(([(C_KA + 128 * i, 128)], "rope", QK[R_KA + 128 * i:R_KA + 128 * (i + 1), :], b_QK))
            for i in range(2):
                specs.append(([(C_GA + 128 * i, 128)], "silu", G_d[128 * i:128 * (i + 1), :], b_G))
            for i in range(2):
                specs.append(([(C_QB + 128 * i, 128)], "both", (QK[R_QB + 128 * i:R_QB + 128 * (i + 1), :],
                                                                  QK[R_QBR + 128 * i:R_QBR + 128 * (i + 1), :]), b_QK))
            specs.append(([(C_KVB, 128)], "f32", KVC[:, :], b_KVC))
            specs.append(([(C_KVB + 128, 64), (C_KVB + 256, 64)], "rope", QK[R_KSW:R_KSW + 128, :], b_QK))
            for i in range(2):
                specs.append(([(C_GB + 128 * i, 128)], "silu", G_d[256 + 128 * i:256 + 128 * (i + 1), :], b_G))
            specs.append(([(C_NG, 12)], "sigmoid", NG_d[:, :], b_NG))
            for i in range(2):
                specs.append(([(C_QC + 128 * i, 128)], "plain", QK[R_QC + 128 * i:R_QC + 128 * (i + 1), :], b_QK))
            for i in range(2):
                specs.append(([(C_KC + 128 * i, 128)], "plain", QK[R_KC + 128 * i:R_KC + 128 * (i + 1), :], b_QK))
            for i in range(2):
                specs.append(([(C_GC + 128 * i, 128)], "silu", G_d[512 + 128 * i:512 + 128 * (i + 1), :], b_G))
            for i in range(6):
                specs.append(([(C_QD + 128 * i, 128)], "rope", QK[R_QD + 128 * i:R_QD + 128 * (i + 1), :], b_QK))
            for i in range(6):
                specs.append(([(C_KD + 128 * i, 128)], "rope", QK[R_KD + 128 * i:R_KD + 128 * (i + 1), :], b_QK))
            for i in range(2):
                specs.append(([(C_GD + 128 * i, 128)], "silu", G_d[768 + 128 * i:768 + 128 * (i + 1), :], b_G))
            if debug.get("p1_specs") is not None:
                specs = [specs[i] for i in debug["p1_specs"]]

            with ExitStack() as pb:
                NW = 2
                wst = [sb("wst%d" % i, [128, 8, 128], F32, pb) for i in range(NW)]
                wsw = [sb("wsw%d" % i, [128, 8, 128], F32, pb) for i in range(NW)]
                wbf = [sb("wbf%d" % i, [128, 8, 128], BF16, pb) for i in range(NW)]
                wbs = [sb("wbs%d" % i, [128, 8, 128], BF16, pb) for i in range(NW)]
                b_wst = [Buf() for _ in range(NW)]
                b_wsw = [Buf() for _ in range(NW)]
                b_wbf = [Buf() for _ in range(NW)]
                b_wbs = [Buf() for _ in range(NW)]
                NO = 3
                ob = [sb("ob%d" % i, [128, 512], BF16, pb) for i in range(NO)]
                of = [sb("of%d" % i, [128, 512], F32, pb) for i in range(NO)]
                t1 = [sb("t1_%d" % i, [128, 512], F32, pb) for i in range(2)]
                t2 = [sb("t2_%d" % i, [128, 512], F32, pb) for i in range(2)]
                b_ob = [Buf() for _ in range(NO)]
                b_of = [Buf() for _ in range(NO)]
                b_t1 = [Buf(), Buf()]
                b_t2 = [Buf(), Buf()]
                oi = 0
                ti = 0
                pi = 0
                wl = w_in[l]

                def load_w(si):
                    cols, mode, dst, bdst = specs[si]
                    k = si % NW
                    need_sw = mode in ("rope", "both")
                    c0 = 0
                    for (col, n) in cols:
                        cx.dma(wst[k][:, :, c0:c0 + n],
                               wl[:, col:col + n].rearrange("(kc p) c -> p kc c", p=128), w=[b_wst[k]])
                        if need_sw:
                            for hh in range(n // 64):
                                b0 = col + hh * 64
                                d0 = c0 + hh * 64
                                cx.dma(wsw[k][:, :, d0:d0 + 32],
                                       wl[:, b0 + 32:b0 + 64].rearrange("(kc p) c -> p kc c", p=128),
                                       w=[b_wsw[k]])
                                cx.dma(wsw[k][:, :, d0 + 32:d0 + 64],
                                       wl[:, b0:b0 + 32].rearrange("(kc p) c -> p kc c", p=128),
                                       w=[b_wsw[k]])
                        c0 += n
                    ncol = c0
                    cx.op("gpsimd", lambda e: e.tensor_copy(out=wbf[k][:, :, 0:ncol], in_=wst[k][:, :, 0:ncol]),
                          r=[b_wst[k]], w=[b_wbf[k]])
                    if need_sw:
                        cx.op("gpsimd", lambda e: e.tensor_copy(out=wbs[k][:, :, 0:ncol], in_=wsw[k][:, :, 0:ncol]),
                              r=[b_wsw[k]], w=[b_wbs[k]])
                    return ncol

                ncols = {}
                ncols[0] = load_w(0)
                for si in range(len(specs)):
                    if si + 1 < len(specs):
                        ncols[si + 1] = load_w(si + 1)
                    cols, mode, dst, bdst = specs[si]
                    k = si % NW
                    M = ncols[si]
                    need_sw = mode in ("rope", "both")
                    for tb in range(8):
                        t0 = tb * 512
                        pa_, bpa = PS[pi % 8], bPS[pi % 8]
                        pi += 1
                        for kc in range(8):
                            cx.op("tensor", lambda e: e.matmul(pa_[0:M, :], lhsT=wbf[k][:, kc, 0:M],
                                                               rhs=hT[:, kc, t0:t0 + 512],
                                                               start=(kc == 0), stop=(kc == 7)),
                                  r=[b_wbf[k], b_hT[tb]] if kc == 0 else [],
                                  w=[bpa] if kc == 0 else [], acc=[] if kc == 0 else [bpa])
                        if need_sw:
                            ps_, bps = PS[pi % 8], bPS[pi % 8]
                            pi += 1
                            for kc in range(8):
                                cx.op("tensor", lambda e: e.matmul(ps_[0:M, :], lhsT=wbs[k][:, kc, 0:M],
                                                                   rhs=hT[:, kc, t0:t0 + 512],
                                                                   start=(kc == 0), stop=(kc == 7)),
                                      r=[b_wbs[k], b_hT[tb]] if kc == 0 else [],
                                      w=[bps] if kc == 0 else [], acc=[] if kc == 0 else [bps])
                        if mode in ("plain", "both"):
                            o, bo = ob[oi % NO], b_ob[oi % NO]
                            oi += 1
                            d = dst[0] if mode == "both" else dst
                            cx.op("scalar", lambda e: e.activation(out=o[0:M, :], in_=pa_[0:M, :], func=AF.Copy),
                                  r=[bpa], w=[bo])
                            cx.dma(d[:, t0:t0 + 512], o[0:M, :], r=[bo], w=[bdst])
                        if mode in ("rope", "both"):
                            o, bo = ob[oi % NO], b_ob[oi % NO]
                            oi += 1
                            a1, ba1 = t1[ti % 2], b_t1[ti % 2]
                            a2, ba2 = t2[ti % 2], b_t2[ti % 2]
                            ti += 1
                            d = dst[1] if mode == "both" else dst
                            cx.op("vector", lambda e: e.tensor_tensor(out=a1[0:M, :], in0=pa_[0:M, :],
                                                                      in1=cs[0:M, 0, t0:t0 + 512], op=ALU.mult),
                                  r=[bpa, b_cs], w=[ba1])
                            cx.op("vector", lambda e: e.tensor_tensor(out=a2[0:M, :], in0=ps_[0:M, :],
                                                                      in1=cs[0:M, 1, t0:t0 + 512], op=ALU.mult),
                                  r=[bps, b_cs], w=[ba2])
                            cx.op("gpsimd", lambda e: e.tensor_tensor(out=o[0:M, :], in0=a1[0:M, :],
                                                                      in1=a2[0:M, :], op=ALU.add),
                                  r=[ba1, ba2], w=[bo])
                            cx.dma(d[:, t0:t0 + 512], o[0:M, :], r=[bo], w=[bdst])
                        if mode in ("f32", "silu", "sigmoid"):
                            o, bo = of[oi % NO], b_of[oi % NO]
                            oi += 1
                            fn = {"f32": AF.Copy, "silu": AF.Silu, "sigmoid": AF.Sigmoid}[mode]
                            cx.op("scalar", lambda e: e.activation(out=o[0:M, :], in_=pa_[0:M, :], func=fn),
                                  r=[bpa], w=[bo])
                            cx.dma(dst[:, t0:t0 + 512], o[0:M, :], r=[bo], w=[bdst])
                cx.barrier()
            if "p1c" in phases:
                phase1c(l, hT, b_hT)
        cx.barrier()

    def make_pt(ph, tag, n=3):
        return ([sb("pt%s%d" % (tag, i), [128, 512], BF16, ph) for i in range(n)], [Buf() for _ in range(n)], [0])

    def causal_attn(ptp, Qb, bQ, Kb, bK, Kc, Vb, bV, epilogue):
        pt, b_pt, itc = ptp
        NPT = len(pt)
        it = itc[0]
        for qb in range(8):
            po, bpo = PS[3 + qb % 2], bPS[3 + qb % 2]
            nkt = 4 * qb + 4
            for kt in range(nkt):
                n0 = max(0, kt * 128 - qb * 512)
                q0 = qb * 512
                pS, bpS = PS[it % 3], bPS[it % 3]
                P, bP = pt[it % NPT], b_pt[it % NPT]
                it += 1
                cx.op("tensor", lambda e: e.matmul(pS[:, n0:512], lhsT=Kb[0:Kc, kt * 128:(kt + 1) * 128],
                                                   rhs=Qb[0:Kc, q0 + n0:q0 + 512], start=True, stop=True),
                      r=list(bQ) + list(bK), w=[bpS])
                cx.op("scalar", lambda e: e.activation(out=P[:, n0:512], in_=pS[:, n0:512], func=AF.Exp,
                                                       scale=0.125),
                      r=[bpS], w=[bP])
                if kt * 128 >= q0:
                    cx.op("gpsimd", lambda e: e.tensor_tensor(out=P[:, n0:n0 + 128], in0=P[:, n0:n0 + 128],
                                                              in1=tri_le, op=ALU.mult),
                          r=[bP], w=[bP])
                cx.op("tensor", lambda e: e.matmul(po[:, n0:512], lhsT=Vb[:, kt, :], rhs=P[:, n0:512],
                                                   start=(kt == 0), stop=(kt == nkt - 1)),
                      r=[bP] + list(bV), w=[bpo] if kt == 0 else [], acc=[] if kt == 0 else [bpo])
            epilogue(qb, po, bpo)
        itc[0] = it

    def make_norm_epilogue(ph, row0, tag):
        rz = [sb("rz%s%d" % (tag, i), [128, 512], F32, ph) for i in range(2)]
        on = [sb("on%s%d" % (tag, i), [64, 512], F32, ph) for i in range(2)]
        gt = [sb("gt%s%d" % (tag, i), [64, 512], F32, ph) for i in range(2)]
        ow = [sb("ow%s%d" % (tag, i), [64, 512], BF16, ph) for i in range(2)]
        b_rz, b_on, b_gt, b_ow = [Buf(), Buf()], [Buf(), Buf()], [Buf(), Buf()], [Buf(), Buf()]
        cnt = [0]

        def ep(qb, po, bpo, r0=None):
            i = cnt[0] % 2
            cnt[0] += 1
            rr = row0[0]
            q0 = qb * 512
            cx.dma(gt[i][:, :], G_d[rr:rr + 64, q0:q0 + 512], r=[b_G], w=[b_gt[i]])
            cx.op("vector", lambda e: e.reciprocal(out=rz[i][64:128, :], in_=po[64:128, :]),
                  r=[bpo], w=[b_rz[i]])
            cx.op("vector", lambda e: e.tensor_tensor(out=on[i][:, :], in0=po[0:64, :], in1=rz[i][64:128, :],
                                                      op=ALU.mult),
                  r=[bpo, b_rz[i]], w=[b_on[i]])
            cx.op("gpsimd", lambda e: e.tensor_tensor(out=ow[i][:, :], in0=on[i][:, :], in1=gt[i][:, :],
                                                      op=ALU.mult),
                  r=[b_on[i], b_gt[i]], w=[b_ow[i]])
            cx.dma(WD_d[rr:rr + 64, q0:q0 + 512], ow[i][:, :], r=[b_ow[i]], w=[b_WD])
        return ep

    def phase1c(l, hT, b_hT):
        with ExitStack() as pc:
            wv = sb("wv", [128, 8, 1408], BF16, pc)
            b_wv = Buf()
            wvs = [sb("wvs%d" % i, [128, 8, 128], F32, pc) for i in range(2)]
            b_wvs = [Buf(), Buf()]
            wl = w_in[l]
            pieces = [(C_VA, 256, 0), (C_KVB + 192, 64, 256), (C_KVB + 320, 64, 320), (C_VC, 256, 384),
                      (C_VD, 768, 640)]
            k = 0
            for (col, n, dc) in pieces:
                for j in range(0, n, 128):
                    m = min(128, n - j)
                    cx.dma(wvs[k % 2][:, :, 0:m], wl[:, col + j:col + j + m].rearrange("(kc p) c -> p kc c", p=128),
                           w=[b_wvs[k % 2]])
                    cx.op("gpsimd", lambda e: e.tensor_copy(out=wv[:, :, dc + j:dc + j + m], in_=wvs[k % 2][:, :, 0:m]),
                          r=[b_wvs[k % 2]], w=[], acc=[b_wv])
                    k += 1
            dests = [("A", VA_d, b_VA, 4), ("B", VB_d, b_VB, 2), ("C", VC_d, b_VC, 4),
                     ("D0", VD_d[0], b_VD, 4), ("D1", VD_d[1], b_VD, 4), ("D2", VD_d[2], b_VD, 4)]
            st = {}
            for (nm, _, _, nh) in dests:
                st[nm] = ([sb("sv%s%d" % (nm, i), [128, nh, 128], BF16, pc) for i in range(2)], [Buf(), Buf()])
                for i in range(2):
                    cx.op("gpsimd", lambda e: e.memset(st[nm][0][i][:], 1.0), w=[st[nm][1][i]])
            pi = 0
            for tt in range(32):
                tok_nat = slice(tt * 128, (tt + 1) * 128)
                c4, m4 = tt // 8, (tt % 8) * 128
                c16, m16 = tt // 2, (tt % 2) * 128
                tok4 = slice(c4 + 4 * m4, c4 + 4 * m4 + 4 * 127 + 1, 4)
                tok16 = slice(c16 + 16 * m16, c16 + 16 * m16 + 16 * 127 + 1, 16)
                groups = [(0, 384, tok_nat, [("A", 0, 4), ("B", 256, 2)]),
                          (384, 512, tok_nat, [("C", 0, 4), ("D0", 256, 4)]),
                          (896, 256, tok4, [("D1", 0, 4)]),
                          (1152, 256, tok16, [("D2", 0, 4)])]
                for (gc, N, tok, outs) in groups:
                    pp, bpp = PS[pi % 8], bPS[pi % 8]
                    pi += 1
                    for kc in range(8):
                        cx.op("tensor", lambda e: e.matmul(pp[:, 0:N], lhsT=hT[:, kc, tok], rhs=wv[:, kc, gc:gc + N],
                                                           start=(kc == 0), stop=(kc == 7)),
                              r=[b_wv] + b_hT if kc == 0 else [], w=[bpp] if kc == 0 else [],
                              acc=[] if kc == 0 else [bpp])
                    for (nm, c0, nh) in outs:
                        tiles, bufs = st[nm]
                        T, bT = tiles[tt % 2], bufs[tt % 2]
                        dst, bdst = [(d[1], d[2]) for d in dests if d[0] == nm][0]
                        cx.op("scalar", lambda e: e.activation(
                            out=T[:, :, 0:64], in_=pp[:, c0:c0 + nh * 64].rearrange("p (h d) -> p h d", d=64),
                            func=AF.Copy), r=[bpp], w=[bT])
                        cx.dma(dst[tt * 128:(tt + 1) * 128, :, :], T[:], r=[bT], w=[bdst])
            cx.barrier()

    def phase2a(l):
        with ExitStack() as ph:
            Qa = [sb("Qa%d" % i, [128, S], BF16, ph) for i in range(2)]
            Ka = [sb("Ka%d" % i, [128, S], BF16, ph) for i in range(2)]
            Va = [sb("Va%d" % i, [128, 32, 128], BF16, ph) for i in range(2)]
            bQa, bQb, bKa, bVa = [Buf(), Buf()], [Buf(), Buf()], [Buf(), Buf()], [Buf(), Buf()]
            km = sb("km", [64, 16], F32, ph)
            kml = sb("kml", [64, 32], BF16, ph)
            gs = sb("gs", [128, 16], F32, ph)
            m8 = sb("m8", [128, 8], F32, ph)
            BT = sb("BT", [128, 80], F32, ph)
            b_km, b_kml, b_gs, b_m8, b_BT = Buf(), Buf(), Buf(), Buf(), Buf()
            for i in range(2):
                cx.dma(Ka[i][64:80, :], ind16[:, :], w=[bKa[i]])
            row0 = [0]
            ep = make_norm_epilogue(ph, row0, "a")
            ptp = make_pt(ph, "a")
            lvl = debug.get("p2a_level", 3)
            for h in range(debug.get("p2a_heads", 4)):
                i = h % 2
                Q, K, V = Qa[i], Ka[i], Va[i]
                cx.dma(Q[0:64, :], QK[R_QA + 64 * h:R_QA + 64 * (h + 1), :], r=[b_QK], w=[bQa[i]])
                cx.dma(K[0:64, :], QK[R_KA + 64 * h:R_KA + 64 * (h + 1), :], r=[b_QK], w=[bKa[i]])
                cx.dma(V[:, :, :], VA_d[:, h, :].rearrange("(kt p) c -> p kt c", p=128), r=[b_VA], w=[bVa[i]])
                cx.op("vector", lambda e: e.tensor_reduce(out=km[:, :], in_=K[0:64, :].rearrange("p (n s) -> p n s", s=256),
                                                          axis=AX.X, op=ALU.add), r=[bKa[i]], w=[b_km])
                cx.op("scalar", lambda e: e.activation(out=kml[:, 0:16], in_=km[:, :], func=AF.Copy, scale=1.0 / 256.0),
                      r=[b_km], w=[b_kml])
                cx.op("vector", lambda e: e.scalar_tensor_tensor(out=kml[:, 16:32], in0=km[:, :], scalar=1.0 / 256.0,
                                                                 in1=kml[:, 0:16], op0=ALU.mult, op1=ALU.subtract),
                      r=[b_km, b_kml], w=[b_kml])
                cx.op("vector", lambda e: e.memset(gs[:, :], -1e30), w=[b_gs])
                cx.op("vector", lambda e: e.memset(BT[:, 0:64], 0.0), w=[b_BT])
                cx.op("vector", lambda e: e.memset(BT[:, 64:80], NEG), w=[b_BT])
                for tt in range(32 if lvl >= 1 else 0):
                    b = tt // 2
                    tk = slice(tt * 128, (tt + 1) * 128)
                    if tt % 2 == 0:
                        if b <= 3:
                            cx.op("vector", lambda e: e.memset(BT[:, 64:65 + b], 0.0), w=[b_BT])
                        else:
                            cx.op("vector", lambda e: e.memset(BT[:, 64 + b:65 + b], 0.0), w=[b_BT])
                    if b >= 4:
                        pg, bpg = PS[5], bPS[5]
                        cx.op("tensor", lambda e: e.matmul(pg[:, 0:16], lhsT=Q[0:64, tk], rhs=kml[:, 0:16],
                                                           start=True, stop=False), r=[bQa[i], b_kml], w=[bpg])
                        cx.op("tensor", lambda e: e.matmul(pg[:, 0:16], lhsT=Q[0:64, tk], rhs=kml[:, 16:32],
                                                           start=False, stop=True), acc=[bpg])
                        cx.op("vector", lambda e: e.tensor_copy(out=gs[:, 0:b], in_=pg[:, 0:b]), r=[bpg], w=[b_gs])
                        cx.op("vector", lambda e: e.max(out=m8[:, :], in_=gs[:, :]), r=[b_gs], w=[b_m8])
                        cx.op("vector", lambda e: e.tensor_scalar(out=BT[:, 64:64 + b], in0=gs[:, 0:b],
                                                                  scalar1=m8[:, 2:3], scalar2=NEG,
                                                                  op0=ALU.is_lt, op1=ALU.mult),
                              r=[b_gs, b_m8], w=[b_BT])
                    ptr, bptr = PS[6 + tt % 2], bPS[6 + tt % 2]
                    cx.op("tensor", lambda e: e.transpose(ptr[0:80, 0:128], BT[:, 0:80], ident), r=[b_BT], w=[bptr])
                    cx.op("scalar", lambda e: e.activation(out=Q[64:80, tk], in_=ptr[64:80, 0:128], func=AF.Copy),
                          r=[bptr], w=[bQb[i]])
                row0[0] = 0 + 64 * h
                if lvl >= 2:
                    causal_attn(ptp, Q, [bQa[i], bQb[i]], K, [bKa[i]], 80, V, [bVa[i]], ep if lvl >= 3 else (lambda *a: None))
            cx.barrier()

    def phase2c(l):
        with ExitStack() as ph:
            Qc = [sb("Qc%d" % i, [64, S], BF16, ph) for i in range(2)]
            Kc_ = [sb("Kc%d" % i, [64, S], BF16, ph) for i in range(2)]
            Vc = [sb("Vc%d" % i, [128, 32, 128], BF16, ph) for i in range(2)]
            bQ, bK, bV = [Buf(), Buf()], [Buf(), Buf()], [Buf(), Buf()]

            def two(nm, dt=F32):
                return [sb("%s%d" % (nm, i), [128, 512], dt, ph) for i in range(2)], [Buf(), Buf()]
            e_sb, b_e = two("ce")
            sp_sb, b_sp = two("csp")
            hi_sb, b_hi = two("chi", BF16)
            lo_sb, b_lo = two("clo", BF16)
            t_sb, b_t = two("ct")
            X_sb, b_X = two("cX")
            a_sb, b_a = two("ca", BF16)
            carry = sb("carry", [128, 512], F32, ph)
            b_carry = Buf()
            gt = [sb("cgt%d" % i, [64, 512], F32, ph) for i in range(2)]
            ow = [sb("cow%d" % i, [64, 512], BF16, ph) for i in range(2)]
            b_gt, b_ow = [Buf(), Buf()], [Buf(), Buf()]
            it = 0
            for h in range(debug.get("p2c_heads", 4)):
                i = h % 2
                Q, K, V = Qc[i], Kc_[i], Vc[i]
                cx.dma(Q[:, :], QK[R_QC + 64 * h:R_QC + 64 * (h + 1), :], r=[b_QK], w=[bQ[i]])
                cx.dma(K[:, :], QK[R_KC + 64 * h:R_KC + 64 * (h + 1), :], r=[b_QK], w=[bK[i]])
                cx.dma(V[:, :, :], VC_d[:, h, :].rearrange("(kt p) c -> p kt c", p=128), r=[b_VC], w=[bV[i]])
                for qb in range(8):
                    q0 = qb * 512
                    po, bpo = PS[3 + qb % 2], bPS[3 + qb % 2]
                    cx.op("vector", lambda e: e.memset(carry[:, :], 0.0), w=[b_carry])
                    cx.op("tensor", lambda e: e.matmul(po[:, :], lhsT=zeros_bf, rhs=cB[:, 0:512], start=True, stop=False),
                          w=[bpo])
                    rr = 512 + 64 * h
                    cx.dma(gt[qb % 2][:, :], G_d[rr:rr + 64, q0:q0 + 512], r=[b_G], w=[b_gt[qb % 2]])
                    for kt in range(4 * qb + 3, -1, -1):
                        n0 = max(0, kt * 128 - q0)
                        diag = kt * 128 >= q0
                        j = it % 2
                        it += 1
                        pS, bpS = PS[j], bPS[j]
                        pC, bpC = (PS[2], bPS[2]) if j == 0 else (PS[5], bPS[5])
                        pR, bpR = PS[6 + j], bPS[6 + j]
                        E_, SP, HI, LO, T_, X_, A_ = e_sb[j], sp_sb[j], hi_sb[j], lo_sb[j], t_sb[j], X_sb[j], a_sb[j]
                        cs_ = slice(n0, 512)
                        cx.op("tensor", lambda e: e.matmul(pS[:, cs_], lhsT=K[:, kt * 128:(kt + 1) * 128],
                                                           rhs=Q[:, q0 + n0:q0 + 512], start=True, stop=True),
                              r=[bQ[i], bK[i]], w=[bpS])
                        cx.op("scalar", lambda e: e.activation(out=E_[:, cs_], in_=pS[:, cs_], func=AF.Exp, scale=0.125),
                              r=[bpS], w=[b_e[j]])
                        cx.op("scalar", lambda e: e.activation(out=SP[:, cs_], in_=E_[:, cs_], func=AF.Ln, bias=1.0, scale=1.0),
                              r=[b_e[j]], w=[b_sp[j]])
                        if diag:
                            cx.op("vector", lambda e: e.tensor_tensor(out=SP[:, n0:n0 + 128], in0=SP[:, n0:n0 + 128],
                                                                      in1=tri_lt, op=ALU.mult),
                                  r=[b_sp[j]], w=[b_sp[j]])
                        cx.op("gpsimd", lambda e: e.tensor_copy(out=HI[:, cs_], in_=SP[:, cs_]), r=[b_sp[j]], w=[b_hi[j]])
                        cx.op("vector", lambda e: e.tensor_tensor(out=LO[:, cs_], in0=SP[:, cs_], in1=HI[:, cs_],
                                                                  op=ALU.subtract),
                              r=[b_sp[j], b_hi[j]], w=[b_lo[j]])
                        cx.op("tensor", lambda e: e.matmul(pC[:, cs_], lhsT=UTneg, rhs=HI[:, cs_], start=True, stop=False),
                              r=[b_hi[j]], w=[bpC])
                        cx.op("tensor", lambda e: e.matmul(pC[:, cs_], lhsT=UTneg, rhs=LO[:, cs_], start=False, stop=True),
                              r=[b_lo[j]], acc=[bpC])
                        cx.op("tensor", lambda e: e.matmul(pR[:, cs_], lhsT=onesneg, rhs=HI[:, cs_], start=True, stop=False),
                              w=[bpR])
                        cx.op("tensor", lambda e: e.matmul(pR[:, cs_], lhsT=onesneg, rhs=LO[:, cs_], start=False, stop=True),
                              acc=[bpR])
                        cx.op("vector", lambda e: e.tensor_tensor(out=T_[:, cs_], in0=pC[:, cs_], in1=carry[:, cs_], op=ALU.add),
                              r=[bpC, b_carry], w=[b_t[j]])
                        cx.op("vector", lambda e: e.tensor_tensor(out=carry[:, cs_], in0=pR[:, cs_], in1=carry[:, cs_], op=ALU.add),
                              r=[bpR], w=[b_carry])
                        cx.op("scalar", lambda e: e.activation(out=X_[:, cs_], in_=T_[:, cs_], func=AF.Exp),
                              r=[b_t[j]], w=[b_X[j]])
                        cx.op("gpsimd", lambda e: e.tensor_tensor(out=A_[:, cs_], in0=E_[:, cs_], in1=X_[:, cs_], op=ALU.mult),
                              r=[b_e[j], b_X[j]], w=[b_a[j]])
                        if diag:
                            cx.op("gpsimd", lambda e: e.tensor_tensor(out=A_[:, n0:n0 + 128], in0=A_[:, n0:n0 + 128],
                                                                      in1=tri_lt, op=ALU.mult),
                                  r=[b_a[j]], w=[b_a[j]])
                        cx.op("tensor", lambda e: e.matmul(po[:, cs_], lhsT=V[:, kt, :], rhs=A_[:, cs_],
                                                           start=False, stop=(kt == 0)),
                              r=[b_a[j], bV[i]], acc=[bpo])
                    cx.op("vector", lambda e: e.tensor_tensor(out=ow[qb % 2][:, :], in0=po[0:64, :], in1=gt[qb % 2][:, :],
                                                              op=ALU.mult),
                          r=[bpo, b_gt[qb % 2]], w=[b_ow[qb % 2]])
                    cx.dma(WD_d[rr:rr + 64, q0:q0 + 512], ow[qb % 2][:, :], r=[b_ow[qb % 2]], w=[b_WD])
            cx.barrier()

    def phase2d(l):
        with ExitStack() as ph:
            Qd = [sb("Qd%d" % i, [64, S], BF16, ph) for i in range(2)]
            Kd = [sb("Kd%d" % i, [64, S], BF16, ph) for i in range(2)]
            Vd = [sb("Vd%d" % i, [128, 32, 128], BF16, ph) for i in range(2)]
            bQ, bK, bV = [Buf(), Buf()], [Buf(), Buf()], [Buf(), Buf()]
            accs = [sb("dacc%d" % i, [128, S], F32, ph) for i in range(2)]
            b_acc = [Buf(), Buf()]
            rzl = sb("drzl", [64, S], F32, ph)
            b_rzl = Buf()
            gtd = sb("dgt", [64, S], F32, ph)
            b_gtd = Buf()
            owd = sb("dow", [64, S], BF16, ph)
            b_owd = Buf()
            NP_ = 3
            Pt = [sb("dP%d" % i, [128, 256], BF16, ph) for i in range(NP_)]
            b_P = [Buf() for _ in range(NP_)]
            it = 0
            bi = 0
            pi = 0
            for h in range(debug.get("p2d_heads", 4)):
                acc, bacc = accs[h % 2], b_acc[h % 2]
                for g in range(3):
                    dil = (1, 4, 16)[g]
                    hg = g * 4 + h
                    i = bi % 2
                    bi += 1
                    Q, K, V = Qd[i], Kd[i], Vd[i]
                    cx.dma(Q[:, :], QK[R_QD + 64 * hg:R_QD + 64 * (hg + 1), :], r=[b_QK], w=[bQ[i]])
                    cx.dma(K[:, :], QK[R_KD + 64 * hg:R_KD + 64 * (hg + 1), :], r=[b_QK], w=[bK[i]])
                    cx.dma(V[:, :, :], VD_d[g][:, h, :].rearrange("(kt p) c -> p kt c", p=128), r=[b_VD], w=[bV[i]])
                    nt = (S // dil) // 128
                    for c in range(dil):
                        prev = None
                        for k in range(nt):
                            N = 256 if k + 1 < nt else 128
                            base = c + dil * 128 * k
                            ktok = slice(base, base + dil * 127 + 1, dil)
                            qtok = slice(base, base + dil * (N - 1) + 1, dil)
                            pS, bpS = PS[it % 3], bPS[it % 3]
                            P, bP = Pt[it % NP_], b_P[it % NP_]
                            it += 1
                            cx.op("tensor", lambda e: e.matmul(pS[:, 0:N], lhsT=K[:, ktok], rhs=Q[:, qtok], start=True, stop=True),
                                  r=[bQ[i], bK[i]], w=[bpS])
                            cx.op("scalar", lambda e: e.activation(out=P[:, 0:N], in_=pS[:, 0:N], func=AF.Exp, scale=0.125),
                                  r=[bpS], w=[bP])
                            cx.op("gpsimd", lambda e: e.tensor_tensor(out=P[:, 0:N], in0=P[:, 0:N], in1=band[:, 0:N], op=ALU.mult),
                                  r=[bP], w=[bP])
                            po, bpo = PS[3 + pi % 2], bPS[3 + pi % 2]
                            pi += 1
                            ti = c * nt + k
                            if prev is not None:
                                Pp, bPp = prev
                                cx.op("tensor", lambda e: e.matmul(po[:, 0:128], lhsT=V[:, ti - 1, :], rhs=Pp[:, 128:256],
                                                                   start=True, stop=False), r=[bPp, bV[i]], w=[bpo])
                                cx.op("tensor", lambda e: e.matmul(po[:, 0:128], lhsT=V[:, ti, :], rhs=P[:, 0:128],
                                                                   start=False, stop=True), r=[bP], acc=[bpo])
                            else:
                                cx.op("tensor", lambda e: e.matmul(po[:, 0:128], lhsT=V[:, ti, :], rhs=P[:, 0:128],
                                                                   start=True, stop=True), r=[bP, bV[i]], w=[bpo])
                            prev = (P, bP)
                            av = acc[:, ktok]
                            if g == 0:
                                cx.op("scalar", lambda e: e.activation(out=av, in_=po[:, 0:128], func=AF.Copy),
                                      r=[bpo], w=[], acc=[bacc])
                            else:
                                cx.op("vector", lambda e: e.tensor_tensor(out=av, in0=po[:, 0:128], in1=av, op=ALU.add),
                                      r=[bpo, bacc] if (c == 0 and k == 0) else [bpo], w=[], acc=[bacc])
                rr = 768 + 64 * h
                cx.dma(gtd[:, :], G_d[rr:rr + 64, :], r=[b_G], w=[b_gtd])
                cx.op("vector", lambda e: e.reciprocal(out=acc[64:128, :], in_=acc[64:128, :]), r=[bacc], w=[bacc])
                cx.dma(rzl[:, :], acc[64:128, :], r=[bacc], w=[b_rzl])
                cx.op("vector", lambda e: e.tensor_tensor(out=acc[0:64, :], in0=acc[0:64, :], in1=rzl[:, :], op=ALU.mult),
                      r=[b_rzl], w=[bacc])
                cx.op("gpsimd", lambda e: e.tensor_tensor(out=owd[:, :], in0=acc[0:64, :], in1=gtd[:, :], op=ALU.mult),
                      r=[bacc, b_gtd], w=[b_owd])
                cx.dma(WD_d[rr:rr + 64, :], owd[:, :], r=[b_owd], w=[b_WD])
            cx.barrier()

    def phase2b(l):
        with ExitStack() as ph:
            KCm = sb("KCm", [64, 256], BF16, ph)
            VCa = [sb("VCa%d" % i, [128, 128], BF16, ph) for i in range(2)]
            OVb = sb("OVb", [128, 256], BF16, ph)
            Sel = sb("Sel", [12, 12 * 128], F32, ph)
            NGs = sb("NGs", [12, S], F32, ph)
            b_KCm, b_VCa, b_cst, b_NGs = Buf(), Buf(), Buf(), Buf()
            cx.dma(OVb[:], ovb_c[:, :], w=[b_cst])
            cx.dma(Sel[:], sel_c[:, :], w=[b_cst])
            cx.dma(NGs[:], NG_d[:, :], r=[b_NG], w=[b_NGs])
            with ExitStack() as p1_:
                KV = sb("KVs", [128, S], F32, p1_)
                peT = sb("peT", [128, 32], F32, p1_)
                W1s = sb("W1s", [128, 32, 256], F32, p1_)
                W1 = sb("W1", [128, 32, 256], BF16, p1_)
                W2s = sb("W2s", [128, 4, 64], F32, p1_)
                W2 = sb("W2", [128, 4, 64], BF16, p1_)
                X = sb("Xc", [128, 32, 256], BF16, p1_)
                hid = [sb("hid%d" % i, [128, 256], BF16, p1_) for i in range(4)]
                x2 = sb("gx2", [128, 256], F32, p1_)
                u_ = sb("gu", [128, 256], F32, p1_)
                b_KV, b_pe, b_W1s, b_W1, b_W2s, b_W2, b_X, b_x2, b_u = [Buf() for _ in range(9)]
                b_hid = [Buf() for _ in range(4)]
                cx.dma(KV[:], KVC[:, :], r=[b_KVC], w=[b_KV])
                cx.dma(peT[:], cmp_peT[l], w=[b_pe])
                for kv in range(2):
                    cx.dma(W1s[kv * 64:(kv + 1) * 64, :, :], cmp_w1[l, kv].rearrange("(l d) j -> d l j", d=64), w=[b_W1s])
                    cx.dma(W2s[:, kv * 2:kv * 2 + 2, :], cmp_w2[l, kv].rearrange("(jc p) d -> p jc d", p=128), w=[b_W2s])
                cx.op("gpsimd", lambda e: e.tensor_copy(out=W1[:], in_=W1s[:]), r=[b_W1s], w=[b_W1])
                cx.op("gpsimd", lambda e: e.tensor_copy(out=W2[:], in_=W2s[:]), r=[b_W2s], w=[b_W2])
                cx.op("vector", lambda e: e.memset(X[:], 0.0), w=[b_X])
                for ll in range(32):
                    cx.op("vector", lambda e: e.tensor_scalar(out=X[:, ll, 0:255], in0=KV[:, ll:ll + 16 * 254 + 1:16],
                                                              scalar1=peT[:, ll:ll + 1], scalar2=None, op0=ALU.add),
                          r=[b_KV, b_pe], w=[], acc=[b_X])
                for i in range(4):
                    cx.op("gpsimd", lambda e: e.memset(hid[i][:], 0.0), w=[b_hid[i]])
                for i in range(2):
                    cx.op("gpsimd", lambda e: e.memset(VCa[i][:], 1.0), w=[b_VCa])
                cx.op("gpsimd", lambda e: e.memset(KCm[:], 0.0), w=[b_KCm])
                for kv in range(2):
                    for jc in range(2):
                        pp, bpp = PS[kv * 2 + jc], bPS[kv * 2 + jc]
                        for ll in range(32):
                            cx.op("tensor", lambda e: e.matmul(pp[:, 0:255], lhsT=W1[kv * 64:(kv + 1) * 64, ll, jc * 128:(jc + 1) * 128],
                                                               rhs=X[kv * 64:(kv + 1) * 64, ll, 0:255],
                                                               start=(ll == 0), stop=(ll == 31)),
                                  r=[b_W1, b_X] if ll == 0 else [], w=[bpp] if ll == 0 else [], acc=[] if ll == 0 else [bpp])
                        hh = hid[kv * 2 + jc]
                        bh = b_hid[kv * 2 + jc]
                        cx.op("scalar", lambda e: e.activation(out=x2[:, 0:255], in_=pp[:, 0:255], func=AF.Square), r=[bpp], w=[b_x2])
                        cx.op("vector", lambda e: e.tensor_scalar(out=x2[:, 0:255], in0=x2[:, 0:255], scalar1=0.044715, scalar2=1.0,
                                                                  op0=ALU.mult, op1=ALU.add), r=[b_x2], w=[b_x2])
                        cx.op("vector", lambda e: e.tensor_tensor(out=u_[:, 0:255], in0=pp[:, 0:255], in1=x2[:, 0:255], op=ALU.mult),
                              r=[bpp, b_x2], w=[b_u])
                        cx.op("scalar", lambda e: e.activation(out=u_[:, 0:255], in_=u_[:, 0:255], func=AF.Sigmoid,
                                                               scale=1.5957691216057308), r=[b_u], w=[b_u])
                        cx.op("vector", lambda e: e.tensor_tensor(out=hh[:, 0:255], in0=pp[:, 0:255], in1=u_[:, 0:255], op=ALU.mult),
                              r=[bpp, b_u], w=[bh])
                pk, bpk = PS[4], bPS[4]
                for jc in range(2):
                    cx.op("tensor", lambda e: e.matmul(pk[0:64, 0:255], lhsT=W2[:, jc, :], rhs=hid[jc][:, 0:255],
                                                       start=(jc == 0), stop=(jc == 1)),
                          r=[b_W2, b_hid[jc]], w=[bpk] if jc == 0 else [], acc=[] if jc == 0 else [bpk])
                cx.op("scalar", lambda e: e.activation(out=KCm[:, 0:255], in_=pk[0:64, 0:255], func=AF.Copy), r=[bpk], w=[b_KCm])
                for nt in range(2):
                    rows = 128 if nt == 0 else 127
                    pv, bpv = PS[5 + nt], bPS[5 + nt]
                    for jc in range(2):
                        cx.op("tensor", lambda e: e.matmul(pv[0:rows, 0:64], lhsT=hid[2 + jc][:, nt * 128:nt * 128 + rows],
                                                           rhs=W2[:, 2 + jc, :], start=(jc == 0), stop=(jc == 1)),
                              r=[b_W2, b_hid[2 + jc]], w=[bpv] if jc == 0 else [], acc=[] if jc == 0 else [bpv])
                    cx.op("scalar", lambda e: e.activation(out=VCa[nt][0:rows, 64:128], in_=pv[0:rows, 0:64], func=AF.Copy),
                          r=[bpv], w=[b_VCa])
                cx.barrier()
            if debug.get("p2b_level", 9) < 1:
                return
            impT = sb("impT", [64, S], F32, ph)
            b_imp = Buf()
            with ExitStack() as p2_:
                QB4 = sb("QB4", [64, 4, S], BF16, p2_)
                vis = sb("vis", [128, 2, S], BF16, p2_)
                b_QB4, b_vis = Buf(), Buf()
                for h in range(4):
                    cx.dma(QB4[:, h, :], QK[R_QB + 64 * h:R_QB + 64 * (h + 1), :], r=[b_QK], w=[b_QB4])
                for nt in range(2):
                    cx.dma(vis[:, nt, :], vis_c[nt], w=[b_vis])
                Pc = [sb("Pc%d" % i, [128, 512], BF16, p2_) for i in range(4)]
                b_Pc = [Buf() for _ in range(4)]
                rza = [sb("rza%d" % i, [128, 512], F32, p2_) for i in range(2)]
                ocm = [sb("ocm%d" % i, [128, 512], F32, p2_) for i in range(2)]
                imt = [sb("imt%d" % i, [64, 512], F32, p2_) for i in range(2)]
                b_rza, b_ocm, b_imt = [Buf(), Buf()], [Buf(), Buf()], [Buf(), Buf()]
                it = 0
                ei = 0
                for qb in range(8):
                    q0 = qb * 512
                    for h in range(4):
                        nts = [0] if qb < 4 else [0, 1]
                        Ps = []
                        for nt in nts:
                            pS, bpS = PS[it % 2], bPS[it % 2]
                            P, bP = Pc[it % 4], b_Pc[it % 4]
                            it += 1
                            cx.op("tensor", lambda e: e.matmul(pS[:, :], lhsT=KCm[:, nt * 128:(nt + 1) * 128], rhs=QB4[:, h, q0:q0 + 512],
                                                               start=True, stop=True), r=[b_KCm, b_QB4], w=[bpS])
                            cx.op("scalar", lambda e: e.activation(out=P[:, :], in_=pS[:, :], func=AF.Exp, scale=0.125), r=[bpS], w=[bP])
                            cx.op("gpsimd", lambda e: e.tensor_tensor(out=P[:, :], in0=P[:, :], in1=vis[:, nt, q0:q0 + 512], op=ALU.mult),
                                  r=[bP, b_vis], w=[bP])
                            Ps.append((nt, P, bP))
                        j = ei % 2
                        ei += 1
                        pa_, bpa = PS[2 + j], bPS[2 + j]
                        pb_, bpb = PS[4 + j], bPS[4 + j]
                        pg_, bpg = PS[6 + j], bPS[6 + j]
                        for k_, (nt, P, bP) in enumerate(Ps):
                            cx.op("tensor", lambda e: e.matmul(pa_[:, :], lhsT=VCa[nt][:, :], rhs=P[:, :], start=(k_ == 0), stop=(k_ == len(Ps) - 1)),
                                  r=[bP, b_VCa], w=[bpa] if k_ == 0 else [], acc=[] if k_ == 0 else [bpa])
                        for k_, (nt, P, bP) in enumerate(Ps):
                            cx.op("tensor", lambda e: e.matmul(pb_[:, :], lhsT=OVb[:, nt * 128:(nt + 1) * 128], rhs=P[:, :], start=(k_ == 0), stop=(k_ == len(Ps) - 1)),
                                  r=[bP], w=[bpb] if k_ == 0 else [], acc=[] if k_ == 0 else [bpb])
                        cx.op("tensor", lambda e: e.matmul(pg_[:, :], lhsT=Sel[:, (0 * 4 + h) * 128:(0 * 4 + h + 1) * 128], rhs=NGs[:, q0:q0 + 512],
                                                           start=True, stop=True), r=[b_NGs], w=[bpg])
                        RZ, bRZ = rza[j], b_rza[j]
                        cx.op("vector", lambda e: e.tensor_scalar(out=RZ[0:64, :], in0=pa_[0:64, :], scalar1=1e-30, scalar2=None, op0=ALU.max),
                              r=[bpa], w=[bRZ])
                        cx.op("vector", lambda e: e.reciprocal(out=RZ[0:64, :], in_=RZ[0:64, :]), r=[bRZ], w=[bRZ])
                        cx.op("vector", lambda e: e.tensor_scalar(out=RZ[64:128, :], in0=pb_[64:128, :], scalar1=1e-30, scalar2=None, op0=ALU.max),
                              r=[bpb], w=[bRZ])
                        cx.op("vector", lambda e: e.reciprocal(out=RZ[64:128, :], in_=RZ[64:128, :]), r=[bRZ], w=[bRZ])
                        OC, bOC = ocm[j], b_ocm[j]
                        cx.op("vector", lambda e: e.tensor_tensor(out=OC[64:128, :], in0=pa_[64:128, :], in1=RZ[64:128, :], op=ALU.mult),
                              r=[bpa, bRZ], w=[bOC])
                        cx.op("vector", lambda e: e.tensor_tensor(out=OC[64:128, :], in0=pg_[64:128, :], in1=OC[64:128, :], op=ALU.mult),
                              r=[bpg, bOC], w=[bOC])
                        cx.dma(OB_d[0][64 * h:64 * (h + 1), q0:q0 + 512], OC[64:128, :], r=[bOC], w=[b_OB[0]])
                        if h == 0:
                            cx.op("vector", lambda e: e.tensor_tensor(out=impT[:, q0:q0 + 512], in0=pb_[0:64, :], in1=RZ[0:64, :], op=ALU.mult),
                                  r=[bpb, bRZ], w=[b_imp])
                        else:
                            IT, bIT = imt[j], b_imt[j]
                            cx.op("vector", lambda e: e.tensor_tensor(out=IT[:, :], in0=pb_[0:64, :], in1=RZ[0:64, :], op=ALU.mult),
                                  r=[bpb, bRZ], w=[bIT])
                            cx.op("gpsimd", lambda e: e.tensor_tensor(out=impT[:, q0:q0 + 512], in0=impT[:, q0:q0 + 512], in1=IT[:, :], op=ALU.add),
                                  r=[bIT], w=[b_imp])
                cx.barrier()
            if debug.get("p2b_level", 9) < 2:
                return
            QS = [sb("QS%d" % i, [128, S], BF16, ph) for i in range(4)]
            b_QSq = [Buf() for _ in range(4)]
            b_QSb = [Buf() for _ in range(4)]
            for h in range(4):
                cx.dma(QS[h][0:64, :], QK[R_QBR + 64 * h:R_QBR + 64 * (h + 1), :], r=[b_QK], w=[b_QSq[h]])
            with ExitStack() as p3_:
                Fm = sb("Fm", [128, 32 * 64], F32, p3_)
                PENT = sb("PENT", [128, S], BF16, p3_)
                IM = sb("IM", [128, 64], F32, p3_)
                IM2 = sb("IM2", [128, 64], F32, p3_)
                m8a = sb("m8a", [128, 8], F32, p3_)
                m8b = sb("m8b", [128, 8], F32, p3_)
                PT = sb("PTs", [128, 128], F32, p3_)
                b_Fm, b_PENT, b_IM, b_IM2, b_m8a, b_m8b, b_PT = [Buf() for _ in range(7)]
                cx.dma(Fm[:], fm_c[:, :], w=[b_Fm])
                cx.op("vector", lambda e: e.memset(PT[:, 0:64], 0.0), w=[b_PT])
                for tt in range(32):
                    tk = slice(tt * 128, (tt + 1) * 128)
                    fm = Fm[:, tt * 64:(tt + 1) * 64]
                    if tt < 8:
                        cx.op("vector", lambda e: e.tensor_scalar(out=PT[:, 64:128], in0=fm, scalar1=-1e29, scalar2=NEG,
                                                                  op0=ALU.is_lt, op1=ALU.mult), r=[b_Fm], w=[b_PT])
                    else:
                        p1x, bp1x = PS[tt % 2], bPS[tt % 2]
                        cx.op("tensor", lambda e: e.transpose(p1x[0:128, 0:64], impT[0:64, tk], ident[0:64, 0:64]), r=[b_imp], w=[bp1x])
                        cx.op("vector", lambda e: e.tensor_tensor(out=IM[:, :], in0=p1x[:, 0:64], in1=fm, op=ALU.add), r=[bp1x, b_Fm], w=[b_IM])
                        cx.op("vector", lambda e: e.max(out=m8a[:, :], in_=IM[:, :]), r=[b_IM], w=[b_m8a])
                        cx.op("vector", lambda e: e.match_replace(out=IM2[:, :], in_to_replace=m8a[:, :], in_values=IM[:, :], imm_value=-3e30),
                              r=[b_IM, b_m8a], w=[b_IM2])
                        cx.op("vector", lambda e: e.max(out=m8b[:, :], in_=IM2[:, :]), r=[b_IM2], w=[b_m8b])
                        cx.op("vector", lambda e: e.tensor_scalar(out=PT[:, 64:128], in0=IM[:, :], scalar1=m8b[:, 7:8], scalar2=NEG,
                                                                  op0=ALU.is_lt, op1=ALU.mult), r=[b_IM, b_m8b], w=[b_PT])
                    p2x, bp2x = PS[2 + tt % 2], bPS[2 + tt % 2]
                    cx.op("tensor", lambda e: e.transpose(p2x[:, 0:128], PT[:, :], ident), r=[b_PT], w=[bp2x])
                    cx.op("scalar", lambda e: e.activation(out=PENT[64:128, tk], in_=p2x[64:128, 0:128], func=AF.Copy), r=[bp2x], w=[], acc=[b_PENT])
                for h in range(4):
                    cx.op("gpsimd", lambda e: e.tensor_copy(out=QS[h][64:128, :], in_=PENT[64:128, :]), r=[b_PENT], w=[b_QSb[h]])
                cx.barrier()
            if debug.get("p2b_level", 9) < 3:
                return
            with ExitStack() as p4_:
                KS = sb("KS", [128, S], BF16, p4_)
                VS = sb("VS", [128, 32, 128], BF16, p4_)
                KW = sb("KW", [64, S], BF16, p4_)
                VW = sb("VW", [128, 32, 128], BF16, p4_)
                b_KS, b_VS, b_KW, b_VW = Buf(), Buf(), Buf(), Buf()
                cx.dma(KS[0:64, :], QK[R_KSW:R_KSW + 64, :], r=[b_QK], w=[b_KS])
                cx.dma(KS[64:128, :], ind64[:, :], w=[b_KS])
                cx.dma(VS[:, :, :], VB_d[:, 0, :].rearrange("(kt p) c -> p kt c", p=128), r=[b_VB], w=[b_VS])
                cx.dma(KW[:, :], QK[R_KSW + 64:R_KSW + 128, :], r=[b_QK], w=[b_KW])
                cx.dma(VW[:, :, :], VB_d[:, 1, :].rearrange("(kt p) c -> p kt c", p=128), r=[b_VB], w=[b_VW])
                rz = [sb("brz%d" % i, [128, 512], F32, p4_) for i in range(2)]
                on = [sb("bon%d" % i, [64, 512], F32, p4_) for i in range(2)]
                b_rz, b_on = [Buf(), Buf()], [Buf(), Buf()]
                cnt = [0]

                def make_ep(branch, h):
                    def ep(qb, po, bpo):
                        i = cnt[0] % 2
                        cnt[0] += 1
                        q0 = qb * 512
                        pg_, bpg = PS[6 + i], bPS[6 + i]
                        cx.op("tensor", lambda e: e.matmul(pg_[:, :], lhsT=Sel[:, (branch * 4 + h) * 128:(branch * 4 + h + 1) * 128],
                                                           rhs=NGs[:, q0:q0 + 512], start=True, stop=True), r=[b_NGs], w=[bpg])
                        cx.op("vector", lambda e: e.reciprocal(out=rz[i][64:128, :], in_=po[64:128, :]), r=[bpo], w=[b_rz[i]])
                        cx.op("vector", lambda e: e.tensor_tensor(out=on[i][:, :], in0=po[0:64, :], in1=rz[i][64:128, :], op=ALU.mult),
                              r=[bpo, b_rz[i]], w=[b_on[i]])
                        cx.op("vector", lambda e: e.tensor_tensor(out=on[i][:, :], in0=pg_[0:64, :], in1=on[i][:, :], op=ALU.mult),
                              r=[bpg, b_on[i]], w=[b_on[i]])
                        cx.dma(OB_d[branch][64 * h:64 * (h + 1), q0:q0 + 512], on[i][:, :], r=[b_on[i]], w=[b_OB[branch]])
                    return ep
                ptp = make_pt(p4_, "b")
                if debug.get("p2b_level", 9) >= 3:
                    for h in range(debug.get("p2b_heads", 4)):
                        causal_attn(ptp, QS[h], [b_QSq[h], b_QSb[h]], KS, [b_KS], 128, VS, [b_VS], make_ep(1, h))
                if debug.get("p2b_level", 9) >= 4:
                    pt, b_pt, itc = ptp
                    it = itc[0]
                    for h in range(debug.get("p2b_heads", 4)):
                        ep = make_ep(2, h)
                        for qb in range(8):
                            q0 = qb * 512
                            po, bpo = PS[3 + qb % 2], bPS[3 + qb % 2]
                            tiles = [(kt, 128 * (kt - 4 * qb), 512, 128 * (kt - 4 * qb), tri_le) for kt in range(4 * qb, 4 * qb + 4)]
                            if qb > 0:
                                tiles += [(kt, 0, 128 * (kt - 4 * qb + 5), 128 * (kt - 4 * qb + 4), tri_gt) for kt in range(4 * qb - 4, 4 * qb)]
                            for k_, (kt, c0, c1, m0, mk) in enumerate(tiles):
                                pS, bpS = PS[it % 3], bPS[it % 3]
                                P, bP = pt[it % 3], b_pt[it % 3]
                                it += 1
                                cx.op("tensor", lambda e: e.matmul(pS[:, c0:c1], lhsT=KW[:, kt * 128:(kt + 1) * 128], rhs=QS[h][0:64, q0 + c0:q0 + c1],
                                                                   start=True, stop=True), r=[b_QSq[h], b_KW], w=[bpS])
                                cx.op("scalar", lambda e: e.activation(out=P[:, c0:c1], in_=pS[:, c0:c1], func=AF.Exp, scale=0.125), r=[bpS], w=[bP])
                                cx.op("gpsimd", lambda e: e.tensor_tensor(out=P[:, m0:m0 + 128], in0=P[:, m0:m0 + 128], in1=mk, op=ALU.mult),
                                      r=[bP], w=[bP])
                                cx.op("tensor", lambda e: e.matmul(po[:, c0:c1], lhsT=VW[:, kt, :], rhs=P[:, c0:c1],
                                                                   start=(k_ == 0), stop=(k_ == len(tiles) - 1)),
                                      r=[bP, b_VW], w=[bpo] if k_ == 0 else [], acc=[] if k_ == 0 else [bpo])
                            ep(qb, po, bpo)
                    itc[0] = it
                cx.barrier()
            if debug.get("p2b_level", 9) < 5:
                return
            with ExitStack() as p5_:
                ta = [sb("cmb%d" % i, [64, S], F32, p5_) for i in range(4)]
                b_ta = [Buf() for _ in range(4)]
                oo = sb("cmbo", [64, S], BF16, p5_)
                b_oo = Buf()
                for h in range(4):
                    rr = 256 + 64 * h
                    for i in range(3):
                        cx.dma(ta[i][:, :], OB_d[i][64 * h:64 * (h + 1), :], r=[b_OB[i]], w=[b_ta[i]])
                    cx.dma(ta[3][:, :], G_d[rr:rr + 64, :], r=[b_G], w=[b_ta[3]])
                    cx.op("vector", lambda e: e.tensor_tensor(out=ta[0][:, :], in0=ta[0][:, :], in1=ta[1][:, :], op=ALU.add),
                          r=[b_ta[1]], w=[b_ta[0]])
                    cx.op("gpsimd", lambda e: e.tensor_tensor(out=ta[0][:, :], in0=ta[0][:, :], in1=ta[2][:, :], op=ALU.add),
                          r=[b_ta[2]], w=[b_ta[0]])
                    cx.op("vector", lambda e: e.tensor_tensor(out=oo[:, :], in0=ta[0][:, :], in1=ta[3][:, :], op=ALU.mult),
                          r=[b_ta[0], b_ta[3]], w=[b_oo])
                    cx.dma(WD_d[rr:rr + 64, :], oo[:, :], r=[b_oo], w=[b_WD])
                cx.barrier()

    def phase3(l, x_src, b_x_src, x_dst, b_x_dst, last):
        with ExitStack() as ph:
            Wm = sb("Wm", [128, 8, 4096], BF16, ph)
            Wu = sb("Wu", [128, 8, 1024], BF16, ph)
            Wo = sb("Wo", [128, 8, 1024], BF16, ph)
            b_W = Buf()
            stg = ExitStack()
            wst = [sb("w3st%d" % i, [128, 8, 512], F32, stg) for i in range(2)]
            b_wst = [Buf(), Buf()]
            k = 0
            for j in range(8):
                cx.dma(wst[k % 2][:], w_in[l][:, C_MG + 512 * j:C_MG + 512 * (j + 1)].rearrange("(kc p) c -> p kc c", p=128),
                       w=[b_wst[k % 2]])
                cx.op("gpsimd", lambda e: e.tensor_copy(out=Wm[:, :, 512 * j:512 * (j + 1)], in_=wst[k % 2][:]),
                      r=[b_wst[k % 2]], acc=[b_W])
                k += 1
            for j in range(2):
                cx.dma(wst[k % 2][:], w_up[l].rearrange("b (kc p) d -> p (b kc) d", p=128)[:, :, 512 * j:512 * (j + 1)],
                       w=[b_wst[k % 2]])
                cx.op("gpsimd", lambda e: e.tensor_copy(out=Wu[:, :, 512 * j:512 * (j + 1)], in_=wst[k % 2][:]),
                      r=[b_wst[k % 2]], acc=[b_W])
                k += 1
            for j in range(2):
                cx.dma(wst[k % 2][:], w_out[l][:, 512 * j:512 * (j + 1)].rearrange("(kc p) c -> p kc c", p=128),
                       w=[b_wst[k % 2]])
                cx.op("gpsimd", lambda e: e.tensor_copy(out=Wo[:, :, 512 * j:512 * (j + 1)], in_=wst[k % 2][:]),
                      r=[b_wst[k % 2]], acc=[b_W])
                k += 1
            cx.barrier()
            stg.close()
            hb = [sb("hb%d" % i, [128, 8, 512], BF16, ph) for i in range(2)]
            wb = [sb("wb%d" % i, [128, 8, 512], BF16, ph) for i in range(2)]
            xb1 = sb("xb", [128, 8, 512], F32, ph)
            xb = [xb1, xb1]
            b_xb1 = Buf()
            b_hb, b_wb, b_xb = [Buf(), Buf()], [Buf(), Buf()], [b_xb1, b_xb1]
            yac = sb("yac", [128, 512], F32, ph)
            b_yac = Buf()
            ybf = sb("ybf", [128, 8, 512], BF16, ph)
            b_ybf = Buf()
            sg = [sb("sg%d" % i, [128, 512], F32, ph) for i in range(2)]
            tm = [sb("tm%d" % i, [128, 512], F32, ph) for i in range(2)]
            b_sg, b_tm = [Buf(), Buf()], [Buf(), Buf()]
            xn = sb("xn", [128, 8, 512], F32, ph)
            b_xn = Buf()
            sq = sb("sq3", [128, 8, 512], F32, ph)
            b_sq = Buf()
            rstd = sb("rstd3", [128, 512], F32, ph)
            b_rstd = Buf()
            pi = 0
            ci = 0
            for tb in range(8):
                t0 = tb * 512
                i = tb % 2
                cx.dma(hb[i][:], hT_d[:, t0:t0 + 512].rearrange("(kc p) t -> p kc t", p=128), r=[b_hT_d], w=[b_hb[i]])
                cx.dma(wb[i][:], WD_d[:, t0:t0 + 512].rearrange("(kc p) t -> p kc t", p=128), r=[b_WD], w=[b_wb[i]])
                cx.dma(xb[i][:], x_src[:, t0:t0 + 512].rearrange("(kc p) t -> p kc t", p=128), r=[b_x_src], w=[b_xb[i]])
                for dc in range(8):
                    for br in range(4):
                        pm, bpm = PS[pi % 8], bPS[pi % 8]
                        pu, bpu = PS[(pi + 1) % 8], bPS[(pi + 1) % 8]
                        pi += 2
                        c0 = br * 1024 + dc * 128
                        for kc in range(8):
                            cx.op("tensor", lambda e: e.matmul(pm[:, :], lhsT=Wm[:, kc, c0:c0 + 128], rhs=hb[i][:, kc, :],
                                                               start=(kc == 0), stop=(kc == 7)),
                                  r=[b_W, b_hb[i]] if kc == 0 else [], w=[bpm] if kc == 0 else [],
                                  acc=[] if kc == 0 else [bpm])
                        for kc in range(2):
                            cx.op("tensor", lambda e: e.matmul(pu[:, :], lhsT=Wu[:, br * 2 + kc, dc * 128:(dc + 1) * 128],
                                                               rhs=wb[i][:, br * 2 + kc, :],
                                                               start=(kc == 0), stop=(kc == 1)),
                                  r=[b_W, b_wb[i]] if kc == 0 else [], w=[bpu] if kc == 0 else [],
                                  acc=[] if kc == 0 else [bpu])
                        j = ci % 2
                        ci += 1
                        cx.op("scalar", lambda e: e.activation(out=sg[j][:], in_=pm[:, :], func=AF.Sigmoid),
                              r=[bpm], w=[b_sg[j]])
                        if br == 0:
                            cx.op("vector", lambda e: e.tensor_tensor(out=yac[:], in0=pu[:, :], in1=sg[j][:], op=ALU.mult),
                                  r=[bpu, b_sg[j]], w=[b_yac])
                        else:
                            cx.op("vector", lambda e: e.tensor_tensor(out=tm[j][:], in0=pu[:, :], in1=sg[j][:], op=ALU.mult),
                                  r=[bpu, b_sg[j]], w=[b_tm[j]])
                            if br < 3:
                                cx.op("gpsimd", lambda e: e.tensor_tensor(out=yac[:], in0=yac[:], in1=tm[j][:], op=ALU.add),
                                      r=[b_tm[j]], w=[b_yac])
                            else:
                                cx.op("gpsimd", lambda e: e.tensor_tensor(out=ybf[:, dc, :], in0=yac[:], in1=tm[j][:], op=ALU.add),
                                      r=[b_tm[j], b_yac], w=[b_ybf])
                for dp in range(8):
                    po, bpo = PS[pi % 8], bPS[pi % 8]
                    pi += 1
                    for dc in range(8):
                        cx.op("tensor", lambda e: e.matmul(po[:, :], lhsT=Wo[:, dc, dp * 128:(dp + 1) * 128], rhs=ybf[:, dc, :],
                                                           start=(dc == 0), stop=(dc == 7)),
                              r=[b_W, b_ybf] if dc == 0 else [], w=[bpo] if dc == 0 else [],
                              acc=[] if dc == 0 else [bpo])
                    cx.op("vector", lambda e: e.tensor_tensor(out=xn[:, dp, :], in0=po[:, :], in1=xb[i][:, dp, :], op=ALU.add),
                          r=[bpo, b_xb[i]], w=[b_xn])
                if not last:
                    cx.dma(x_dst[:, t0:t0 + 512].rearrange("(kc p) t -> p kc t", p=128), xn[:], r=[b_xn], w=[b_x_dst])
                else:
                    cx.op("scalar", lambda e: e.activation(out=sq[:], in_=xn[:], func=AF.Square), r=[b_xn], w=[b_sq])
                    pm, bpm = PS[pi % 8], bPS[pi % 8]
                    pi += 1
                    for kc in range(8):
                        cx.op("tensor", lambda e: e.matmul(pm[:, :], lhsT=avg, rhs=sq[:, kc, :], start=(kc == 0), stop=(kc == 7)),
                              r=[b_sq] if kc == 0 else [], w=[bpm] if kc == 0 else [], acc=[] if kc == 0 else [bpm])
                    cx.op("vector", lambda e: e.tensor_scalar(out=rstd[:], in0=pm[:, :], scalar1=1e-6, scalar2=None, op0=ALU.add),
                          r=[bpm], w=[b_rstd])
                    cx.op("scalar", lambda e: e.activation(out=rstd[:], in_=rstd[:], func=AF.Sqrt), r=[b_rstd], w=[b_rstd])
                    cx.op("vector", lambda e: e.reciprocal(out=rstd[:], in_=rstd[:]), r=[b_rstd], w=[b_rstd])
                    for kc in range(8):
                        cx.op("vector", lambda e: e.scalar_tensor_tensor(out=sq[:, kc, :], in0=xn[:, kc, :],
                                                                         scalar=fnw_sb[:, kc:kc + 1], in1=rstd[:],
                                                                         op0=ALU.mult, op1=ALU.mult),
                              r=[b_xn, b_rstd], w=[b_sq])
                    cx.dma(x_dst[:, t0:t0 + 512].rearrange("(kc p) t -> p kc t", p=128), sq[:], r=[b_sq], w=[b_x_dst])
            cx.barrier()

    x_cur, b_x_cur = xT_in, Buf("xT_in")
    layers = debug.get("layers", list(range(DEPTH)))
    phases = debug.get("phases", ["p1", "p1c", "p2a", "p2b", "p2c", "p2d", "p3"])
    b_out = Buf("outT")
    for l in layers:
        if "p1" in phases:
            phase1(l, x_cur, b_x_cur)
        if "p2a" in phases:
            phase2a(l)
        if "p2b" in phases:
            phase2b(l)
            cx.barrier()
        if "p2c" in phases:
            phase2c(l)
        if "p2d" in phases:
            phase2d(l)
        if "p3" in phases:
            last = (l == DEPTH - 1)
            if last:
                phase3(l, x_cur, b_x_cur, outT, b_out, True)
            else:
                phase3(l, x_cur, b_x_cur, xr[l], b_xr[l], False)
                x_cur, b_x_cur = xr[l], b_xr[l]

    cx.barrier()
    cx.n_total = cx.n_inst
    return nc, cx


def host_constants():
    half = 32
    inv = 10000.0 ** (-np.arange(half, dtype=np.float32) / half)
    ang = np.arange(S, dtype=np.float32)[:, None] * inv[None, :]
    cos = np.cos(ang).astype(np.float32).T
    sin = np.sin(ang).astype(np.float32).T
    cos64 = np.concatenate([cos, cos], 0)
    sin64 = np.concatenate([-sin, sin], 0)
    cs2 = np.stack([np.concatenate([cos64, cos64], 0), np.concatenate([sin64, sin64], 0)]).astype(np.float32)
    c_f32 = np.zeros((128, 512), np.float32)
    c_f32[:, 0:128] = np.eye(128, dtype=np.float32)
    c_f32[:, 128:256] = 1.0 / 1024.0
    import ml_dtypes
    bf = ml_dtypes.bfloat16
    sI = np.arange(128)[:, None]
    tI = np.arange(128)[None, :]
    c_bf = np.zeros((128, 1024), np.float32)
    c_bf[:, 0:128] = (sI <= tI)
    c_bf[:, 128:256] = (sI < tI)
    c_bf[:, 256:384] = (sI > tI)
    t2 = np.arange(256)[None, :]
    c_bf[:, 384:640] = (sI <= t2) & (t2 <= sI + 128)
    c_bf[:, 768:896] = -(sI >= tI).astype(np.float32)
    c_bf[:, 896:1024] = -1.0
    ind16 = (np.arange(S)[None, :] // 256 == np.arange(16)[:, None]).astype(np.float32)
    ind64 = (np.arange(S)[None, :] // 64 == np.arange(64)[:, None]).astype(np.float32)
    n_all = np.arange(256)
    vis = ((np.arange(S)[None, :] >= 16 * n_all[:, None] + 31) & (n_all[:, None] < 255)).astype(np.float32)
    vis = vis.reshape(2, 128, S)
    starts = n_all * 16
    jj = np.arange(64)
    ov = ((starts[:, None] < (jj[None, :] + 1) * 64) & (starts[:, None] + 32 > jj[None, :] * 64)).astype(np.float32)
    ov[255] = 0.0
    ovb = np.zeros((128, 256), np.float32)
    for nt in range(2):
        ovb[:, nt * 128:nt * 128 + 64] = ov[nt * 128:(nt + 1) * 128]
        ovb[:, nt * 128 + 64:nt * 128 + 128] = 1.0
    fm = np.zeros((128, 32, 64), np.float32)
    for tt in range(32):
        cur = (tt * 128 + np.arange(128)) // 64
        f = np.zeros((128, 64), np.float32)
        f[jj[None, :] > cur[:, None]] = -1e30
        for p in range(128):
            c = cur[p]
            if c - 1 >= 0:
                f[p, c - 1] = 1e30
            f[p, c] = 2e30
            f[p, 0] = 3e30 if c != 0 else 2e30
        fm[:, tt, :] = f
    sel = np.zeros((12, 12, 128), np.float32)
    for g in range(12):
        sel[g, g, :] = 1.0
    return {"cs2": cs2, "c_f32": c_f32, "c_bf": c_bf.astype(bf), "ind16": ind16.astype(bf),
            "ind64": ind64.astype(bf), "vis_c": vis.astype(bf), "ovb_c": ovb.astype(bf),
            "fm_c": np.ascontiguousarray(fm.reshape(128, 32 * 64)), "sel_c": np.ascontiguousarray(sel.reshape(12, 12 * 128))}


def make_in_maps(x, norm_w, w_in, nsa_cmp_pos, nsa_cmp_w1, nsa_cmp_w2, w_up, w_out, final_norm_w):
    consts = host_constants()
    norm_wT = np.ascontiguousarray(norm_w.reshape(DEPTH, 8, 128).transpose(2, 0, 1).reshape(128, DEPTH * 8))
    fnorm_wT = np.ascontiguousarray(final_norm_w.reshape(8, 128).T)
    cmp_peT = np.ascontiguousarray(nsa_cmp_pos.transpose(0, 1, 3, 2).reshape(DEPTH, 128, 32))
    shared = {"norm_wT": norm_wT, "fnorm_wT": fnorm_wT, "w_in": np.ascontiguousarray(w_in),
              "cmp_peT": cmp_peT, "cmp_w1": np.ascontiguousarray(nsa_cmp_w1),
              "cmp_w2": np.ascontiguousarray(nsa_cmp_w2), "w_up": np.ascontiguousarray(w_up),
              "w_out": np.ascontiguousarray(w_out)}
    shared.update(consts)
    maps = []
    for c in range(x.shape[0]):
        m = dict(shared)
        m["xT"] = np.ascontiguousarray(x[c].T)
        maps.append(m)
    return maps


def kernel(x, norm_w, w_in, nsa_cmp_pos, nsa_cmp_w1, nsa_cmp_w2, w_up, w_out, final_norm_w):
    x = np.asarray(x, np.float32)
    in_maps = make_in_maps(x, np.asarray(norm_w, np.float32), np.asarray(w_in, np.float32),
                           np.asarray(nsa_cmp_pos, np.float32), np.asarray(nsa_cmp_w1, np.float32),
                           np.asarray(nsa_cmp_w2, np.float32), np.asarray(w_up, np.float32),
                           np.asarray(w_out, np.float32), np.asarray(final_norm_w, np.float32))
    nc, cx = build_program()
    res = run_bass_kernel_spmd(nc, in_maps, core_ids=list(range(NCORES)))
    out = np.stack([np.ascontiguousarray(r["outT"].T) for r in res.results], 0)
    return out.astype(np.float32)
```

```python
import numpy as np
from contextlib import ExitStack
import concourse.bass as bass
import concourse.mybir as mybir
from concourse.bass_utils import run_bass_kernel_spmd

F32 = mybir.dt.float32
BF16 = mybir.dt.bfloat16
AF = mybir.ActivationFunctionType
ALU = mybir.AluOpType
AX = mybir.AxisListType

S = 4096
D = 1024
NCORES = 8
DEPTH = 2
INW = 9612
NEG = -30000.0

C_QA, C_KA, C_VA, C_GA = 0, 256, 512, 768
C_QB, C_KVB, C_GB, C_NG = 1024, 1280, 1664, 1920
C_QC, C_KC, C_VC, C_GC = 1932, 2188, 2444, 2700
C_QD, C_KD, C_VD, C_GD = 2956, 3724, 4492, 5260
C_MG = 5516


class Buf:
    __slots__ = ("name", "w", "r", "psum")

    def __init__(self, name="", psum=False):
        self.name = name
        self.w = None
        self.r = {}
        self.psum = psum


class EngState:
    def __init__(self, name, eng, sem):
        self.name = name
        self.eng = eng
        self.sem = sem
        self.count = 0
        self.waited = {}


class Ctx:
    NSLOT = 24

    def __init__(self, nc):
        self.nc = nc
        self.es = ExitStack()
        self.E = {}
        for nm in ("tensor", "vector", "scalar", "gpsimd", "sync"):
            sem = self.es.enter_context(nc.semaphore("s_" + nm))
            self.E[nm] = EngState(nm, getattr(nc, nm), sem)
        self.slots = [self.es.enter_context(nc.semaphore("d_%d" % i)) for i in range(self.NSLOT)]
        self.slot_uses = [0] * self.NSLOT
        self.slot_next = 0
        self.n_inst = 0

    def _wait(self, E, ev):
        sem, val = ev
        key = id(sem)
        if E.waited.get(key, 0) >= val:
            return
        if sem is E.sem and False:
            return
        E.eng.wait_ge(sem, val)
        E.waited[key] = val

    def _deps(self, E, r, w, acc):
        for b in r:
            if b.w is not None:
                self._wait(E, b.w)
            if b.psum:
                for k, ev in b.r.items():
                    if ev[0] is not E.sem:
                        self._wait(E, ev)
        for b in w:
            if b.w is not None:
                self._wait(E, b.w)
            for k, ev in b.r.items():
                self._wait(E, ev)

    def _record(self, ev, r, w, acc):
        for b in r:
            b.r[id(ev[0])] = ev
        for b in w:
            b.w = ev
            b.r = {}
        for b in acc:
            b.w = ev

    def op(self, eng, fn, r=(), w=(), acc=()):
        E = self.E[eng]
        self._deps(E, r, w, acc)
        ins = fn(E.eng)
        E.count += 1
        ins.then_inc(E.sem, 1)
        self.n_inst += 1
        self._record((E.sem, E.count), r, w, acc)

    def dma(self, out, in_, r=(), w=(), q="sync"):
        E = self.E[q]
        self._deps(E, r, w, ())
        i = self.slot_next
        self.slot_next = (i + 1) % self.NSLOT
        sem = self.slots[i]
        if self.slot_uses[i] > 0:
            self._wait(E, (sem, 16 * self.slot_uses[i]))
        self.slot_uses[i] += 1
        E.eng.dma_start(out=out, in_=in_).then_inc(sem, 16)
        self.n_inst += 1
        self._record((sem, 16 * self.slot_uses[i]), r, w, ())

    def barrier(self):
        evs = [(E.sem, E.count) for E in self.E.values() if E.count > 0]
        evs += [(self.slots[i], 16 * self.slot_uses[i]) for i in range(self.NSLOT) if self.slot_uses[i] > 0]
        for E in self.E.values():
            for ev in evs:
                if ev[0] is E.sem:
                    continue
                self._wait(E, ev)


def build_program(debug=None):
    debug = debug or {}
    nc = bass.Bass("TRN2", target_bir_lowering=False)
    cx = Ctx(nc)
    es = cx.es
    scratch_kind = "ExternalOutput" if debug.get("dump") else "Internal"

    def din(name, shape, dt=F32):
        return nc.dram_tensor(name, list(shape), dt, kind="ExternalInput").ap()

    def dscr(name, shape, dt):
        kind = "ExternalOutput" if name in debug.get("dump_names", ()) else "Internal"
        return nc.dram_tensor(name, list(shape), dt, kind=kind).ap()

    xT_in = din("xT", [D, S])
    norm_w = din("norm_wT", [128, DEPTH * 8])
    fnorm_w = din("fnorm_wT", [128, 8])
    w_in = din("w_in", [DEPTH, D, INW])
    cmp_peT = din("cmp_peT", [DEPTH, 128, 32])
    cmp_w1 = din("cmp_w1", [DEPTH, 2, 2048, 256])
    cmp_w2 = din("cmp_w2", [DEPTH, 2, 256, 64])
    w_up = din("w_up", [DEPTH, 4, 256, D])
    w_out = din("w_out", [DEPTH, D, D])
    cs2 = din("cs2", [2, 128, S])
    c_f32 = din("c_f32", [128, 512])
    outT = nc.dram_tensor("outT", [D, S], F32, kind="ExternalOutput").ap()

    xr = [dscr("xr0", [D, S], F32), dscr("xr1", [D, S], F32)]
    hT_d = dscr("hT_d", [D, S], BF16)
    QK = dscr("QK", [3968, S], BF16)
    KVC = dscr("KVC", [128, S], F32)
    G_d = dscr("G_d", [1024, S], F32)
    NG_d = dscr("NG_d", [12, S], F32)
    R_QA, R_KA, R_QB, R_QBR, R_KSW, R_QC, R_KC, R_QD, R_KD = 0, 256, 512, 768, 1024, 1152, 1408, 1664, 2432

    VA_d = dscr("VA_d", [S, 4, 128], BF16)
    VB_d = dscr("VB_d", [S, 2, 128], BF16)
    VC_d = dscr("VC_d", [S, 4, 128], BF16)
    VD_d = [dscr("VD%d_d" % g, [S, 4, 128], BF16) for g in range(3)]
    WD_d = dscr("WD_d", [1024, S], BF16)
    b_VA, b_VB, b_VC, b_VD, b_WD = Buf("VA"), Buf("VB"), Buf("VC"), Buf("VD"), Buf("WD")
    c_bf = din("c_bf", [128, 1024], BF16)
    ind16 = din("ind16", [16, S], BF16)
    OB_d = [dscr("OB%d_d" % i, [256, S], F32) for i in range(3)]
    b_OB = [Buf(), Buf(), Buf()]
    ind64 = din("ind64", [64, S], BF16)
    vis_c = din("vis_c", [2, 128, S], BF16)
    ovb_c = din("ovb_c", [128, 256], BF16)
    fm_c = din("fm_c", [128, 32 * 64], F32)
    sel_c = din("sel_c", [12, 12 * 128], F32)
    b_xr = [Buf("xr0"), Buf("xr1")]
    b_hT_d, b_QK, b_KVC, b_G, b_NG = Buf("hT_d"), Buf("QK"), Buf("KVC"), Buf("G"), Buf("NG")

    uniq = [0]

    def sb(name, shape, dt, stack):
        uniq[0] += 1
        return stack.enter_context(nc.sbuf_tensor("%s_%d" % (name, uniq[0]), list(shape), dt))

    PS = [es.enter_context(nc.psum_tensor("ps%d" % i, [128, 512], F32)) for i in range(8)]
    bPS = [Buf("ps%d" % i, psum=True) for i in range(8)]

    cF = sb("cF", [128, 512], F32, es)
    ident = cF[:, 0:128]
    avg = cF[:, 128:256]
    nw_sb = sb("nw_sb", [128, DEPTH * 8], F32, es)
    fnw_sb = sb("fnw_sb", [128, 8], F32, es)
    cB = sb("cB", [128, 1024], BF16, es)
    tri_le = cB[:, 0:128]
    tri_lt = cB[:, 128:256]
    tri_gt = cB[:, 256:384]
    band = cB[:, 384:640]
    zeros_bf = cB[:, 640:768]
    UTneg = cB[:, 768:896]
    onesneg = cB[:, 896:1024]
    b_const = Buf("const")
    cx.dma(cB[:], c_bf[:, :], w=[b_const])
    cx.dma(cF[:], c_f32[:, :], w=[b_const])
    cx.dma(nw_sb[:], norm_w[:, :], w=[b_const])
    cx.dma(fnw_sb[:], fnorm_w[:, :], w=[b_const])
    cx.barrier()

    def phase1(l, x_src, b_x_src):
        with ExitStack() as ph:
            hT = sb("hT", [128, 8, S], BF16, ph)
            b_hT = [Buf("hT%d" % i) for i in range(8)]
            cs = sb("cs", [128, 2, S], F32, ph)
            b_cs = Buf("cs")
            cx.dma(cs[:, 0, :], cs2[0, :, :], w=[b_cs])
            cx.dma(cs[:, 1, :], cs2[1, :, :], w=[b_cs])
            with ExitStack() as pa:
                xt = [sb("xt%d" % i, [128, 8, 512], F32, pa) for i in range(2)]
                b_xt = [Buf(), Buf()]
                sq = sb("sq", [128, 8, 512], F32, pa)
                b_sq = Buf()
                rstd = sb("rstd", [128, 512], F32, pa)
                b_rstd = Buf()
                for tb in range(8):
                    t0 = tb * 512
                    X, bX = xt[tb % 2], b_xt[tb % 2]
                    cx.dma(X[:], x_src[:, t0:t0 + 512].rearrange("(kc p) t -> p kc t", p=128),
                           r=[b_x_src], w=[bX])
                    cx.op("scalar", lambda e: e.activation(out=sq[:], in_=X[:], func=AF.Square),
                          r=[bX], w=[b_sq])
                    pm, bpm = PS[tb % 2], bPS[tb % 2]
                    for kc in range(8):
                        if kc == 0:
                            cx.op("tensor", lambda e: e.matmul(pm[:, :], lhsT=avg, rhs=sq[:, kc, :],
                                                               start=True, stop=False),
                                  r=[b_sq], w=[bpm])
                        else:
                            cx.op("tensor", lambda e: e.matmul(pm[:, :], lhsT=avg, rhs=sq[:, kc, :],
                                                               start=False, stop=(kc == 7)),
                                  r=[b_sq], acc=[bpm])
                    cx.op("vector", lambda e: e.tensor_scalar(out=rstd[:], in0=pm[:, :], scalar1=1e-6,
                                                              scalar2=None, op0=ALU.add),
                          r=[bpm], w=[b_rstd])
                    cx.op("scalar", lambda e: e.activation(out=rstd[:], in_=rstd[:], func=AF.Sqrt),
                          r=[b_rstd], w=[b_rstd])
                    cx.op("vector", lambda e: e.reciprocal(out=rstd[:], in_=rstd[:]),
                          r=[b_rstd], w=[b_rstd])
                    for kc in range(8):
                        cx.op("vector", lambda e: e.scalar_tensor_tensor(
                            out=hT[:, kc, t0:t0 + 512], in0=X[:, kc, :],
                            scalar=nw_sb[:, l * 8 + kc:l * 8 + kc + 1], in1=rstd[:],
                            op0=ALU.mult, op1=ALU.mult),
                            r=[bX, b_rstd], w=[] if kc else [b_hT[tb]],
                            acc=[b_hT[tb]] if kc else [])
                    cx.dma(hT_d[:, t0:t0 + 512].rearrange("(kc p) t -> p kc t", p=128),
                           hT[:, :, t0:t0 + 512], r=[b_hT[tb]], w=[b_hT_d])
                cx.barrier()
            specs = []
            for i in range(2):
                specs.append(([(C_QA + 128 * i, 128)], "rope", QK[R_QA + 128 * i:R_QA + 128 * (i + 1), :], b_QK))
            for i in range(2):
                specs.append(([(C_KA + 128 * i, 128)], "rope", QK[R_KA + 128 * i:R_KA + 128 * (i + 1), :], b_QK))
            for i in range(2):
                specs.append(([(C_GA + 128 * i, 128)], "silu", G_d[128 * i:128 * (i + 1), :], b_G))
            for i in range(2):
                specs.append(([(C_QB + 128 * i, 128)], "both", (QK[R_QB + 128 * i:R_QB + 128 * (i + 1), :],
                                                                  QK[R_QBR + 128 * i:R_QBR + 128 * (i + 1), :]), b_QK))
            specs.append(([(C_KVB, 128)], "f32", KVC[:, :], b_KVC))
            specs.append(([(C_KVB + 128, 64), (C_KVB + 256, 64)], "rope", QK[R_KSW:R_KSW + 128, :], b_QK))
            for i in range(2):
                specs.append(([(C_GB + 128 * i, 128)], "silu", G_d[256 + 128 * i:256 + 128 * (i + 1), :], b_G))
            specs.append(([(C_NG, 12)], "sigmoid", NG_d[:, :], b_NG))
            for i in range(2):
                specs.append(([(C_QC + 128 * i, 128)], "plain", QK[R_QC + 128 * i:R_QC + 128 * (i + 1), :], b_QK))
            for i in range(2):
                specs.append(([(C_KC + 128 * i, 128)], "plain", QK[R_KC + 128 * i:R_KC + 128 * (i + 1), :], b_QK))
            for i in range(2):
                specs.append(([(C_GC + 128 * i, 128)], "silu", G_d[512 + 128 * i:512 + 128 * (i + 1), :], b_G))
            for i in range(6):
                specs.append(([(C_QD + 128 * i, 128)], "rope", QK[R_QD + 128 * i:R_QD + 128 * (i + 1), :], b_QK))
            for i in range(6):
                specs.append(([(C_KD + 128 * i, 128)], "rope", QK[R_KD + 128 * i:R_KD + 128 * (i + 1), :], b_QK))
            for i in range(2):
                specs.append(([(C_GD + 128 * i, 128)], "silu", G_d[768 + 128 * i:768 + 128 * (i + 1), :], b_G))
            if debug.get("p1_specs") is not None:
                specs = [specs[i] for i in debug["p1_specs"]]

            with ExitStack() as pb:
                NW = 2
                wst = [sb("wst%d" % i, [128, 8, 128], F32, pb) for i in range(NW)]
                wsw = [sb("wsw%d" % i, [128, 8, 128], F32, pb) for i in range(NW)]
                wbf = [sb("wbf%d" % i, [128, 8, 128], BF16, pb) for i in range(NW)]
                wbs = [sb("wbs%d" % i, [128, 8, 128], BF16, pb) for i in range(NW)]
                b_wst = [Buf() for _ in range(NW)]
                b_wsw = [Buf() for _ in range(NW)]
                b_wbf = [Buf() for _ in range(NW)]
                b_wbs = [Buf() for _ in range(NW)]
                NO = 3
                ob = [sb("ob%d" % i, [128, 512], BF16, pb) for i in range(NO)]
                of = [sb("of%d" % i, [128, 512], F32, pb) for i in range(NO)]
                t1 = [sb("t1_%d" % i, [128, 512], F32, pb) for i in range(2)]
                t2 = [sb("t2_%d" % i, [128, 512], F32, pb) for i in range(2)]
                b_ob = [Buf() for _ in range(NO)]
                b_of = [Buf() for _ in range(NO)]
                b_t1 = [Buf(), Buf()]
                b_t2 = [Buf(), Buf()]
                oi = 0
                ti = 0
                pi = 0
                wl = w_in[l]

                def load_w(si):
                    cols, mode, dst, bdst = specs[si]
                    k = si % NW
                    need_sw = mode in ("rope", "both")
                    c0 = 0
                    for (col, n) in cols:
                        cx.dma(wst[k][:, :, c0:c0 + n],
                               wl[:, col:col + n].rearrange("(kc p) c -> p kc c", p=128), w=[b_wst[k]])
                        if need_sw:
                            for hh in range(n // 64):
                                b0 = col + hh * 64
                                d0 = c0 + hh * 64
                                cx.dma(wsw[k][:, :, d0:d0 + 32],
                                       wl[:, b0 + 32:b0 + 64].rearrange("(kc p) c -> p kc c", p=128),
                                       w=[b_wsw[k]])
                                cx.dma(wsw[k][:, :, d0 + 32:d0 + 64],
                                       wl[:, b0:b0 + 32].rearrange("(kc p) c -> p kc c", p=128),
                                       w=[b_wsw[k]])
                        c0 += n
                    ncol = c0
                    cx.op("gpsimd", lambda e: e.tensor_copy(out=wbf[k][:, :, 0:ncol], in_=wst[k][:, :, 0:ncol]),
                          r=[b_wst[k]], w=[b_wbf[k]])
                    if need_sw:
                        cx.op("gpsimd", lambda e: e.tensor_copy(out=wbs[k][:, :, 0:ncol], in_=wsw[k][:, :, 0:ncol]),
                              r=[b_wsw[k]], w=[b_wbs[k]])
                    return ncol

                ncols = {}
                ncols[0] = load_w(0)
                for si in range(len(specs)):
                    if si + 1 < len(specs):
                        ncols[si + 1] = load_w(si + 1)
                    cols, mode, dst, bdst = specs[si]
                    k = si % NW
                    M = ncols[si]
                    need_sw = mode in ("rope", "both")
                    for tb in range(8):
                        t0 = tb * 512
                        pa_, bpa = PS[pi % 8], bPS[pi % 8]
                        pi += 1
                        for kc in range(8):
                            cx.op("tensor", lambda e: e.matmul(pa_[0:M, :], lhsT=wbf[k][:, kc, 0:M],
                                                               rhs=hT[:, kc, t0:t0 + 512],
                                                               start=(kc == 0), stop=(kc == 7)),
                                  r=[b_wbf[k], b_hT[tb]] if kc == 0 else [],
                                  w=[bpa] if kc == 0 else [], acc=[] if kc == 0 else [bpa])
                        if need_sw:
                            ps_, bps = PS[pi % 8], bPS[pi % 8]
                            pi += 1
                            for kc in range(8):
                                cx.op("tensor", lambda e: e.matmul(ps_[0:M, :], lhsT=wbs[k][:, kc, 0:M],
                                                                   rhs=hT[:, kc, t0:t0 + 512],
                                                                   start=(kc == 0), stop=(kc == 7)),
                                      r=[b_wbs[k], b_hT[tb]] if kc == 0 else [],
                                      w=[bps] if kc == 0 else [], acc=[] if kc == 0 else [bps])
                        if mode in ("plain", "both"):
                            o, bo = ob[oi % NO], b_ob[oi % NO]
                            oi += 1
                            d = dst[0] if mode == "both" else dst
                            cx.op("scalar", lambda e: e.activation(out=o[0:M, :], in_=pa_[0:M, :], func=AF.Copy),
                                  r=[bpa], w=[bo])
                            cx.dma(d[:, t0:t0 + 512], o[0:M, :], r=[bo], w=[bdst])
                        if mode in ("rope", "both"):
                            o, bo = ob[oi % NO], b_ob[oi % NO]
                            oi += 1
                            a1, ba1 = t1[ti % 2], b_t1[ti % 2]
                            a2, ba2 = t2[ti % 2], b_t2[ti % 2]
                            ti += 1
                            d = dst[1] if mode == "both" else dst
                            cx.op("vector", lambda e: e.tensor_tensor(out=a1[0:M, :], in0=pa_[0:M, :],
                                                                      in1=cs[0:M, 0, t0:t0 + 512], op=ALU.mult),
                                  r=[bpa, b_cs], w=[ba1])
                            cx.op("vector", lambda e: e.tensor_tensor(out=a2[0:M, :], in0=ps_[0:M, :],
                                                                      in1=cs[0:M, 1, t0:t0 + 512], op=ALU.mult),
                                  r=[bps, b_cs], w=[ba2])
                            cx.op("gpsimd", lambda e: e.tensor_tensor(out=o[0:M, :], in0=a1[0:M, :],
                                                                      in1=a2[0:M, :], op=ALU.add),
                                  r=[ba1, ba2], w=[bo])
                            cx.dma(d[:, t0:t0 + 512], o[0:M, :], r=[bo], w=[bdst])
                        if mode in ("f32", "silu", "sigmoid"):
                            o, bo = of[oi % NO], b_of[oi % NO]
                            oi += 1
                            fn = {"f32": AF.Copy, "silu": AF.Silu, "sigmoid": AF.Sigmoid}[mode]
                            cx.op("scalar", lambda e: e.activation(out=o[0:M, :], in_=pa_[0:M, :], func=fn),
                                  r=[bpa], w=[bo])
                            cx.dma(dst[:, t0:t0 + 512], o[0:M, :], r=[bo], w=[bdst])
                cx.barrier()
            if "p1c" in phases:
                phase1c(l, hT, b_hT)
        cx.barrier()

    def make_pt(ph, tag, n=3):
        return ([sb("pt%s%d" % (tag, i), [128, 512], BF16, ph) for i in range(n)], [Buf() for _ in range(n)], [0])

    def attn_items(ptp, Qb, bQ, Kb, bK, Kc, Vb, bV, items, epilogue):
        pt, b_pt, itc = ptp
        base = itc[0]

        def bufs(n):
            k = (base + n) % 3
            return PS[k], bPS[k], pt[k], b_pt[k]

        def s1(n, it):
            pS, bpS, P, bP = bufs(n)
            q0, kt, c0, c1 = it["qb"] * 512, it["kt"], it["c0"], it["c1"]
            cx.op("tensor", lambda e: e.matmul(pS[:, c0:c1], lhsT=Kb[0:Kc, kt * 128:(kt + 1) * 128],
                                               rhs=Qb[0:Kc, q0 + c0:q0 + c1], start=True, stop=True),
                  r=list(bQ) + list(bK), w=[bpS])

        def s2(n, it):
            pS, bpS, P, bP = bufs(n)
            c0, c1, m0 = it["c0"], it["c1"], it["m0"]
            cx.op("scalar", lambda e: e.activation(out=P[:, c0:c1], in_=pS[:, c0:c1], func=AF.Exp, scale=0.125),
                  r=[bpS], w=[bP])
            if m0 is not None:
                cx.op("gpsimd", lambda e: e.tensor_tensor(out=P[:, m0:m0 + 128], in0=P[:, m0:m0 + 128],
                                                          in1=it["mk"], op=ALU.mult), r=[bP], w=[bP])

        def s3(n, it):
            pS, bpS, P, bP = bufs(n)
            qb, kt, c0, c1 = it["qb"], it["kt"], it["c0"], it["c1"]
            po, bpo = PS[3 + qb % 2], bPS[3 + qb % 2]
            cx.op("tensor", lambda e: e.matmul(po[:, c0:c1], lhsT=Vb[:, kt, :], rhs=P[:, c0:c1],
                                               start=it["first"], stop=it["last"]),
                  r=[bP] + list(bV), w=[bpo] if it["first"] else [], acc=[] if it["first"] else [bpo])
            if it["last"]:
                epilogue(qb, po, bpo)

        N = len(items)
        for n in range(N + 2):
            if n < N:
                s1(n, items[n])
            if 0 <= n - 1 < N:
                s2(n - 1, items[n - 1])
            if 0 <= n - 2 < N:
                s3(n - 2, items[n - 2])
        itc[0] = base + N

    def causal_items():
        items = []
        for qb in range(8):
            nkt = 4 * qb + 4
            for kt in range(nkt):
                n0 = max(0, kt * 128 - qb * 512)
                diag = kt * 128 >= qb * 512
                items.append(dict(qb=qb, kt=kt, c0=n0, c1=512, m0=(n0 if diag else None), mk=tri_le,
                                  first=(kt == 0), last=(kt == nkt - 1)))
        return items

    def window_items():
        items = []
        for qb in range(8):
            tl = [(kt, 128 * (kt - 4 * qb), 512, 128 * (kt - 4 * qb), tri_le) for kt in range(4 * qb, 4 * qb + 4)]
            if qb > 0:
                tl += [(kt, 0, 128 * (kt - 4 * qb + 5), 128 * (kt - 4 * qb + 4), tri_gt) for kt in range(4 * qb - 4, 4 * qb)]
            for k_, (kt, c0, c1, m0, mk) in enumerate(tl):
                items.append(dict(qb=qb, kt=kt, c0=c0, c1=c1, m0=m0, mk=mk, first=(k_ == 0), last=(k_ == len(tl) - 1)))
        return items

    def causal_attn(ptp, Qb, bQ, Kb, bK, Kc, Vb, bV, epilogue):
        attn_items(ptp, Qb, bQ, Kb, bK, Kc, Vb, bV, causal_items(), epilogue)

    def make_norm_epilogue(ph, row0, tag):
        rz = [sb("rz%s%d" % (tag, i), [128, 512], F32, ph) for i in range(2)]
        on = [sb("on%s%d" % (tag, i), [64, 512], F32, ph) for i in range(2)]
        gt = [sb("gt%s%d" % (tag, i), [64, 512], F32, ph) for i in range(2)]
        ow = [sb("ow%s%d" % (tag, i), [64, 512], BF16, ph) for i in range(2)]
        b_rz, b_on, b_gt, b_ow = [Buf(), Buf()], [Buf(), Buf()], [Buf(), Buf()], [Buf(), Buf()]
        cnt = [0]

        def ep(qb, po, bpo, r0=None):
            i = cnt[0] % 2
            cnt[0] += 1
            rr = row0[0]
            q0 = qb * 512
            cx.dma(gt[i][:, :], G_d[rr:rr + 64, q0:q0 + 512], r=[b_G], w=[b_gt[i]])
            cx.op("vector", lambda e: e.reciprocal(out=rz[i][64:128, :], in_=po[64:128, :]),
                  r=[bpo], w=[b_rz[i]])
            cx.op("vector", lambda e: e.tensor_tensor(out=on[i][:, :], in0=po[0:64, :], in1=rz[i][64:128, :],
                                                      op=ALU.mult),
                  r=[bpo, b_rz[i]], w=[b_on[i]])
            cx.op("gpsimd", lambda e: e.tensor_tensor(out=ow[i][:, :], in0=on[i][:, :], in1=gt[i][:, :],
                                                      op=ALU.mult),
                  r=[b_on[i], b_gt[i]], w=[b_ow[i]])
            cx.dma(WD_d[rr:rr + 64, q0:q0 + 512], ow[i][:, :], r=[b_ow[i]], w=[b_WD])
        return ep

    def phase1c(l, hT, b_hT):
        with ExitStack() as pc:
            wv = sb("wv", [128, 8, 1408], BF16, pc)
            b_wv = Buf()
            wvs = [sb("wvs%d" % i, [128, 8, 128], F32, pc) for i in range(2)]
            b_wvs = [Buf(), Buf()]
            wl = w_in[l]
            pieces = [(C_VA, 256, 0), (C_KVB + 192, 64, 256), (C_KVB + 320, 64, 320), (C_VC, 256, 384),
                      (C_VD, 768, 640)]
            k = 0
            for (col, n, dc) in pieces:
                for j in range(0, n, 128):
                    m = min(128, n - j)
                    cx.dma(wvs[k % 2][:, :, 0:m], wl[:, col + j:col + j + m].rearrange("(kc p) c -> p kc c", p=128),
                           w=[b_wvs[k % 2]])
                    cx.op("gpsimd", lambda e: e.tensor_copy(out=wv[:, :, dc + j:dc + j + m], in_=wvs[k % 2][:, :, 0:m]),
                          r=[b_wvs[k % 2]], w=[], acc=[b_wv])
                    k += 1
            dests = [("A", VA_d, b_VA, 4), ("B", VB_d, b_VB, 2), ("C", VC_d, b_VC, 4),
                     ("D0", VD_d[0], b_VD, 4), ("D1", VD_d[1], b_VD, 4), ("D2", VD_d[2], b_VD, 4)]
            st = {}
            for (nm, _, _, nh) in dests:
                st[nm] = ([sb("sv%s%d" % (nm, i), [128, nh, 128], BF16, pc) for i in range(2)], [Buf(), Buf()])
                for i in range(2):
                    cx.op("gpsimd", lambda e: e.memset(st[nm][0][i][:], 1.0), w=[st[nm][1][i]])
            pi = 0
            for tt in range(32):
                tok_nat = slice(tt * 128, (tt + 1) * 128)
                c4, m4 = tt // 8, (tt % 8) * 128
                c16, m16 = tt // 2, (tt % 2) * 128
                tok4 = slice(c4 + 4 * m4, c4 + 4 * m4 + 4 * 127 + 1, 4)
                tok16 = slice(c16 + 16 * m16, c16 + 16 * m16 + 16 * 127 + 1, 16)
                groups = [(0, 384, tok_nat, [("A", 0, 4), ("B", 256, 2)]),
                          (384, 512, tok_nat, [("C", 0, 4), ("D0", 256, 4)]),
                          (896, 256, tok4, [("D1", 0, 4)]),
                          (1152, 256, tok16, [("D2", 0, 4)])]
                for (gc, N, tok, outs) in groups:
                    pp, bpp = PS[pi % 8], bPS[pi % 8]
                    pi += 1
                    for kc in range(8):
                        cx.op("tensor", lambda e: e.matmul(pp[:, 0:N], lhsT=hT[:, kc, tok], rhs=wv[:, kc, gc:gc + N],
                                                           start=(kc == 0), stop=(kc == 7)),
                              r=[b_wv] + b_hT if kc == 0 else [], w=[bpp] if kc == 0 else [],
                              acc=[] if kc == 0 else [bpp])
                    for (nm, c0, nh) in outs:
                        tiles, bufs = st[nm]
                        T, bT = tiles[tt % 2], bufs[tt % 2]
                        dst, bdst = [(d[1], d[2]) for d in dests if d[0] == nm][0]
                        cx.op("scalar", lambda e: e.activation(
                            out=T[:, :, 0:64], in_=pp[:, c0:c0 + nh * 64].rearrange("p (h d) -> p h d", d=64),
                            func=AF.Copy), r=[bpp], w=[bT])
                        cx.dma(dst[tt * 128:(tt + 1) * 128, :, :], T[:], r=[bT], w=[bdst])
            cx.barrier()

    def phase2a(l):
        with ExitStack() as ph:
            Qa = [sb("Qa%d" % i, [128, S], BF16, ph) for i in range(2)]
            Ka = [sb("Ka%d" % i, [128, S], BF16, ph) for i in range(2)]
            Va = [sb("Va%d" % i, [128, 32, 128], BF16, ph) for i in range(2)]
            bQa, bQb, bKa, bVa = [Buf(), Buf()], [Buf(), Buf()], [Buf(), Buf()], [Buf(), Buf()]
            km = sb("km", [64, 16], F32, ph)
            kml = sb("kml", [64, 32], BF16, ph)
            gs = sb("gs", [128, 16], F32, ph)
            m8 = sb("m8", [128, 8], F32, ph)
            BT = sb("BT", [128, 80], F32, ph)
            b_km, b_kml, b_gs, b_m8, b_BT = Buf(), Buf(), Buf(), Buf(), Buf()
            for i in range(2):
                cx.dma(Ka[i][64:80, :], ind16[:, :], w=[bKa[i]])
            row0 = [0]
            ep = make_norm_epilogue(ph, row0, "a")
            ptp = make_pt(ph, "a")
            lvl = debug.get("p2a_level", 3)
            for h in range(debug.get("p2a_heads", 4)):
                i = h % 2
                Q, K, V = Qa[i], Ka[i], Va[i]
                cx.dma(Q[0:64, :], QK[R_QA + 64 * h:R_QA + 64 * (h + 1), :], r=[b_QK], w=[bQa[i]])
                cx.dma(K[0:64, :], QK[R_KA + 64 * h:R_KA + 64 * (h + 1), :], r=[b_QK], w=[bKa[i]])
                cx.dma(V[:, :, :], VA_d[:, h, :].rearrange("(kt p) c -> p kt c", p=128), r=[b_VA], w=[bVa[i]])
                cx.op("vector", lambda e: e.tensor_reduce(out=km[:, :], in_=K[0:64, :].rearrange("p (n s) -> p n s", s=256),
                                                          axis=AX.X, op=ALU.add), r=[bKa[i]], w=[b_km])
                cx.op("scalar", lambda e: e.activation(out=kml[:, 0:16], in_=km[:, :], func=AF.Copy, scale=1.0 / 256.0),
                      r=[b_km], w=[b_kml])
                cx.op("vector", lambda e: e.scalar_tensor_tensor(out=kml[:, 16:32], in0=km[:, :], scalar=1.0 / 256.0,
                                                                 in1=kml[:, 0:16], op0=ALU.mult, op1=ALU.subtract),
                      r=[b_km, b_kml], w=[b_kml])
                cx.op("vector", lambda e: e.memset(gs[:, :], -1e30), w=[b_gs])
                cx.op("vector", lambda e: e.memset(BT[:, 0:64], 0.0), w=[b_BT])
                cx.op("vector", lambda e: e.memset(BT[:, 64:80], NEG), w=[b_BT])
                for tt in range(32 if lvl >= 1 else 0):
                    b = tt // 2
                    tk = slice(tt * 128, (tt + 1) * 128)
                    if tt % 2 == 0:
                        if b <= 3:
                            cx.op("vector", lambda e: e.memset(BT[:, 64:65 + b], 0.0), w=[b_BT])
                        else:
                            cx.op("vector", lambda e: e.memset(BT[:, 64 + b:65 + b], 0.0), w=[b_BT])
                    if b >= 4:
                        pg, bpg = PS[5], bPS[5]
                        cx.op("tensor", lambda e: e.matmul(pg[:, 0:16], lhsT=Q[0:64, tk], rhs=kml[:, 0:16],
                                                           start=True, stop=False), r=[bQa[i], b_kml], w=[bpg])
                        cx.op("tensor", lambda e: e.matmul(pg[:, 0:16], lhsT=Q[0:64, tk], rhs=kml[:, 16:32],
                                                           start=False, stop=True), acc=[bpg])
                        cx.op("vector", lambda e: e.tensor_copy(out=gs[:, 0:b], in_=pg[:, 0:b]), r=[bpg], w=[b_gs])
                        cx.op("vector", lambda e: e.max(out=m8[:, :], in_=gs[:, :]), r=[b_gs], w=[b_m8])
                        cx.op("vector", lambda e: e.tensor_scalar(out=BT[:, 64:64 + b], in0=gs[:, 0:b],
                                                                  scalar1=m8[:, 2:3], scalar2=NEG,
                                                                  op0=ALU.is_lt, op1=ALU.mult),
                              r=[b_gs, b_m8], w=[b_BT])
                    ptr, bptr = PS[6 + tt % 2], bPS[6 + tt % 2]
                    cx.op("tensor", lambda e: e.transpose(ptr[0:80, 0:128], BT[:, 0:80], ident), r=[b_BT], w=[bptr])
                    cx.op("scalar", lambda e: e.activation(out=Q[64:80, tk], in_=ptr[64:80, 0:128], func=AF.Copy),
                          r=[bptr], w=[bQb[i]])
                row0[0] = 0 + 64 * h
                if lvl >= 2:
                    causal_attn(ptp, Q, [bQa[i], bQb[i]], K, [bKa[i]], 80, V, [bVa[i]], ep if lvl >= 3 else (lambda *a: None))
            cx.barrier()

    def phase2c(l):
        for pair in range(2):
            with ExitStack() as ph:
                Qc = [sb("Qc%d" % i, [64, S], BF16, ph) for i in range(2)]
                Kc_ = [sb("Kc%d" % i, [64, S], BF16, ph) for i in range(2)]
                Vc = [sb("Vc%d" % i, [128, 32, 128], BF16, ph) for i in range(2)]
                bQ, bK, bV = [Buf(), Buf()], [Buf(), Buf()], [Buf(), Buf()]

                def tiles(nm, dt=F32):
                    return ([[sb("%s%d%d" % (nm, a, b), [128, 512], dt, ph) for b in range(2)] for a in range(2)],
                            [[Buf(), Buf()] for a in range(2)])
                e_sb, b_e = tiles("ce")
                sp_sb, b_sp = tiles("csp")
                hi_sb, b_hi = tiles("chi", BF16)
                lo_sb, b_lo = tiles("clo", BF16)
                t_sb, b_t = tiles("ct")
                X_sb, b_X = tiles("cX")
                a_sb, b_a = tiles("ca", BF16)
                carry = [sb("carry%d" % a, [128, 512], F32, ph) for a in range(2)]
                b_carry = [Buf(), Buf()]
                gt = [sb("cgt%d" % a, [64, 512], F32, ph) for a in range(2)]
                ow = [sb("cow%d" % a, [64, 512], BF16, ph) for a in range(2)]
                b_gt, b_ow = [Buf(), Buf()], [Buf(), Buf()]
                for a in range(2):
                    h = pair * 2 + a
                    cx.dma(Qc[a][:, :], QK[R_QC + 64 * h:R_QC + 64 * (h + 1), :], r=[b_QK], w=[bQ[a]])
                    cx.dma(Kc_[a][:, :], QK[R_KC + 64 * h:R_KC + 64 * (h + 1), :], r=[b_QK], w=[bK[a]])
                    cx.dma(Vc[a][:, :, :], VC_d[:, h, :].rearrange("(kt p) c -> p kt c", p=128), r=[b_VC], w=[bV[a]])
                per_head = []
                for a in range(2):
                    lst = []
                    cnt = 0
                    for qb in range(8):
                        kts = list(range(4 * qb + 3, -1, -1))
                        for kt in kts:
                            lst.append(dict(a=a, h=pair * 2 + a, qb=qb, kt=kt, q0=qb * 512, n0=max(0, kt * 128 - qb * 512),
                                            diag=(kt * 128 >= qb * 512), j=cnt % 2, first=(kt == kts[0]), last=(kt == 0)))
                            cnt += 1
                    per_head.append(lst)
                items = [x for pr in zip(per_head[0], per_head[1]) for x in pr]

                def banks(it):
                    a, j = it["a"], it["j"]
                    base = a * 4
                    return (PS[base + j], bPS[base + j]), (PS[base + 2], bPS[base + 2]), (PS[base + 3], bPS[base + 3])

                def s1(it):
                    a, j, kt, q0, n0 = it["a"], it["j"], it["kt"], it["q0"], it["n0"]
                    (pS, bpS), _, _ = banks(it)
                    cx.op("tensor", lambda e: e.matmul(pS[:, n0:512], lhsT=Kc_[a][:, kt * 128:(kt + 1) * 128],
                                                       rhs=Qc[a][:, q0 + n0:q0 + 512], start=True, stop=True),
                          r=[bQ[a], bK[a]], w=[bpS])

                def s2(it):
                    a, j, n0 = it["a"], it["j"], it["n0"]
                    (pS, bpS), _, _ = banks(it)
                    cs_ = slice(n0, 512)
                    E_, SP, HI, LO = e_sb[a][j], sp_sb[a][j], hi_sb[a][j], lo_sb[a][j]
                    cx.op("scalar", lambda e: e.activation(out=E_[:, cs_], in_=pS[:, cs_], func=AF.Exp, scale=0.125),
                          r=[bpS], w=[b_e[a][j]])
                    cx.op("scalar", lambda e: e.activation(out=SP[:, cs_], in_=E_[:, cs_], func=AF.Ln, bias=1.0, scale=1.0),
                          r=[b_e[a][j]], w=[b_sp[a][j]])
                    if it["diag"]:
                        cx.op("vector", lambda e: e.tensor_tensor(out=SP[:, n0:n0 + 128], in0=SP[:, n0:n0 + 128],
                                                                  in1=tri_lt, op=ALU.mult),
                              r=[b_sp[a][j]], w=[b_sp[a][j]])
                    cx.op("scalar", lambda e: e.activation(out=HI[:, cs_], in_=SP[:, cs_], func=AF.Copy),
                          r=[b_sp[a][j]], w=[b_hi[a][j]])
                    cx.op("vector", lambda e: e.tensor_tensor(out=LO[:, cs_], in0=SP[:, cs_], in1=HI[:, cs_], op=ALU.subtract),
                          r=[b_sp[a][j], b_hi[a][j]], w=[b_lo[a][j]])

                def s3(it):
                    a, j, n0 = it["a"], it["j"], it["n0"]
                    (pC, bpC), (pR, bpR), _ = banks(it)
                    cs_ = slice(n0, 512)
                    HI, LO = hi_sb[a][j], lo_sb[a][j]
                    cx.op("tensor", lambda e: e.matmul(pC[:, cs_], lhsT=UTneg, rhs=HI[:, cs_], start=True, stop=False),
                          r=[b_hi[a][j]], w=[bpC])
                    cx.op("tensor", lambda e: e.matmul(pC[:, cs_], lhsT=UTneg, rhs=LO[:, cs_], start=False, stop=True),
                          r=[b_lo[a][j]], acc=[bpC])
                    cx.op("tensor", lambda e: e.matmul(pR[:, cs_], lhsT=onesneg, rhs=HI[:, cs_], start=True, stop=False),
                          w=[bpR])
                    cx.op("tensor", lambda e: e.matmul(pR[:, cs_], lhsT=onesneg, rhs=LO[:, cs_], start=False, stop=True),
                          acc=[bpR])

                def s4(it):
                    a, j, n0, q0, qb = it["a"], it["j"], it["n0"], it["q0"], it["qb"]
                    (pC, bpC), (pR, bpR), _ = banks(it)
                    cs_ = slice(n0, 512)
                    E_, T_, X_, A_ = e_sb[a][j], t_sb[a][j], X_sb[a][j], a_sb[a][j]
                    if it["first"]:
                        cx.op("vector", lambda e: e.memset(carry[a][:, :], 0.0), w=[b_carry[a]])
                    cx.op("vector", lambda e: e.tensor_tensor(out=T_[:, cs_], in0=pC[:, cs_], in1=carry[a][:, cs_], op=ALU.add),
                          r=[bpC, b_carry[a]], w=[b_t[a][j]])
                    cx.op("vector", lambda e: e.tensor_tensor(out=carry[a][:, cs_], in0=pR[:, cs_], in1=carry[a][:, cs_], op=ALU.add),
                          r=[bpR], w=[b_carry[a]])
                    cx.op("scalar", lambda e: e.activation(out=X_[:, cs_], in_=T_[:, cs_], func=AF.Exp),
                          r=[b_t[a][j]], w=[b_X[a][j]])
                    cx.op("gpsimd", lambda e: e.tensor_tensor(out=A_[:, cs_], in0=E_[:, cs_], in1=X_[:, cs_], op=ALU.mult),
                          r=[b_e[a][j], b_X[a][j]], w=[b_a[a][j]])
                    if it["diag"]:
                        cx.op("gpsimd", lambda e: e.tensor_tensor(out=A_[:, n0:n0 + 128], in0=A_[:, n0:n0 + 128],
                                                                  in1=tri_lt, op=ALU.mult),
                              r=[b_a[a][j]], w=[b_a[a][j]])

                def s5(it):
                    a, j, n0, q0, kt, h = it["a"], it["j"], it["n0"], it["q0"], it["kt"], it["h"]
                    _, _, (po, bpo) = banks(it)
                    cs_ = slice(n0, 512)
                    rr = 512 + 64 * h
                    if it["first"]:
                        cx.op("tensor", lambda e: e.matmul(po[:, :], lhsT=zeros_bf, rhs=cB[:, 0:512], start=True, stop=False),
                              w=[bpo])
                        cx.dma(gt[a][:, :], G_d[rr:rr + 64, q0:q0 + 512], r=[b_G], w=[b_gt[a]])
                    cx.op("tensor", lambda e: e.matmul(po[:, cs_], lhsT=Vc[a][:, kt, :], rhs=a_sb[a][j][:, cs_],
                                                       start=False, stop=it["last"]),
                          r=[b_a[a][j], bV[a]], acc=[bpo])
                    if it["last"]:
                        cx.op("vector", lambda e: e.tensor_tensor(out=ow[a][:, :], in0=po[0:64, :], in1=gt[a][:, :], op=ALU.mult),
                              r=[bpo, b_gt[a]], w=[b_ow[a]])
                        cx.dma(WD_d[rr:rr + 64, q0:q0 + 512], ow[a][:, :], r=[b_ow[a]], w=[b_WD])

                stages = [s1, s2, s3, s4, s5]
                N = len(items)
                for n in range(N + 4):
                    for k_, st in enumerate(stages):
                        m = n - k_
                        if 0 <= m < N:
                            st(items[m])
                cx.barrier()

    def phase2d(l):
        with ExitStack() as ph:
            Qd = [sb("Qd%d" % i, [64, S], BF16, ph) for i in range(2)]
            Kd = [sb("Kd%d" % i, [64, S], BF16, ph) for i in range(2)]
            Vd = [sb("Vd%d" % i, [128, 32, 128], BF16, ph) for i in range(2)]
            bQ, bK, bV = [Buf(), Buf()], [Buf(), Buf()], [Buf(), Buf()]
            accs = [sb("dacc%d" % i, [128, S], F32, ph) for i in range(2)]
            b_acc = [Buf(), Buf()]
            rzl = sb("drzl", [64, S], F32, ph)
            b_rzl = Buf()
            gtd = sb("dgt", [64, S], F32, ph)
            b_gtd = Buf()
            owd = sb("dow", [64, S], BF16, ph)
            b_owd = Buf()
            NP_ = 3
            Pt = [sb("dP%d" % i, [128, 256], BF16, ph) for i in range(NP_)]
            b_P = [Buf() for _ in range(NP_)]
            it = 0
            bi = 0
            pi = 0
            for h in range(debug.get("p2d_heads", 4)):
                acc, bacc = accs[h % 2], b_acc[h % 2]
                for g in range(3):
                    dil = (1, 4, 16)[g]
                    hg = g * 4 + h
                    i = bi % 2
                    bi += 1
                    Q, K, V = Qd[i], Kd[i], Vd[i]
                    cx.dma(Q[:, :], QK[R_QD + 64 * hg:R_QD + 64 * (hg + 1), :], r=[b_QK], w=[bQ[i]])
                    cx.dma(K[:, :], QK[R_KD + 64 * hg:R_KD + 64 * (hg + 1), :], r=[b_QK], w=[bK[i]])
                    cx.dma(V[:, :, :], VD_d[g][:, h, :].rearrange("(kt p) c -> p kt c", p=128), r=[b_VD], w=[bV[i]])
                    nt = (S // dil) // 128
                    for c in range(dil):
                        prev = None
                        for k in range(nt):
                            N = 256 if k + 1 < nt else 128
                            base = c + dil * 128 * k
                            ktok = slice(base, base + dil * 127 + 1, dil)
                            qtok = slice(base, base + dil * (N - 1) + 1, dil)
                            pS, bpS = PS[it % 3], bPS[it % 3]
                            P, bP = Pt[it % NP_], b_P[it % NP_]
                            it += 1
                            cx.op("tensor", lambda e: e.matmul(pS[:, 0:N], lhsT=K[:, ktok], rhs=Q[:, qtok], start=True, stop=True),
                                  r=[bQ[i], bK[i]], w=[bpS])
                            cx.op("scalar", lambda e: e.activation(out=P[:, 0:N], in_=pS[:, 0:N], func=AF.Exp, scale=0.125),
                                  r=[bpS], w=[bP])
                            cx.op("gpsimd", lambda e: e.tensor_tensor(out=P[:, 0:N], in0=P[:, 0:N], in1=band[:, 0:N], op=ALU.mult),
                                  r=[bP], w=[bP])
                            po, bpo = PS[3 + pi % 2], bPS[3 + pi % 2]
                            pi += 1
                            ti = c * nt + k
                            if prev is not None:
                                Pp, bPp = prev
                                cx.op("tensor", lambda e: e.matmul(po[:, 0:128], lhsT=V[:, ti - 1, :], rhs=Pp[:, 128:256],
                                                                   start=True, stop=False), r=[bPp, bV[i]], w=[bpo])
                                cx.op("tensor", lambda e: e.matmul(po[:, 0:128], lhsT=V[:, ti, :], rhs=P[:, 0:128],
                                                                   start=False, stop=True), r=[bP], acc=[bpo])
                            else:
                                cx.op("tensor", lambda e: e.matmul(po[:, 0:128], lhsT=V[:, ti, :], rhs=P[:, 0:128],
                                                                   start=True, stop=True), r=[bP, bV[i]], w=[bpo])
                            prev = (P, bP)
                            av = acc[:, ktok]
                            if g == 0:
                                cx.op("scalar", lambda e: e.activation(out=av, in_=po[:, 0:128], func=AF.Copy),
                                      r=[bpo], w=[], acc=[bacc])
                            else:
                                cx.op("vector", lambda e: e.tensor_tensor(out=av, in0=po[:, 0:128], in1=av, op=ALU.add),
                                      r=[bpo, bacc] if (c == 0 and k == 0) else [bpo], w=[], acc=[bacc])
                rr = 768 + 64 * h
                cx.dma(gtd[:, :], G_d[rr:rr + 64, :], r=[b_G], w=[b_gtd])
                cx.op("vector", lambda e: e.reciprocal(out=acc[64:128, :], in_=acc[64:128, :]), r=[bacc], w=[bacc])
                cx.dma(rzl[:, :], acc[64:128, :], r=[bacc], w=[b_rzl])
                cx.op("vector", lambda e: e.tensor_tensor(out=acc[0:64, :], in0=acc[0:64, :], in1=rzl[:, :], op=ALU.mult),
                      r=[b_rzl], w=[bacc])
                cx.op("gpsimd", lambda e: e.tensor_tensor(out=owd[:, :], in0=acc[0:64, :], in1=gtd[:, :], op=ALU.mult),
                      r=[bacc, b_gtd], w=[b_owd])
                cx.dma(WD_d[rr:rr + 64, :], owd[:, :], r=[b_owd], w=[b_WD])
            cx.barrier()

    def phase2b(l):
        with ExitStack() as ph:
            KCm = sb("KCm", [64, 256], BF16, ph)
            VCa = [sb("VCa%d" % i, [128, 128], BF16, ph) for i in range(2)]
            OVb = sb("OVb", [128, 256], BF16, ph)
            Sel = sb("Sel", [12, 12 * 128], F32, ph)
            NGs = sb("NGs", [12, S], F32, ph)
            b_KCm, b_VCa, b_cst, b_NGs = Buf(), Buf(), Buf(), Buf()
            cx.dma(OVb[:], ovb_c[:, :], w=[b_cst])
            cx.dma(Sel[:], sel_c[:, :], w=[b_cst])
            cx.dma(NGs[:], NG_d[:, :], r=[b_NG], w=[b_NGs])
            with ExitStack() as p1_:
                KV = sb("KVs", [128, S], F32, p1_)
                peT = sb("peT", [128, 32], F32, p1_)
                W1s = sb("W1s", [128, 32, 256], F32, p1_)
                W1 = sb("W1", [128, 32, 256], BF16, p1_)
                W2s = sb("W2s", [128, 4, 64], F32, p1_)
                W2 = sb("W2", [128, 4, 64], BF16, p1_)
                X = sb("Xc", [128, 32, 256], BF16, p1_)
                hid = [sb("hid%d" % i, [128, 256], BF16, p1_) for i in range(4)]
                x2 = sb("gx2", [128, 256], F32, p1_)
                u_ = sb("gu", [128, 256], F32, p1_)
                b_KV, b_pe, b_W1s, b_W1, b_W2s, b_W2, b_X, b_x2, b_u = [Buf() for _ in range(9)]
                b_hid = [Buf() for _ in range(4)]
                cx.dma(KV[:], KVC[:, :], r=[b_KVC], w=[b_KV])
                cx.dma(peT[:], cmp_peT[l], w=[b_pe])
                for kv in range(2):
                    cx.dma(W1s[kv * 64:(kv + 1) * 64, :, :], cmp_w1[l, kv].rearrange("(l d) j -> d l j", d=64), w=[b_W1s])
                    cx.dma(W2s[:, kv * 2:kv * 2 + 2, :], cmp_w2[l, kv].rearrange("(jc p) d -> p jc d", p=128), w=[b_W2s])
                cx.op("gpsimd", lambda e: e.tensor_copy(out=W1[:], in_=W1s[:]), r=[b_W1s], w=[b_W1])
                cx.op("gpsimd", lambda e: e.tensor_copy(out=W2[:], in_=W2s[:]), r=[b_W2s], w=[b_W2])
                cx.op("vector", lambda e: e.memset(X[:], 0.0), w=[b_X])
                for ll in range(32):
                    cx.op("vector", lambda e: e.tensor_scalar(out=X[:, ll, 0:255], in0=KV[:, ll:ll + 16 * 254 + 1:16],
                                                              scalar1=peT[:, ll:ll + 1], scalar2=None, op0=ALU.add),
                          r=[b_KV, b_pe], w=[], acc=[b_X])
                for i in range(4):
                    cx.op("gpsimd", lambda e: e.memset(hid[i][:], 0.0), w=[b_hid[i]])
                for i in range(2):
                    cx.op("gpsimd", lambda e: e.memset(VCa[i][:], 1.0), w=[b_VCa])
                cx.op("gpsimd", lambda e: e.memset(KCm[:], 0.0), w=[b_KCm])
                for kv in range(2):
                    for jc in range(2):
                        pp, bpp = PS[kv * 2 + jc], bPS[kv * 2 + jc]
                        for ll in range(32):
                            cx.op("tensor", lambda e: e.matmul(pp[:, 0:255], lhsT=W1[kv * 64:(kv + 1) * 64, ll, jc * 128:(jc + 1) * 128],
                                                               rhs=X[kv * 64:(kv + 1) * 64, ll, 0:255],
                                                               start=(ll == 0), stop=(ll == 31)),
                                  r=[b_W1, b_X] if ll == 0 else [], w=[bpp] if ll == 0 else [], acc=[] if ll == 0 else [bpp])
                        hh = hid[kv * 2 + jc]
                        bh = b_hid[kv * 2 + jc]
                        cx.op("scalar", lambda e: e.activation(out=x2[:, 0:255], in_=pp[:, 0:255], func=AF.Square), r=[bpp], w=[b_x2])
                        cx.op("vector", lambda e: e.tensor_scalar(out=x2[:, 0:255], in0=x2[:, 0:255], scalar1=0.044715, scalar2=1.0,
                                                                  op0=ALU.mult, op1=ALU.add), r=[b_x2], w=[b_x2])
                        cx.op("vector", lambda e: e.tensor_tensor(out=u_[:, 0:255], in0=pp[:, 0:255], in1=x2[:, 0:255], op=ALU.mult),
                              r=[bpp, b_x2], w=[b_u])
                        cx.op("scalar", lambda e: e.activation(out=u_[:, 0:255], in_=u_[:, 0:255], func=AF.Sigmoid,
                                                               scale=1.5957691216057308), r=[b_u], w=[b_u])
                        cx.op("vector", lambda e: e.tensor_tensor(out=hh[:, 0:255], in0=pp[:, 0:255], in1=u_[:, 0:255], op=ALU.mult),
                              r=[bpp, b_u], w=[bh])
                pk, bpk = PS[4], bPS[4]
                for jc in range(2):
                    cx.op("tensor", lambda e: e.matmul(pk[0:64, 0:255], lhsT=W2[:, jc, :], rhs=hid[jc][:, 0:255],
                                                       start=(jc == 0), stop=(jc == 1)),
                          r=[b_W2, b_hid[jc]], w=[bpk] if jc == 0 else [], acc=[] if jc == 0 else [bpk])
                cx.op("scalar", lambda e: e.activation(out=KCm[:, 0:255], in_=pk[0:64, 0:255], func=AF.Copy), r=[bpk], w=[b_KCm])
                for nt in range(2):
                    rows = 128 if nt == 0 else 127
                    pv, bpv = PS[5 + nt], bPS[5 + nt]
                    for jc in range(2):
                        cx.op("tensor", lambda e: e.matmul(pv[0:rows, 0:64], lhsT=hid[2 + jc][:, nt * 128:nt * 128 + rows],
                                                           rhs=W2[:, 2 + jc, :], start=(jc == 0), stop=(jc == 1)),
                              r=[b_W2, b_hid[2 + jc]], w=[bpv] if jc == 0 else [], acc=[] if jc == 0 else [bpv])
                    cx.op("scalar", lambda e: e.activation(out=VCa[nt][0:rows, 64:128], in_=pv[0:rows, 0:64], func=AF.Copy),
                          r=[bpv], w=[b_VCa])
                cx.barrier()
            if debug.get("p2b_level", 9) < 1:
                return
            impT = sb("impT", [64, S], F32, ph)
            b_imp = Buf()
            with ExitStack() as p2_:
                QB4 = sb("QB4", [64, 4, S], BF16, p2_)
                vis = sb("vis", [128, 2, S], BF16, p2_)
                b_QB4, b_vis = Buf(), Buf()
                for h in range(4):
                    cx.dma(QB4[:, h, :], QK[R_QB + 64 * h:R_QB + 64 * (h + 1), :], r=[b_QK], w=[b_QB4])
                for nt in range(2):
                    cx.dma(vis[:, nt, :], vis_c[nt], w=[b_vis])
                Pc = [sb("Pc%d" % i, [128, 512], BF16, p2_) for i in range(4)]
                b_Pc = [Buf() for _ in range(4)]
                rza = [sb("rza%d" % i, [128, 512], F32, p2_) for i in range(2)]
                ocm = [sb("ocm%d" % i, [128, 512], F32, p2_) for i in range(2)]
                imt = [sb("imt%d" % i, [64, 512], F32, p2_) for i in range(2)]
                b_rza, b_ocm, b_imt = [Buf(), Buf()], [Buf(), Buf()], [Buf(), Buf()]
                it = 0
                ei = 0
                for qb in range(8):
                    q0 = qb * 512
                    for h in range(4):
                        nts = [0] if qb < 4 else [0, 1]
                        Ps = []
                        for nt in nts:
                            pS, bpS = PS[it % 2], bPS[it % 2]
                            P, bP = Pc[it % 4], b_Pc[it % 4]
                            it += 1
                            cx.op("tensor", lambda e: e.matmul(pS[:, :], lhsT=KCm[:, nt * 128:(nt + 1) * 128], rhs=QB4[:, h, q0:q0 + 512],
                                                               start=True, stop=True), r=[b_KCm, b_QB4], w=[bpS])
                            cx.op("scalar", lambda e: e.activation(out=P[:, :], in_=pS[:, :], func=AF.Exp, scale=0.125), r=[bpS], w=[bP])
                            cx.op("gpsimd", lambda e: e.tensor_tensor(out=P[:, :], in0=P[:, :], in1=vis[:, nt, q0:q0 + 512], op=ALU.mult),
                                  r=[bP, b_vis], w=[bP])
                            Ps.append((nt, P, bP))
                        j = ei % 2
                        ei += 1
                        pa_, bpa = PS[2 + j], bPS[2 + j]
                        pb_, bpb = PS[4 + j], bPS[4 + j]
                        pg_, bpg = PS[6 + j], bPS[6 + j]
                        for k_, (nt, P, bP) in enumerate(Ps):
                            cx.op("tensor", lambda e: e.matmul(pa_[:, :], lhsT=VCa[nt][:, :], rhs=P[:, :], start=(k_ == 0), stop=(k_ == len(Ps) - 1)),
                                  r=[bP, b_VCa], w=[bpa] if k_ == 0 else [], acc=[] if k_ == 0 else [bpa])
                        for k_, (nt, P, bP) in enumerate(Ps):
                            cx.op("tensor", lambda e: e.matmul(pb_[:, :], lhsT=OVb[:, nt * 128:(nt + 1) * 128], rhs=P[:, :], start=(k_ == 0), stop=(k_ == len(Ps) - 1)),
                                  r=[bP], w=[bpb] if k_ == 0 else [], acc=[] if k_ == 0 else [bpb])
                        cx.op("tensor", lambda e: e.matmul(pg_[:, :], lhsT=Sel[:, (0 * 4 + h) * 128:(0 * 4 + h + 1) * 128], rhs=NGs[:, q0:q0 + 512],
                                                           start=True, stop=True), r=[b_NGs], w=[bpg])
                        RZ, bRZ = rza[j], b_rza[j]
                        cx.op("vector", lambda e: e.tensor_scalar(out=RZ[0:64, :], in0=pa_[0:64, :], scalar1=1e-30, scalar2=None, op0=ALU.max),
                              r=[bpa], w=[bRZ])
                        cx.op("vector", lambda e: e.reciprocal(out=RZ[0:64, :], in_=RZ[0:64, :]), r=[bRZ], w=[bRZ])
                        cx.op("vector", lambda e: e.tensor_scalar(out=RZ[64:128, :], in0=pb_[64:128, :], scalar1=1e-30, scalar2=None, op0=ALU.max),
                              r=[bpb], w=[bRZ])
                        cx.op("vector", lambda e: e.reciprocal(out=RZ[64:128, :], in_=RZ[64:128, :]), r=[bRZ], w=[bRZ])
                        OC, bOC = ocm[j], b_ocm[j]
                        cx.op("vector", lambda e: e.tensor_tensor(out=OC[64:128, :], in0=pa_[64:128, :], in1=RZ[64:128, :], op=ALU.mult),
                              r=[bpa, bRZ], w=[bOC])
                        cx.op("vector", lambda e: e.tensor_tensor(out=OC[64:128, :], in0=pg_[64:128, :], in1=OC[64:128, :], op=ALU.mult),
                              r=[bpg, bOC], w=[bOC])
                        cx.dma(OB_d[0][64 * h:64 * (h + 1), q0:q0 + 512], OC[64:128, :], r=[bOC], w=[b_OB[0]])
                        if h == 0:
                            cx.op("vector", lambda e: e.tensor_tensor(out=impT[:, q0:q0 + 512], in0=pb_[0:64, :], in1=RZ[0:64, :], op=ALU.mult),
                                  r=[bpb, bRZ], w=[b_imp])
                        else:
                            IT, bIT = imt[j], b_imt[j]
                            cx.op("vector", lambda e: e.tensor_tensor(out=IT[:, :], in0=pb_[0:64, :], in1=RZ[0:64, :], op=ALU.mult),
                                  r=[bpb, bRZ], w=[bIT])
                            cx.op("gpsimd", lambda e: e.tensor_tensor(out=impT[:, q0:q0 + 512], in0=impT[:, q0:q0 + 512], in1=IT[:, :], op=ALU.add),
                                  r=[bIT], w=[b_imp])
                cx.barrier()
            if debug.get("p2b_level", 9) < 2:
                return
            QS = [sb("QS%d" % i, [128, S], BF16, ph) for i in range(4)]
            b_QSq = [Buf() for _ in range(4)]
            b_QSb = [Buf() for _ in range(4)]
            for h in range(4):
                cx.dma(QS[h][0:64, :], QK[R_QBR + 64 * h:R_QBR + 64 * (h + 1), :], r=[b_QK], w=[b_QSq[h]])
            with ExitStack() as p3_:
                Fm = sb("Fm", [128, 32 * 64], F32, p3_)
                PENT = sb("PENT", [128, S], BF16, p3_)
                IM = sb("IM", [128, 64], F32, p3_)
                IM2 = sb("IM2", [128, 64], F32, p3_)
                m8a = sb("m8a", [128, 8], F32, p3_)
                m8b = sb("m8b", [128, 8], F32, p3_)
                PT = sb("PTs", [128, 128], F32, p3_)
                b_Fm, b_PENT, b_IM, b_IM2, b_m8a, b_m8b, b_PT = [Buf() for _ in range(7)]
                cx.dma(Fm[:], fm_c[:, :], w=[b_Fm])
                cx.op("vector", lambda e: e.memset(PT[:, 0:64], 0.0), w=[b_PT])
                for tt in range(32):
                    tk = slice(tt * 128, (tt + 1) * 128)
                    fm = Fm[:, tt * 64:(tt + 1) * 64]
                    if tt < 8:
                        cx.op("vector", lambda e: e.tensor_scalar(out=PT[:, 64:128], in0=fm, scalar1=-1e29, scalar2=NEG,
                                                                  op0=ALU.is_lt, op1=ALU.mult), r=[b_Fm], w=[b_PT])
                    else:
                        p1x, bp1x = PS[tt % 2], bPS[tt % 2]
                        cx.op("tensor", lambda e: e.transpose(p1x[0:128, 0:64], impT[0:64, tk], ident[0:64, 0:64]), r=[b_imp], w=[bp1x])
                        cx.op("vector", lambda e: e.tensor_tensor(out=IM[:, :], in0=p1x[:, 0:64], in1=fm, op=ALU.add), r=[bp1x, b_Fm], w=[b_IM])
                        cx.op("vector", lambda e: e.max(out=m8a[:, :], in_=IM[:, :]), r=[b_IM], w=[b_m8a])
                        cx.op("vector", lambda e: e.match_replace(out=IM2[:, :], in_to_replace=m8a[:, :], in_values=IM[:, :], imm_value=-3e30),
                              r=[b_IM, b_m8a], w=[b_IM2])
                        cx.op("vector", lambda e: e.max(out=m8b[:, :], in_=IM2[:, :]), r=[b_IM2], w=[b_m8b])
                        cx.op("vector", lambda e: e.tensor_scalar(out=PT[:, 64:128], in0=IM[:, :], scalar1=m8b[:, 7:8], scalar2=NEG,
                                                                  op0=ALU.is_lt, op1=ALU.mult), r=[b_IM, b_m8b], w=[b_PT])
                    p2x, bp2x = PS[2 + tt % 2], bPS[2 + tt % 2]
                    cx.op("tensor", lambda e: e.transpose(p2x[:, 0:128], PT[:, :], ident), r=[b_PT], w=[bp2x])
                    cx.op("scalar", lambda e: e.activation(out=PENT[64:128, tk], in_=p2x[64:128, 0:128], func=AF.Copy), r=[bp2x], w=[], acc=[b_PENT])
                for h in range(4):
                    cx.op("gpsimd", lambda e: e.tensor_copy(out=QS[h][64:128, :], in_=PENT[64:128, :]), r=[b_PENT], w=[b_QSb[h]])
                cx.barrier()
            if debug.get("p2b_level", 9) < 3:
                return
            with ExitStack() as p4_:
                KS = sb("KS", [128, S], BF16, p4_)
                VS = sb("VS", [128, 32, 128], BF16, p4_)
                KW = sb("KW", [64, S], BF16, p4_)
                VW = sb("VW", [128, 32, 128], BF16, p4_)
                b_KS, b_VS, b_KW, b_VW = Buf(), Buf(), Buf(), Buf()
                cx.dma(KS[0:64, :], QK[R_KSW:R_KSW + 64, :], r=[b_QK], w=[b_KS])
                cx.dma(KS[64:128, :], ind64[:, :], w=[b_KS])
                cx.dma(VS[:, :, :], VB_d[:, 0, :].rearrange("(kt p) c -> p kt c", p=128), r=[b_VB], w=[b_VS])
                cx.dma(KW[:, :], QK[R_KSW + 64:R_KSW + 128, :], r=[b_QK], w=[b_KW])
                cx.dma(VW[:, :, :], VB_d[:, 1, :].rearrange("(kt p) c -> p kt c", p=128), r=[b_VB], w=[b_VW])
                rz = [sb("brz%d" % i, [128, 512], F32, p4_) for i in range(2)]
                on = [sb("bon%d" % i, [64, 512], F32, p4_) for i in range(2)]
                b_rz, b_on = [Buf(), Buf()], [Buf(), Buf()]
                cnt = [0]

                def make_ep(branch, h):
                    def ep(qb, po, bpo):
                        i = cnt[0] % 2
                        cnt[0] += 1
                        q0 = qb * 512
                        pg_, bpg = PS[6 + i], bPS[6 + i]
                        cx.op("tensor", lambda e: e.matmul(pg_[:, :], lhsT=Sel[:, (branch * 4 + h) * 128:(branch * 4 + h + 1) * 128],
                                                           rhs=NGs[:, q0:q0 + 512], start=True, stop=True), r=[b_NGs], w=[bpg])
                        cx.op("vector", lambda e: e.reciprocal(out=rz[i][64:128, :], in_=po[64:128, :]), r=[bpo], w=[b_rz[i]])
                        cx.op("vector", lambda e: e.tensor_tensor(out=on[i][:, :], in0=po[0:64, :], in1=rz[i][64:128, :], op=ALU.mult),
                              r=[bpo, b_rz[i]], w=[b_on[i]])
                        cx.op("vector", lambda e: e.tensor_tensor(out=on[i][:, :], in0=pg_[0:64, :], in1=on[i][:, :], op=ALU.mult),
                              r=[bpg, b_on[i]], w=[b_on[i]])
                        cx.dma(OB_d[branch][64 * h:64 * (h + 1), q0:q0 + 512], on[i][:, :], r=[b_on[i]], w=[b_OB[branch]])
                    return ep
                ptp = make_pt(p4_, "b")
                if debug.get("p2b_level", 9) >= 3:
                    for h in range(debug.get("p2b_heads", 4)):
                        causal_attn(ptp, QS[h], [b_QSq[h], b_QSb[h]], KS, [b_KS], 128, VS, [b_VS], make_ep(1, h))
                if debug.get("p2b_level", 9) >= 4:
                    for h in range(debug.get("p2b_heads", 4)):
                        attn_items(ptp, QS[h], [b_QSq[h]], KW, [b_KW], 64, VW, [b_VW], window_items(), make_ep(2, h))
                cx.barrier()
            if debug.get("p2b_level", 9) < 5:
                return
            with ExitStack() as p5_:
                ta = [sb("cmb%d" % i, [64, S], F32, p5_) for i in range(4)]
                b_ta = [Buf() for _ in range(4)]
                oo = sb("cmbo", [64, S], BF16, p5_)
                b_oo = Buf()
                for h in range(4):
                    rr = 256 + 64 * h
                    for i in range(3):
                        cx.dma(ta[i][:, :], OB_d[i][64 * h:64 * (h + 1), :], r=[b_OB[i]], w=[b_ta[i]])
                    cx.dma(ta[3][:, :], G_d[rr:rr + 64, :], r=[b_G], w=[b_ta[3]])
                    cx.op("vector", lambda e: e.tensor_tensor(out=ta[0][:, :], in0=ta[0][:, :], in1=ta[1][:, :], op=ALU.add),
                          r=[b_ta[1]], w=[b_ta[0]])
                    cx.op("gpsimd", lambda e: e.tensor_tensor(out=ta[0][:, :], in0=ta[0][:, :], in1=ta[2][:, :], op=ALU.add),
                          r=[b_ta[2]], w=[b_ta[0]])
                    cx.op("vector", lambda e: e.tensor_tensor(out=oo[:, :], in0=ta[0][:, :], in1=ta[3][:, :], op=ALU.mult),
                          r=[b_ta[0], b_ta[3]], w=[b_oo])
                    cx.dma(WD_d[rr:rr + 64, :], oo[:, :], r=[b_oo], w=[b_WD])
                cx.barrier()

    def phase3(l, x_src, b_x_src, x_dst, b_x_dst, last):
        with ExitStack() as ph:
            Wm = sb("Wm", [128, 8, 4096], BF16, ph)
            Wu = sb("Wu", [128, 8, 1024], BF16, ph)
            Wo = sb("Wo", [128, 8, 1024], BF16, ph)
            b_W = Buf()
            stg = ExitStack()
            wst = [sb("w3st%d" % i, [128, 8, 512], F32, stg) for i in range(2)]
            b_wst = [Buf(), Buf()]
            k = 0
            for j in range(8):
                cx.dma(wst[k % 2][:], w_in[l][:, C_MG + 512 * j:C_MG + 512 * (j + 1)].rearrange("(kc p) c -> p kc c", p=128),
                       w=[b_wst[k % 2]])
                cx.op("gpsimd", lambda e: e.tensor_copy(out=Wm[:, :, 512 * j:512 * (j + 1)], in_=wst[k % 2][:]),
                      r=[b_wst[k % 2]], acc=[b_W])
                k += 1
            for j in range(2):
                cx.dma(wst[k % 2][:], w_up[l].rearrange("b (kc p) d -> p (b kc) d", p=128)[:, :, 512 * j:512 * (j + 1)],
                       w=[b_wst[k % 2]])
                cx.op("gpsimd", lambda e: e.tensor_copy(out=Wu[:, :, 512 * j:512 * (j + 1)], in_=wst[k % 2][:]),
                      r=[b_wst[k % 2]], acc=[b_W])
                k += 1
            for j in range(2):
                cx.dma(wst[k % 2][:], w_out[l][:, 512 * j:512 * (j + 1)].rearrange("(kc p) c -> p kc c", p=128),
                       w=[b_wst[k % 2]])
                cx.op("gpsimd", lambda e: e.tensor_copy(out=Wo[:, :, 512 * j:512 * (j + 1)], in_=wst[k % 2][:]),
                      r=[b_wst[k % 2]], acc=[b_W])
                k += 1
            cx.barrier()
            stg.close()
            hb = [sb("hb%d" % i, [128, 8, 512], BF16, ph) for i in range(2)]
            wb = [sb("wb%d" % i, [128, 8, 512], BF16, ph) for i in range(2)]
            xb1 = sb("xb", [128, 8, 512], F32, ph)
            xb = [xb1, xb1]
            b_xb1 = Buf()
            b_hb, b_wb, b_xb = [Buf(), Buf()], [Buf(), Buf()], [b_xb1, b_xb1]
            yac = sb("yac", [128, 512], F32, ph)
            b_yac = Buf()
            ybf = sb("ybf", [128, 8, 512], BF16, ph)
            b_ybf = Buf()
            sg = [sb("sg%d" % i, [128, 512], F32, ph) for i in range(2)]
            tm = [sb("tm%d" % i, [128, 512], F32, ph) for i in range(2)]
            b_sg, b_tm = [Buf(), Buf()], [Buf(), Buf()]
            xn = sb("xn", [128, 8, 512], F32, ph)
            b_xn = Buf()
            sq = sb("sq3", [128, 8, 512], F32, ph)
            b_sq = Buf()
            rstd = sb("rstd3", [128, 512], F32, ph)
            b_rstd = Buf()
            pi = 0
            ci = 0
            for tb in range(8):
                t0 = tb * 512
                i = tb % 2
                cx.dma(hb[i][:], hT_d[:, t0:t0 + 512].rearrange("(kc p) t -> p kc t", p=128), r=[b_hT_d], w=[b_hb[i]])
                cx.dma(wb[i][:], WD_d[:, t0:t0 + 512].rearrange("(kc p) t -> p kc t", p=128), r=[b_WD], w=[b_wb[i]])
                cx.dma(xb[i][:], x_src[:, t0:t0 + 512].rearrange("(kc p) t -> p kc t", p=128), r=[b_x_src], w=[b_xb[i]])
                for dc in range(8):
                    for br in range(4):
                        pm, bpm = PS[pi % 8], bPS[pi % 8]
                        pu, bpu = PS[(pi + 1) % 8], bPS[(pi + 1) % 8]
                        pi += 2
                        c0 = br * 1024 + dc * 128
                        for kc in range(8):
                            cx.op("tensor", lambda e: e.matmul(pm[:, :], lhsT=Wm[:, kc, c0:c0 + 128], rhs=hb[i][:, kc, :],
                                                               start=(kc == 0), stop=(kc == 7)),
                                  r=[b_W, b_hb[i]] if kc == 0 else [], w=[bpm] if kc == 0 else [],
                                  acc=[] if kc == 0 else [bpm])
                        for kc in range(2):
                            cx.op("tensor", lambda e: e.matmul(pu[:, :], lhsT=Wu[:, br * 2 + kc, dc * 128:(dc + 1) * 128],
                                                               rhs=wb[i][:, br * 2 + kc, :],
                                                               start=(kc == 0), stop=(kc == 1)),
                                  r=[b_W, b_wb[i]] if kc == 0 else [], w=[bpu] if kc == 0 else [],
                                  acc=[] if kc == 0 else [bpu])
                        j = ci % 2
                        ci += 1
                        cx.op("scalar", lambda e: e.activation(out=sg[j][:], in_=pm[:, :], func=AF.Sigmoid),
                              r=[bpm], w=[b_sg[j]])
                        if br == 0:
                            cx.op("vector", lambda e: e.tensor_tensor(out=yac[:], in0=pu[:, :], in1=sg[j][:], op=ALU.mult),
                                  r=[bpu, b_sg[j]], w=[b_yac])
                        else:
                            cx.op("vector", lambda e: e.tensor_tensor(out=tm[j][:], in0=pu[:, :], in1=sg[j][:], op=ALU.mult),
                                  r=[bpu, b_sg[j]], w=[b_tm[j]])
                            if br < 3:
                                cx.op("gpsimd", lambda e: e.tensor_tensor(out=yac[:], in0=yac[:], in1=tm[j][:], op=ALU.add),
                                      r=[b_tm[j]], w=[b_yac])
                            else:
                                cx.op("gpsimd", lambda e: e.tensor_tensor(out=ybf[:, dc, :], in0=yac[:], in1=tm[j][:], op=ALU.add),
                                      r=[b_tm[j], b_yac], w=[b_ybf])
                for dp in range(8):
                    po, bpo = PS[pi % 8], bPS[pi % 8]
                    pi += 1
                    for dc in range(8):
                        cx.op("tensor", lambda e: e.matmul(po[:, :], lhsT=Wo[:, dc, dp * 128:(dp + 1) * 128], rhs=ybf[:, dc, :],
                                                           start=(dc == 0), stop=(dc == 7)),
                              r=[b_W, b_ybf] if dc == 0 else [], w=[bpo] if dc == 0 else [],
                              acc=[] if dc == 0 else [bpo])
                    cx.op("vector", lambda e: e.tensor_tensor(out=xn[:, dp, :], in0=po[:, :], in1=xb[i][:, dp, :], op=ALU.add),
                          r=[bpo, b_xb[i]], w=[b_xn])
                if not last:
                    cx.dma(x_dst[:, t0:t0 + 512].rearrange("(kc p) t -> p kc t", p=128), xn[:], r=[b_xn], w=[b_x_dst])
                else:
                    cx.op("scalar", lambda e: e.activation(out=sq[:], in_=xn[:], func=AF.Square), r=[b_xn], w=[b_sq])
                    pm, bpm = PS[pi % 8], bPS[pi % 8]
                    pi += 1
                    for kc in range(8):
                        cx.op("tensor", lambda e: e.matmul(pm[:, :], lhsT=avg, rhs=sq[:, kc, :], start=(kc == 0), stop=(kc == 7)),
                              r=[b_sq] if kc == 0 else [], w=[bpm] if kc == 0 else [], acc=[] if kc == 0 else [bpm])
                    cx.op("vector", lambda e: e.tensor_scalar(out=rstd[:], in0=pm[:, :], scalar1=1e-6, scalar2=None, op0=ALU.add),
                          r=[bpm], w=[b_rstd])
                    cx.op("scalar", lambda e: e.activation(out=rstd[:], in_=rstd[:], func=AF.Sqrt), r=[b_rstd], w=[b_rstd])
                    cx.op("vector", lambda e: e.reciprocal(out=rstd[:], in_=rstd[:]), r=[b_rstd], w=[b_rstd])
                    for kc in range(8):
                        cx.op("vector", lambda e: e.scalar_tensor_tensor(out=sq[:, kc, :], in0=xn[:, kc, :],
                                                                         scalar=fnw_sb[:, kc:kc + 1], in1=rstd[:],
                                                                         op0=ALU.mult, op1=ALU.mult),
                              r=[b_xn, b_rstd], w=[b_sq])
                    cx.dma(x_dst[:, t0:t0 + 512].rearrange("(kc p) t -> p kc t", p=128), sq[:], r=[b_sq], w=[b_x_dst])
            cx.barrier()

    x_cur, b_x_cur = xT_in, Buf("xT_in")
    layers = debug.get("layers", list(range(DEPTH)))
    phases = debug.get("phases", ["p1", "p1c", "p2a", "p2b", "p2c", "p2d", "p3"])
    b_out = Buf("outT")
    for l in layers:
        if "p1" in phases:
            phase1(l, x_cur, b_x_cur)
        if "p2a" in phases:
            phase2a(l)
        if "p2b" in phases:
            phase2b(l)
            cx.barrier()
        if "p2c" in phases:
            phase2c(l)
        if "p2d" in phases:
            phase2d(l)
        if "p3" in phases:
            last = (l == DEPTH - 1)
            if last:
                phase3(l, x_cur, b_x_cur, outT, b_out, True)
            else:
                phase3(l, x_cur, b_x_cur, xr[l], b_xr[l], False)
                x_cur, b_x_cur = xr[l], b_xr[l]

    cx.barrier()
    cx.n_total = cx.n_inst
    return nc, cx


def host_constants():
    half = 32
    inv = 10000.0 ** (-np.arange(half, dtype=np.float32) / half)
    ang = np.arange(S, dtype=np.float32)[:, None] * inv[None, :]
    cos = np.cos(ang).astype(np.float32).T
    sin = np.sin(ang).astype(np.float32).T
    cos64 = np.concatenate([cos, cos], 0)
    sin64 = np.concatenate([-sin, sin], 0)
    cs2 = np.stack([np.concatenate([cos64, cos64], 0), np.concatenate([sin64, sin64], 0)]).astype(np.float32)
    c_f32 = np.zeros((128, 512), np.float32)
    c_f32[:, 0:128] = np.eye(128, dtype=np.float32)
    c_f32[:, 128:256] = 1.0 / 1024.0
    import ml_dtypes
    bf = ml_dtypes.bfloat16
    sI = np.arange(128)[:, None]
    tI = np.arange(128)[None, :]
    c_bf = np.zeros((128, 1024), np.float32)
    c_bf[:, 0:128] = (sI <= tI)
    c_bf[:, 128:256] = (sI < tI)
    c_bf[:, 256:384] = (sI > tI)
    t2 = np.arange(256)[None, :]
    c_bf[:, 384:640] = (sI <= t2) & (t2 <= sI + 128)
    c_bf[:, 768:896] = -(sI >= tI).astype(np.float32)
    c_bf[:, 896:1024] = -1.0
    ind16 = (np.arange(S)[None, :] // 256 == np.arange(16)[:, None]).astype(np.float32)
    ind64 = (np.arange(S)[None, :] // 64 == np.arange(64)[:, None]).astype(np.float32)
    n_all = np.arange(256)
    vis = ((np.arange(S)[None, :] >= 16 * n_all[:, None] + 31) & (n_all[:, None] < 255)).astype(np.float32)
    vis = vis.reshape(2, 128, S)
    starts = n_all * 16
    jj = np.arange(64)
    ov = ((starts[:, None] < (jj[None, :] + 1) * 64) & (starts[:, None] + 32 > jj[None, :] * 64)).astype(np.float32)
    ov[255] = 0.0
    ovb = np.zeros((128, 256), np.float32)
    for nt in range(2):
        ovb[:, nt * 128:nt * 128 + 64] = ov[nt * 128:(nt + 1) * 128]
        ovb[:, nt * 128 + 64:nt * 128 + 128] = 1.0
    fm = np.zeros((128, 32, 64), np.float32)
    for tt in range(32):
        cur = (tt * 128 + np.arange(128)) // 64
        f = np.zeros((128, 64), np.float32)
        f[jj[None, :] > cur[:, None]] = -1e30
        for p in range(128):
            c = cur[p]
            if c - 1 >= 0:
                f[p, c - 1] = 1e30
            f[p, c] = 2e30
            f[p, 0] = 3e30 if c != 0 else 2e30
        fm[:, tt, :] = f
    sel = np.zeros((12, 12, 128), np.float32)
    for g in range(12):
        sel[g, g, :] = 1.0
    return {"cs2": cs2, "c_f32": c_f32, "c_bf": c_bf.astype(bf), "ind16": ind16.astype(bf),
            "ind64": ind64.astype(bf), "vis_c": vis.astype(bf), "ovb_c": ovb.astype(bf),
            "fm_c": np.ascontiguousarray(fm.reshape(128, 32 * 64)), "sel_c": np.ascontiguousarray(sel.reshape(12, 12 * 128))}


def make_in_maps(x, norm_w, w_in, nsa_cmp_pos, nsa_cmp_w1, nsa_cmp_w2, w_up, w_out, final_norm_w):
    consts = host_constants()
    norm_wT = np.ascontiguousarray(norm_w.reshape(DEPTH, 8, 128).transpose(2, 0, 1).reshape(128, DEPTH * 8))
    fnorm_wT = np.ascontiguousarray(final_norm_w.reshape(8, 128).T)
    cmp_peT = np.ascontiguousarray(nsa_cmp_pos.transpose(0, 1, 3, 2).reshape(DEPTH, 128, 32))
    shared = {"norm_wT": norm_wT, "fnorm_wT": fnorm_wT, "w_in": np.ascontiguousarray(w_in),
              "cmp_peT": cmp_peT, "cmp_w1": np.ascontiguousarray(nsa_cmp_w1),
              "cmp_w2": np.ascontiguousarray(nsa_cmp_w2), "w_up": np.ascontiguousarray(w_up),
              "w_out": np.ascontiguousarray(w_out)}
    shared.update(consts)
    maps = []
    for c in range(x.shape[0]):
        m = dict(shared)
        m["xT"] = np.ascontiguousarray(x[c].T)
        maps.append(m)
    return maps


def kernel(x, norm_w, w_in, nsa_cmp_pos, nsa_cmp_w1, nsa_cmp_w2, w_up, w_out, final_norm_w):
    x = np.asarray(x, np.float32)
    in_maps = make_in_maps(x, np.asarray(norm_w, np.float32), np.asarray(w_in, np.float32),
                           np.asarray(nsa_cmp_pos, np.float32), np.asarray(nsa_cmp_w1, np.float32),
                           np.asarray(nsa_cmp_w2, np.float32), np.asarray(w_up, np.float32),
                           np.asarray(w_out, np.float32), np.asarray(final_norm_w, np.float32))
    nc, cx = build_program()
    res = run_bass_kernel_spmd(nc, in_maps, core_ids=list(range(NCORES)))
    out = np.stack([np.ascontiguousarray(r["outT"].T) for r in res.results], 0)
    return out.astype(np.float32)
```

```python
import numpy as np
from contextlib import ExitStack
import concourse.bass as bass
import concourse.mybir as mybir
from concourse.bass_utils import run_bass_kernel_spmd

F32 = mybir.dt.float32
BF16 = mybir.dt.bfloat16
AF = mybir.ActivationFunctionType
ALU = mybir.AluOpType
AX = mybir.AxisListType

S = 4096
D = 1024
NCORES = 8
DEPTH = 2
INW = 9612
NEG = -30000.0

C_QA, C_KA, C_VA, C_GA = 0, 256, 512, 768
C_QB, C_KVB, C_GB, C_NG = 1024, 1280, 1664, 1920
C_QC, C_KC, C_VC, C_GC = 1932, 2188, 2444, 2700
C_QD, C_KD, C_VD, C_GD = 2956, 3724, 4492, 5260
C_MG = 5516


class Buf:
    __slots__ = ("name", "w", "r", "psum")

    def __init__(self, name="", psum=False):
        self.name = name
        self.w = None
        self.r = {}
        self.psum = psum


class EngState:
    def __init__(self, name, eng, sem):
        self.name = name
        self.eng = eng
        self.sem = sem
        self.count = 0
        self.waited = {}


class Ctx:
    NSLOT = 24

    def __init__(self, nc):
        self.nc = nc
        self.es = ExitStack()
        self.E = {}
        for nm in ("tensor", "vector", "scalar", "gpsimd", "sync"):
            sem = self.es.enter_context(nc.semaphore("s_" + nm))
            self.E[nm] = EngState(nm, getattr(nc, nm), sem)
        self.slots = [self.es.enter_context(nc.semaphore("d_%d" % i)) for i in range(self.NSLOT)]
        self.slot_uses = [0] * self.NSLOT
        self.slot_next = 0
        self.n_inst = 0

    def _wait(self, E, ev):
        sem, val = ev
        key = id(sem)
        if E.waited.get(key, 0) >= val:
            return
        if sem is E.sem and False:
            return
        E.eng.wait_ge(sem, val)
        E.waited[key] = val

    def _deps(self, E, r, w, acc):
        for b in r:
            if b.w is not None:
                self._wait(E, b.w)
            if b.psum:
                for k, ev in b.r.items():
                    if ev[0] is not E.sem:
                        self._wait(E, ev)
        for b in w:
            if b.w is not None:
                self._wait(E, b.w)
            for k, ev in b.r.items():
                self._wait(E, ev)

    def _record(self, ev, r, w, acc):
        for b in r:
            b.r[id(ev[0])] = ev
        for b in w:
            b.w = ev
            b.r = {}
        for b in acc:
            b.w = ev

    def op(self, eng, fn, r=(), w=(), acc=()):
        E = self.E[eng]
        self._deps(E, r, w, acc)
        ins = fn(E.eng)
        E.count += 1
        ins.then_inc(E.sem, 1)
        self.n_inst += 1
        self._record((E.sem, E.count), r, w, acc)

    def dma(self, out, in_, r=(), w=(), q="sync"):
        E = self.E[q]
        self._deps(E, r, w, ())
        i = self.slot_next
        self.slot_next = (i + 1) % self.NSLOT
        sem = self.slots[i]
        if self.slot_uses[i] > 0:
            self._wait(E, (sem, 16 * self.slot_uses[i]))
        self.slot_uses[i] += 1
        E.eng.dma_start(out=out, in_=in_).then_inc(sem, 16)
        self.n_inst += 1
        self._record((sem, 16 * self.slot_uses[i]), r, w, ())

    def barrier(self):
        evs = [(E.sem, E.count) for E in self.E.values() if E.count > 0]
        evs += [(self.slots[i], 16 * self.slot_uses[i]) for i in range(self.NSLOT) if self.slot_uses[i] > 0]
        for E in self.E.values():
            for ev in evs:
                if ev[0] is E.sem:
                    continue
                self._wait(E, ev)


def build_program(debug=None):
    debug = debug or {}
    nc = bass.Bass("TRN2", target_bir_lowering=False)
    cx = Ctx(nc)
    es = cx.es
    scratch_kind = "ExternalOutput" if debug.get("dump") else "Internal"

    def din(name, shape, dt=F32):
        return nc.dram_tensor(name, list(shape), dt, kind="ExternalInput").ap()

    def dscr(name, shape, dt):
        kind = "ExternalOutput" if name in debug.get("dump_names", ()) else "Internal"
        return nc.dram_tensor(name, list(shape), dt, kind=kind).ap()

    xT_in = din("xT", [D, S])
    norm_w = din("norm_wT", [128, DEPTH * 8])
    fnorm_w = din("fnorm_wT", [128, 8])
    w_in = din("w_in", [DEPTH, D, INW])
    cmp_peT = din("cmp_peT", [DEPTH, 128, 32])
    cmp_w1 = din("cmp_w1", [DEPTH, 2, 2048, 256])
    cmp_w2 = din("cmp_w2", [DEPTH, 2, 256, 64])
    w_up = din("w_up", [DEPTH, 4, 256, D])
    w_out = din("w_out", [DEPTH, D, D])
    cs2 = din("cs2", [2, 128, S])
    c_f32 = din("c_f32", [128, 512])
    outT = nc.dram_tensor("outT", [D, S], F32, kind="ExternalOutput").ap()

    xr = [dscr("xr0", [D, S], F32), dscr("xr1", [D, S], F32)]
    hT_d = dscr("hT_d", [D, S], BF16)
    QK = dscr("QK", [3968, S], BF16)
    KVC = dscr("KVC", [128, S], F32)
    G_d = dscr("G_d", [1024, S], F32)
    NG_d = dscr("NG_d", [12, S], F32)
    R_QA, R_KA, R_QB, R_QBR, R_KSW, R_QC, R_KC, R_QD, R_KD = 0, 256, 512, 768, 1024, 1152, 1408, 1664, 2432

    VA_d = dscr("VA_d", [S, 4, 128], BF16)
    VB_d = dscr("VB_d", [S, 2, 128], BF16)
    VC_d = dscr("VC_d", [S, 4, 128], BF16)
    VD_d = [dscr("VD%d_d" % g, [S, 4, 128], BF16) for g in range(3)]
    WD_d = dscr("WD_d", [1024, S], BF16)
    b_VA, b_VB, b_VC, b_VD, b_WD = Buf("VA"), Buf("VB"), Buf("VC"), Buf("VD"), Buf("WD")
    c_bf = din("c_bf", [128, 1024], BF16)
    ind16 = din("ind16", [16, S], BF16)
    OB_d = [dscr("OB%d_d" % i, [256, S], F32) for i in range(3)]
    b_OB = [Buf(), Buf(), Buf()]
    ind64 = din("ind64", [64, S], BF16)
    vis_c = din("vis_c", [2, 128, S], BF16)
    ovb_c = din("ovb_c", [128, 256], BF16)
    fm_c = din("fm_c", [128, 32 * 64], F32)
    sel_c = din("sel_c", [12, 12 * 128], F32)
    b_xr = [Buf("xr0"), Buf("xr1")]
    b_hT_d, b_QK, b_KVC, b_G, b_NG = Buf("hT_d"), Buf("QK"), Buf("KVC"), Buf("G"), Buf("NG")

    uniq = [0]

    def sb(name, shape, dt, stack):
        uniq[0] += 1
        return stack.enter_context(nc.sbuf_tensor("%s_%d" % (name, uniq[0]), list(shape), dt))

    PS = [es.enter_context(nc.psum_tensor("ps%d" % i, [128, 512], F32)) for i in range(8)]
    bPS = [Buf("ps%d" % i, psum=True) for i in range(8)]

    cF = sb("cF", [128, 512], F32, es)
    ident = cF[:, 0:128]
    avg = cF[:, 128:256]
    nw_sb = sb("nw_sb", [128, DEPTH * 8], F32, es)
    fnw_sb = sb("fnw_sb", [128, 8], F32, es)
    cB = sb("cB", [128, 1024], BF16, es)
    tri_le = cB[:, 0:128]
    tri_lt = cB[:, 128:256]
    tri_gt = cB[:, 256:384]
    band = cB[:, 384:640]
    zeros_bf = cB[:, 640:768]
    UTneg = cB[:, 768:896]
    onesneg = cB[:, 896:1024]
    b_const = Buf("const")
    cx.dma(cB[:], c_bf[:, :], w=[b_const])
    cx.dma(cF[:], c_f32[:, :], w=[b_const])
    cx.dma(nw_sb[:], norm_w[:, :], w=[b_const])
    cx.dma(fnw_sb[:], fnorm_w[:, :], w=[b_const])
    cx.barrier()

    def phase1(l, x_src, b_x_src):
        with ExitStack() as ph:
            hT = sb("hT", [128, 8, S], BF16, ph)
            b_hT = [Buf("hT%d" % i) for i in range(8)]
            cs = sb("cs", [128, 2, S], F32, ph)
            b_cs = Buf("cs")
            cx.dma(cs[:, 0, :], cs2[0, :, :], w=[b_cs])
            cx.dma(cs[:, 1, :], cs2[1, :, :], w=[b_cs])
            with ExitStack() as pa:
                xt = [sb("xt%d" % i, [128, 8, 512], F32, pa) for i in range(2)]
                b_xt = [Buf(), Buf()]
                sq = sb("sq", [128, 8, 512], F32, pa)
                b_sq = Buf()
                rstd = sb("rstd", [128, 512], F32, pa)
                b_rstd = Buf()
                for tb in range(8):
                    t0 = tb * 512
                    X, bX = xt[tb % 2], b_xt[tb % 2]
                    cx.dma(X[:], x_src[:, t0:t0 + 512].rearrange("(kc p) t -> p kc t", p=128),
                           r=[b_x_src], w=[bX])
                    cx.op("scalar", lambda e: e.activation(out=sq[:], in_=X[:], func=AF.Square),
                          r=[bX], w=[b_sq])
                    pm, bpm = PS[tb % 2], bPS[tb % 2]
                    for kc in range(8):
                        if kc == 0:
                            cx.op("tensor", lambda e: e.matmul(pm[:, :], lhsT=avg, rhs=sq[:, kc, :],
                                                               start=True, stop=False),
                                  r=[b_sq], w=[bpm])
                        else:
                            cx.op("tensor", lambda e: e.matmul(pm[:, :], lhsT=avg, rhs=sq[:, kc, :],
                                                               start=False, stop=(kc == 7)),
                                  r=[b_sq], acc=[bpm])
                    cx.op("vector", lambda e: e.tensor_scalar(out=rstd[:], in0=pm[:, :], scalar1=1e-6,
                                                              scalar2=None, op0=ALU.add),
                          r=[bpm], w=[b_rstd])
                    cx.op("scalar", lambda e: e.activation(out=rstd[:], in_=rstd[:], func=AF.Sqrt),
                          r=[b_rstd], w=[b_rstd])
                    cx.op("vector", lambda e: e.reciprocal(out=rstd[:], in_=rstd[:]),
                          r=[b_rstd], w=[b_rstd])
                    for kc in range(8):
                        cx.op("vector", lambda e: e.scalar_tensor_tensor(
                            out=hT[:, kc, t0:t0 + 512], in0=X[:, kc, :],
                            scalar=nw_sb[:, l * 8 + kc:l * 8 + kc + 1], in1=rstd[:],
                            op0=ALU.mult, op1=ALU.mult),
                            r=[bX, b_rstd], w=[] if kc else [b_hT[tb]],
                            acc=[b_hT[tb]] if kc else [])
                    cx.dma(hT_d[:, t0:t0 + 512].rearrange("(kc p) t -> p kc t", p=128),
                           hT[:, :, t0:t0 + 512], r=[b_hT[tb]], w=[b_hT_d])
                cx.barrier()
            specs = []
            for i in range(2):
                specs.append(([(C_QA + 128 * i, 128)], "rope", QK[R_QA + 128 * i:R_QA + 128 * (i + 1), :], b_QK))
            for i in range(2):
                specs.append(([(C_KA + 128 * i, 128)], "rope", QK[R_KA + 128 * i:R_KA + 128 * (i + 1), :], b_QK))
            for i in range(2):
                specs.append(([(C_GA + 128 * i, 128)], "silu", G_d[128 * i:128 * (i + 1), :], b_G))
            for i in range(2):
                specs.append(([(C_QB + 128 * i, 128)], "both", (QK[R_QB + 128 * i:R_QB + 128 * (i + 1), :],
                                                                  QK[R_QBR + 128 * i:R_QBR + 128 * (i + 1), :]), b_QK))
            specs.append(([(C_KVB, 128)], "f32", KVC[:, :], b_KVC))
            specs.append(([(C_KVB + 128, 64), (C_KVB + 256, 64)], "rope", QK[R_KSW:R_KSW + 128, :], b_QK))
            for i in range(2):
                specs.append(([(C_GB + 128 * i, 128)], "silu", G_d[256 + 128 * i:256 + 128 * (i + 1), :], b_G))
            specs.append(([(C_NG, 12)], "sigmoid", NG_d[:, :], b_NG))
            for i in range(2):
                specs.append(([(C_QC + 128 * i, 128)], "plain", QK[R_QC + 128 * i:R_QC + 128 * (i + 1), :], b_QK))
            for i in range(2):
                specs.append(([(C_KC + 128 * i, 128)], "plain", QK[R_KC + 128 * i:R_KC + 128 * (i + 1), :], b_QK))
            for i in range(2):
                specs.append(([(C_GC + 128 * i, 128)], "silu", G_d[512 + 128 * i:512 + 128 * (i + 1), :], b_G))
            for i in range(6):
                specs.append(([(C_QD + 128 * i, 128)], "rope", QK[R_QD + 128 * i:R_QD + 128 * (i + 1), :], b_QK))
            for i in range(6):
                specs.append(([(C_KD + 128 * i, 128)], "rope", QK[R_KD + 128 * i:R_KD + 128 * (i + 1), :], b_QK))
            for i in range(2):
                specs.append(([(C_GD + 128 * i, 128)], "silu", G_d[768 + 128 * i:768 + 128 * (i + 1), :], b_G))
            if debug.get("p1_specs") is not None:
                specs = [specs[i] for i in debug["p1_specs"]]

            with ExitStack() as pb:
                NW = 2
                wst = [sb("wst%d" % i, [128, 8, 128], F32, pb) for i in range(NW)]
                wsw = [sb("wsw%d" % i, [128, 8, 128], F32, pb) for i in range(NW)]
                wbf = [sb("wbf%d" % i, [128, 8, 128], BF16, pb) for i in range(NW)]
                wbs = [sb("wbs%d" % i, [128, 8, 128], BF16, pb) for i in range(NW)]
                b_wst = [Buf() for _ in range(NW)]
                b_wsw = [Buf() for _ in range(NW)]
                b_wbf = [Buf() for _ in range(NW)]
                b_wbs = [Buf() for _ in range(NW)]
                NO = 3
                ob = [sb("ob%d" % i, [128, 512], BF16, pb) for i in range(NO)]
                of = [sb("of%d" % i, [128, 512], F32, pb) for i in range(NO)]
                t1 = [sb("t1_%d" % i, [128, 512], F32, pb) for i in range(2)]
                t2 = [sb("t2_%d" % i, [128, 512], F32, pb) for i in range(2)]
                b_ob = [Buf() for _ in range(NO)]
                b_of = [Buf() for _ in range(NO)]
                b_t1 = [Buf(), Buf()]
                b_t2 = [Buf(), Buf()]
                oi = 0
                ti = 0
                pi = 0
                wl = w_in[l]

                def load_w(si):
                    cols, mode, dst, bdst = specs[si]
                    k = si % NW
                    need_sw = mode in ("rope", "both")
                    c0 = 0
                    for (col, n) in cols:
                        cx.dma(wst[k][:, :, c0:c0 + n],
                               wl[:, col:col + n].rearrange("(kc p) c -> p kc c", p=128), w=[b_wst[k]])
                        if need_sw:
                            for hh in range(n // 64):
                                b0 = col + hh * 64
                                d0 = c0 + hh * 64
                                cx.dma(wsw[k][:, :, d0:d0 + 32],
                                       wl[:, b0 + 32:b0 + 64].rearrange("(kc p) c -> p kc c", p=128),
                                       w=[b_wsw[k]])
                                cx.dma(wsw[k][:, :, d0 + 32:d0 + 64],
                                       wl[:, b0:b0 + 32].rearrange("(kc p) c -> p kc c", p=128),
                                       w=[b_wsw[k]])
                        c0 += n
                    ncol = c0
                    cx.op("gpsimd", lambda e: e.tensor_copy(out=wbf[k][:, :, 0:ncol], in_=wst[k][:, :, 0:ncol]),
                          r=[b_wst[k]], w=[b_wbf[k]])
                    if need_sw:
                        cx.op("gpsimd", lambda e: e.tensor_copy(out=wbs[k][:, :, 0:ncol], in_=wsw[k][:, :, 0:ncol]),
                              r=[b_wsw[k]], w=[b_wbs[k]])
                    return ncol

                ncols = {}
                ncols[0] = load_w(0)
                for si in range(len(specs)):
                    if si + 1 < len(specs):
                        ncols[si + 1] = load_w(si + 1)
                    cols, mode, dst, bdst = specs[si]
                    k = si % NW
                    M = ncols[si]
                    need_sw = mode in ("rope", "both")
                    for tb in range(8):
                        t0 = tb * 512
                        pa_, bpa = PS[pi % 8], bPS[pi % 8]
                        pi += 1
                        for kc in range(8):
                            cx.op("tensor", lambda e: e.matmul(pa_[0:M, :], lhsT=wbf[k][:, kc, 0:M],
                                                               rhs=hT[:, kc, t0:t0 + 512],
                                                               start=(kc == 0), stop=(kc == 7)),
                                  r=[b_wbf[k], b_hT[tb]] if kc == 0 else [],
                                  w=[bpa] if kc == 0 else [], acc=[] if kc == 0 else [bpa])
                        if need_sw:
                            ps_, bps = PS[pi % 8], bPS[pi % 8]
                            pi += 1
                            for kc in range(8):
                                cx.op("tensor", lambda e: e.matmul(ps_[0:M, :], lhsT=wbs[k][:, kc, 0:M],
                                                                   rhs=hT[:, kc, t0:t0 + 512],
                                                                   start=(kc == 0), stop=(kc == 7)),
                                      r=[b_wbs[k], b_hT[tb]] if kc == 0 else [],
                                      w=[bps] if kc == 0 else [], acc=[] if kc == 0 else [bps])
                        if mode in ("plain", "both"):
                            o, bo = ob[oi % NO], b_ob[oi % NO]
                            oi += 1
                            d = dst[0] if mode == "both" else dst
                            cx.op("scalar", lambda e: e.activation(out=o[0:M, :], in_=pa_[0:M, :], func=AF.Copy),
                                  r=[bpa], w=[bo])
                            cx.dma(d[:, t0:t0 + 512], o[0:M, :], r=[bo], w=[bdst])
                        if mode in ("rope", "both"):
                            o, bo = ob[oi % NO], b_ob[oi % NO]
                            oi += 1
                            a1, ba1 = t1[ti % 2], b_t1[ti % 2]
                            a2, ba2 = t2[ti % 2], b_t2[ti % 2]
                            ti += 1
                            d = dst[1] if mode == "both" else dst
                            cx.op("vector", lambda e: e.tensor_tensor(out=a1[0:M, :], in0=pa_[0:M, :],
                                                                      in1=cs[0:M, 0, t0:t0 + 512], op=ALU.mult),
                                  r=[bpa, b_cs], w=[ba1])
                            cx.op("vector", lambda e: e.tensor_tensor(out=a2[0:M, :], in0=ps_[0:M, :],
                                                                      in1=cs[0:M, 1, t0:t0 + 512], op=ALU.mult),
                                  r=[bps, b_cs], w=[ba2])
                            cx.op("gpsimd", lambda e: e.tensor_tensor(out=o[0:M, :], in0=a1[0:M, :],
                                                                      in1=a2[0:M, :], op=ALU.add),
                                  r=[ba1, ba2], w=[bo])
                            cx.dma(d[:, t0:t0 + 512], o[0:M, :], r=[bo], w=[bdst])
                        if mode in ("f32", "silu", "sigmoid"):
                            o, bo = of[oi % NO], b_of[oi % NO]
                            oi += 1
                            fn = {"f32": AF.Copy, "silu": AF.Silu, "sigmoid": AF.Sigmoid}[mode]
                            cx.op("scalar", lambda e: e.activation(out=o[0:M, :], in_=pa_[0:M, :], func=fn),
                                  r=[bpa], w=[bo])
                            cx.dma(dst[:, t0:t0 + 512], o[0:M, :], r=[bo], w=[bdst])
                cx.barrier()
            if "p1c" in phases:
                phase1c(l, hT, b_hT)
        cx.barrier()

    def make_pt(ph, tag, n=3):
        return ([sb("pt%s%d" % (tag, i), [128, 512], BF16, ph) for i in range(n)], [Buf() for _ in range(n)], [0])

    def attn_items(ptp, Qb, bQ, Kb, bK, Kc, Vb, bV, items, epilogue):
        pt, b_pt, itc = ptp
        base = itc[0]

        def bufs(n):
            k = (base + n) % 3
            return PS[k], bPS[k], pt[k], b_pt[k]

        def s1(n, it):
            pS, bpS, P, bP = bufs(n)
            q0, kt, c0, c1 = it["qb"] * 512, it["kt"], it["c0"], it["c1"]
            cx.op("tensor", lambda e: e.matmul(pS[:, c0:c1], lhsT=Kb[0:Kc, kt * 128:(kt + 1) * 128],
                                               rhs=Qb[0:Kc, q0 + c0:q0 + c1], start=True, stop=True),
                  r=list(bQ) + list(bK), w=[bpS])

        def s2(n, it):
            pS, bpS, P, bP = bufs(n)
            c0, c1, m0 = it["c0"], it["c1"], it["m0"]
            cx.op("scalar", lambda e: e.activation(out=P[:, c0:c1], in_=pS[:, c0:c1], func=AF.Exp, scale=0.125),
                  r=[bpS], w=[bP])
            if m0 is not None:
                cx.op("gpsimd", lambda e: e.tensor_tensor(out=P[:, m0:m0 + 128], in0=P[:, m0:m0 + 128],
                                                          in1=it["mk"], op=ALU.mult), r=[bP], w=[bP])

        def s3(n, it):
            pS, bpS, P, bP = bufs(n)
            qb, kt, c0, c1 = it["qb"], it["kt"], it["c0"], it["c1"]
            po, bpo = PS[3 + qb % 2], bPS[3 + qb % 2]
            cx.op("tensor", lambda e: e.matmul(po[:, c0:c1], lhsT=Vb[:, kt, :], rhs=P[:, c0:c1],
                                               start=it["first"], stop=it["last"]),
                  r=[bP] + list(bV), w=[bpo] if it["first"] else [], acc=[] if it["first"] else [bpo])
            if it["last"]:
                epilogue(qb, po, bpo)

        N = len(items)
        for n in range(N + 2):
            if n < N:
                s1(n, items[n])
            if 0 <= n - 1 < N:
                s2(n - 1, items[n - 1])
            if 0 <= n - 2 < N:
                s3(n - 2, items[n - 2])
        itc[0] = base + N

    def causal_items():
        items = []
        for qb in range(8):
            nkt = 4 * qb + 4
            for kt in range(nkt):
                n0 = max(0, kt * 128 - qb * 512)
                diag = kt * 128 >= qb * 512
                items.append(dict(qb=qb, kt=kt, c0=n0, c1=512, m0=(n0 if diag else None), mk=tri_le,
                                  first=(kt == 0), last=(kt == nkt - 1)))
        return items

    def window_items():
        items = []
        for qb in range(8):
            tl = [(kt, 128 * (kt - 4 * qb), 512, 128 * (kt - 4 * qb), tri_le) for kt in range(4 * qb, 4 * qb + 4)]
            if qb > 0:
                tl += [(kt, 0, 128 * (kt - 4 * qb + 5), 128 * (kt - 4 * qb + 4), tri_gt) for kt in range(4 * qb - 4, 4 * qb)]
            for k_, (kt, c0, c1, m0, mk) in enumerate(tl):
                items.append(dict(qb=qb, kt=kt, c0=c0, c1=c1, m0=m0, mk=mk, first=(k_ == 0), last=(k_ == len(tl) - 1)))
        return items

    def causal_attn(ptp, Qb, bQ, Kb, bK, Kc, Vb, bV, epilogue):
        attn_items(ptp, Qb, bQ, Kb, bK, Kc, Vb, bV, causal_items(), epilogue)

    def make_norm_epilogue(ph, row0, tag):
        rz = [sb("rz%s%d" % (tag, i), [128, 512], F32, ph) for i in range(2)]
        on = [sb("on%s%d" % (tag, i), [64, 512], F32, ph) for i in range(2)]
        gt = [sb("gt%s%d" % (tag, i), [64, 512], F32, ph) for i in range(2)]
        ow = [sb("ow%s%d" % (tag, i), [64, 512], BF16, ph) for i in range(2)]
        b_rz, b_on, b_gt, b_ow = [Buf(), Buf()], [Buf(), Buf()], [Buf(), Buf()], [Buf(), Buf()]
        cnt = [0]

        def ep(qb, po, bpo, r0=None):
            i = cnt[0] % 2
            cnt[0] += 1
            rr = row0[0]
            q0 = qb * 512
            cx.dma(gt[i][:, :], G_d[rr:rr + 64, q0:q0 + 512], r=[b_G], w=[b_gt[i]])
            cx.op("vector", lambda e: e.reciprocal(out=rz[i][64:128, :], in_=po[64:128, :]),
                  r=[bpo], w=[b_rz[i]])
            cx.op("vector", lambda e: e.tensor_tensor(out=on[i][:, :], in0=po[0:64, :], in1=rz[i][64:128, :],
                                                      op=ALU.mult),
                  r=[bpo, b_rz[i]], w=[b_on[i]])
            cx.op("gpsimd", lambda e: e.tensor_tensor(out=ow[i][:, :], in0=on[i][:, :], in1=gt[i][:, :],
                                                      op=ALU.mult),
                  r=[b_on[i], b_gt[i]], w=[b_ow[i]])
            cx.dma(WD_d[rr:rr + 64, q0:q0 + 512], ow[i][:, :], r=[b_ow[i]], w=[b_WD])
        return ep

    def phase1c(l, hT, b_hT):
        with ExitStack() as pc:
            wv = sb("wv", [128, 8, 1408], BF16, pc)
            b_wv = Buf()
            wvs = [sb("wvs%d" % i, [128, 8, 128], F32, pc) for i in range(2)]
            b_wvs = [Buf(), Buf()]
            wl = w_in[l]
            pieces = [(C_VA, 256, 0), (C_KVB + 192, 64, 256), (C_KVB + 320, 64, 320), (C_VC, 256, 384),
                      (C_VD, 768, 640)]
            k = 0
            for (col, n, dc) in pieces:
                for j in range(0, n, 128):
                    m = min(128, n - j)
                    cx.dma(wvs[k % 2][:, :, 0:m], wl[:, col + j:col + j + m].rearrange("(kc p) c -> p kc c", p=128),
                           w=[b_wvs[k % 2]])
                    cx.op("gpsimd", lambda e: e.tensor_copy(out=wv[:, :, dc + j:dc + j + m], in_=wvs[k % 2][:, :, 0:m]),
                          r=[b_wvs[k % 2]], w=[], acc=[b_wv])
                    k += 1
            dests = [("A", VA_d, b_VA, 4), ("B", VB_d, b_VB, 2), ("C", VC_d, b_VC, 4),
                     ("D0", VD_d[0], b_VD, 4), ("D1", VD_d[1], b_VD, 4), ("D2", VD_d[2], b_VD, 4)]
            st = {}
            for (nm, _, _, nh) in dests:
                st[nm] = ([sb("sv%s%d" % (nm, i), [128, nh, 128], BF16, pc) for i in range(2)], [Buf(), Buf()])
                for i in range(2):
                    cx.op("gpsimd", lambda e: e.memset(st[nm][0][i][:], 1.0), w=[st[nm][1][i]])
            pi = 0
            for tt in range(32):
                tok_nat = slice(tt * 128, (tt + 1) * 128)
                c4, m4 = tt // 8, (tt % 8) * 128
                c16, m16 = tt // 2, (tt % 2) * 128
                tok4 = slice(c4 + 4 * m4, c4 + 4 * m4 + 4 * 127 + 1, 4)
                tok16 = slice(c16 + 16 * m16, c16 + 16 * m16 + 16 * 127 + 1, 16)
                groups = [(0, 384, tok_nat, [("A", 0, 4), ("B", 256, 2)]),
                          (384, 512, tok_nat, [("C", 0, 4), ("D0", 256, 4)]),
                          (896, 256, tok4, [("D1", 0, 4)]),
                          (1152, 256, tok16, [("D2", 0, 4)])]
                for (gc, N, tok, outs) in groups:
                    pp, bpp = PS[pi % 8], bPS[pi % 8]
                    pi += 1
                    for kc in range(8):
                        cx.op("tensor", lambda e: e.matmul(pp[:, 0:N], lhsT=hT[:, kc, tok], rhs=wv[:, kc, gc:gc + N],
                                                           start=(kc == 0), stop=(kc == 7)),
                              r=[b_wv] + b_hT if kc == 0 else [], w=[bpp] if kc == 0 else [],
                              acc=[] if kc == 0 else [bpp])
                    for (nm, c0, nh) in outs:
                        tiles, bufs = st[nm]
                        T, bT = tiles[tt % 2], bufs[tt % 2]
                        dst, bdst = [(d[1], d[2]) for d in dests if d[0] == nm][0]
                        cx.op("scalar", lambda e: e.activation(
                            out=T[:, :, 0:64], in_=pp[:, c0:c0 + nh * 64].rearrange("p (h d) -> p h d", d=64),
                            func=AF.Copy), r=[bpp], w=[bT])
                        cx.dma(dst[tt * 128:(tt + 1) * 128, :, :], T[:], r=[bT], w=[bdst])
            cx.barrier()

    def phase2a(l):
        with ExitStack() as ph:
            Qa = [sb("Qa%d" % i, [128, S], BF16, ph) for i in range(2)]
            Ka = [sb("Ka%d" % i, [128, S], BF16, ph) for i in range(2)]
            Va = [sb("Va%d" % i, [128, 32, 128], BF16, ph) for i in range(2)]
            bQa, bQb, bKa, bVa = [Buf(), Buf()], [Buf(), Buf()], [Buf(), Buf()], [Buf(), Buf()]
            km = sb("km", [64, 16], F32, ph)
            kml = sb("kml", [64, 32], BF16, ph)
            gs = sb("gs", [128, 16], F32, ph)
            m8 = sb("m8", [128, 8], F32, ph)
            BT = sb("BT", [128, 80], F32, ph)
            b_km, b_kml, b_gs, b_m8, b_BT = Buf(), Buf(), Buf(), Buf(), Buf()
            for i in range(2):
                cx.dma(Ka[i][64:80, :], ind16[:, :], w=[bKa[i]])
            row0 = [0]
            ep = make_norm_epilogue(ph, row0, "a")
            ptp = make_pt(ph, "a")
            lvl = debug.get("p2a_level", 3)
            for h in range(debug.get("p2a_heads", 4)):
                i = h % 2
                Q, K, V = Qa[i], Ka[i], Va[i]
                cx.dma(Q[0:64, :], QK[R_QA + 64 * h:R_QA + 64 * (h + 1), :], r=[b_QK], w=[bQa[i]])
                cx.dma(K[0:64, :], QK[R_KA + 64 * h:R_KA + 64 * (h + 1), :], r=[b_QK], w=[bKa[i]])
                cx.dma(V[:, :, :], VA_d[:, h, :].rearrange("(kt p) c -> p kt c", p=128), r=[b_VA], w=[bVa[i]])
                cx.op("vector", lambda e: e.tensor_reduce(out=km[:, :], in_=K[0:64, :].rearrange("p (n s) -> p n s", s=256),
                                                          axis=AX.X, op=ALU.add), r=[bKa[i]], w=[b_km])
                cx.op("scalar", lambda e: e.activation(out=kml[:, 0:16], in_=km[:, :], func=AF.Copy, scale=1.0 / 256.0),
                      r=[b_km], w=[b_kml])
                cx.op("vector", lambda e: e.scalar_tensor_tensor(out=kml[:, 16:32], in0=km[:, :], scalar=1.0 / 256.0,
                                                                 in1=kml[:, 0:16], op0=ALU.mult, op1=ALU.subtract),
                      r=[b_km, b_kml], w=[b_kml])
                cx.op("vector", lambda e: e.memset(gs[:, :], -1e30), w=[b_gs])
                cx.op("vector", lambda e: e.memset(BT[:, 0:64], 0.0), w=[b_BT])
                cx.op("vector", lambda e: e.memset(BT[:, 64:80], NEG), w=[b_BT])
                for tt in range(32 if lvl >= 1 else 0):
                    b = tt // 2
                    tk = slice(tt * 128, (tt + 1) * 128)
                    if tt % 2 == 0:
                        if b <= 3:
                            cx.op("vector", lambda e: e.memset(BT[:, 64:65 + b], 0.0), w=[b_BT])
                        else:
                            cx.op("vector", lambda e: e.memset(BT[:, 64 + b:65 + b], 0.0), w=[b_BT])
                    if b >= 4:
                        pg, bpg = PS[5], bPS[5]
                        cx.op("tensor", lambda e: e.matmul(pg[:, 0:16], lhsT=Q[0:64, tk], rhs=kml[:, 0:16],
                                                           start=True, stop=False), r=[bQa[i], b_kml], w=[bpg])
                        cx.op("tensor", lambda e: e.matmul(pg[:, 0:16], lhsT=Q[0:64, tk], rhs=kml[:, 16:32],
                                                           start=False, stop=True), acc=[bpg])
                        cx.op("vector", lambda e: e.tensor_copy(out=gs[:, 0:b], in_=pg[:, 0:b]), r=[bpg], w=[b_gs])
                        cx.op("vector", lambda e: e.max(out=m8[:, :], in_=gs[:, :]), r=[b_gs], w=[b_m8])
                        cx.op("vector", lambda e: e.tensor_scalar(out=BT[:, 64:64 + b], in0=gs[:, 0:b],
                                                                  scalar1=m8[:, 2:3], scalar2=NEG,
                                                                  op0=ALU.is_lt, op1=ALU.mult),
                              r=[b_gs, b_m8], w=[b_BT])
                    ptr, bptr = PS[6 + tt % 2], bPS[6 + tt % 2]
                    cx.op("tensor", lambda e: e.transpose(ptr[0:80, 0:128], BT[:, 0:80], ident), r=[b_BT], w=[bptr])
                    cx.op("scalar", lambda e: e.activation(out=Q[64:80, tk], in_=ptr[64:80, 0:128], func=AF.Copy),
                          r=[bptr], w=[bQb[i]])
                row0[0] = 0 + 64 * h
                if lvl >= 2:
                    causal_attn(ptp, Q, [bQa[i], bQb[i]], K, [bKa[i]], 80, V, [bVa[i]], ep if lvl >= 3 else (lambda *a: None))
            cx.barrier()

    def phase2c(l):
        for pair in range(2):
            with ExitStack() as ph:
                Qc = [sb("Qc%d" % i, [64, S], BF16, ph) for i in range(2)]
                Kc_ = [sb("Kc%d" % i, [64, S], BF16, ph) for i in range(2)]
                Vc = [sb("Vc%d" % i, [128, 32, 128], BF16, ph) for i in range(2)]
                bQ, bK, bV = [Buf(), Buf()], [Buf(), Buf()], [Buf(), Buf()]

                def tiles(nm, dt=F32):
                    return ([[sb("%s%d%d" % (nm, a, b), [128, 512], dt, ph) for b in range(2)] for a in range(2)],
                            [[Buf(), Buf()] for a in range(2)])
                e_sb, b_e = tiles("ce")
                sp_sb, b_sp = tiles("csp")
                hi_sb, b_hi = tiles("chi", BF16)
                lo_sb, b_lo = tiles("clo", BF16)
                t_sb, b_t = tiles("ct")
                X_sb, b_X = tiles("cX")
                a_sb, b_a = tiles("ca", BF16)
                carry = [sb("carry%d" % a, [128, 512], F32, ph) for a in range(2)]
                b_carry = [Buf(), Buf()]
                gt = [sb("cgt%d" % a, [64, 512], F32, ph) for a in range(2)]
                ow = [sb("cow%d" % a, [64, 512], BF16, ph) for a in range(2)]
                b_gt, b_ow = [Buf(), Buf()], [Buf(), Buf()]
                for a in range(2):
                    h = pair * 2 + a
                    cx.dma(Qc[a][:, :], QK[R_QC + 64 * h:R_QC + 64 * (h + 1), :], r=[b_QK], w=[bQ[a]])
                    cx.dma(Kc_[a][:, :], QK[R_KC + 64 * h:R_KC + 64 * (h + 1), :], r=[b_QK], w=[bK[a]])
                    cx.dma(Vc[a][:, :, :], VC_d[:, h, :].rearrange("(kt p) c -> p kt c", p=128), r=[b_VC], w=[bV[a]])
                per_head = []
                for a in range(2):
                    lst = []
                    cnt = 0
                    for qb in range(8):
                        kts = list(range(4 * qb + 3, -1, -1))
                        for kt in kts:
                            lst.append(dict(a=a, h=pair * 2 + a, qb=qb, kt=kt, q0=qb * 512, n0=max(0, kt * 128 - qb * 512),
                                            diag=(kt * 128 >= qb * 512), j=cnt % 2, first=(kt == kts[0]), last=(kt == 0)))
                            cnt += 1
                    per_head.append(lst)
                items = [x for pr in zip(per_head[0], per_head[1]) for x in pr]

                def banks(it):
                    a, j = it["a"], it["j"]
                    base = a * 4
                    return (PS[base + j], bPS[base + j]), (PS[base + 2], bPS[base + 2]), (PS[base + 3], bPS[base + 3])

                def s1(it):
                    a, j, kt, q0, n0 = it["a"], it["j"], it["kt"], it["q0"], it["n0"]
                    (pS, bpS), _, _ = banks(it)
                    cx.op("tensor", lambda e: e.matmul(pS[:, n0:512], lhsT=Kc_[a][:, kt * 128:(kt + 1) * 128],
                                                       rhs=Qc[a][:, q0 + n0:q0 + 512], start=True, stop=True),
                          r=[bQ[a], bK[a]], w=[bpS])

                def s2(it):
                    a, j, n0 = it["a"], it["j"], it["n0"]
                    (pS, bpS), _, _ = banks(it)
                    cs_ = slice(n0, 512)
                    E_, SP, HI, LO = e_sb[a][j], sp_sb[a][j], hi_sb[a][j], lo_sb[a][j]
                    cx.op("scalar", lambda e: e.activation(out=E_[:, cs_], in_=pS[:, cs_], func=AF.Exp, scale=0.125),
                          r=[bpS], w=[b_e[a][j]])
                    cx.op("scalar", lambda e: e.activation(out=SP[:, cs_], in_=E_[:, cs_], func=AF.Ln, bias=1.0, scale=1.0),
                          r=[b_e[a][j]], w=[b_sp[a][j]])
                    if it["diag"]:
                        cx.op("vector", lambda e: e.tensor_tensor(out=SP[:, n0:n0 + 128], in0=SP[:, n0:n0 + 128],
                                                                  in1=tri_lt, op=ALU.mult),
                              r=[b_sp[a][j]], w=[b_sp[a][j]])
                    cx.op("scalar", lambda e: e.activation(out=HI[:, cs_], in_=SP[:, cs_], func=AF.Copy),
                          r=[b_sp[a][j]], w=[b_hi[a][j]])
                    cx.op("vector", lambda e: e.tensor_tensor(out=LO[:, cs_], in0=SP[:, cs_], in1=HI[:, cs_], op=ALU.subtract),
                          r=[b_sp[a][j], b_hi[a][j]], w=[b_lo[a][j]])

                def s3(it):
                    a, j, n0 = it["a"], it["j"], it["n0"]
                    (pC, bpC), (pR, bpR), _ = banks(it)
                    cs_ = slice(n0, 512)
                    HI, LO = hi_sb[a][j], lo_sb[a][j]
                    cx.op("tensor", lambda e: e.matmul(pC[:, cs_], lhsT=UTneg, rhs=HI[:, cs_], start=True, stop=False),
                          r=[b_hi[a][j]], w=[bpC])
                    cx.op("tensor", lambda e: e.matmul(pC[:, cs_], lhsT=UTneg, rhs=LO[:, cs_], start=False, stop=True),
                          r=[b_lo[a][j]], acc=[bpC])
                    cx.op("tensor", lambda e: e.matmul(pR[:, cs_], lhsT=onesneg, rhs=HI[:, cs_], start=True, stop=False),
                          w=[bpR])
                    cx.op("tensor", lambda e: e.matmul(pR[:, cs_], lhsT=onesneg, rhs=LO[:, cs_], start=False, stop=True),
                          acc=[bpR])

                def s4(it):
                    a, j, n0, q0, qb = it["a"], it["j"], it["n0"], it["q0"], it["qb"]
                    (pC, bpC), (pR, bpR), _ = banks(it)
                    cs_ = slice(n0, 512)
                    E_, T_, X_, A_ = e_sb[a][j], t_sb[a][j], X_sb[a][j], a_sb[a][j]
                    if it["first"]:
                        cx.op("vector", lambda e: e.memset(carry[a][:, :], 0.0), w=[b_carry[a]])
                    cx.op("vector", lambda e: e.tensor_tensor(out=T_[:, cs_], in0=pC[:, cs_], in1=carry[a][:, cs_], op=ALU.add),
                          r=[bpC, b_carry[a]], w=[b_t[a][j]])
                    cx.op("vector", lambda e: e.tensor_tensor(out=carry[a][:, cs_], in0=pR[:, cs_], in1=carry[a][:, cs_], op=ALU.add),
                          r=[bpR], w=[b_carry[a]])
                    cx.op("scalar", lambda e: e.activation(out=X_[:, cs_], in_=T_[:, cs_], func=AF.Exp),
                          r=[b_t[a][j]], w=[b_X[a][j]])
                    cx.op("gpsimd", lambda e: e.tensor_tensor(out=A_[:, cs_], in0=E_[:, cs_], in1=X_[:, cs_], op=ALU.mult),
                          r=[b_e[a][j], b_X[a][j]], w=[b_a[a][j]])
                    if it["diag"]:
                        cx.op("gpsimd", lambda e: e.tensor_tensor(out=A_[:, n0:n0 + 128], in0=A_[:, n0:n0 + 128],
                                                                  in1=tri_lt, op=ALU.mult),
                              r=[b_a[a][j]], w=[b_a[a][j]])

                def s5(it):
                    a, j, n0, q0, kt, h = it["a"], it["j"], it["n0"], it["q0"], it["kt"], it["h"]
                    _, _, (po, bpo) = banks(it)
                    cs_ = slice(n0, 512)
                    rr = 512 + 64 * h
                    if it["first"]:
                        cx.op("tensor", lambda e: e.matmul(po[:, :], lhsT=zeros_bf, rhs=cB[:, 0:512], start=True, stop=False),
                              w=[bpo])
                        cx.dma(gt[a][:, :], G_d[rr:rr + 64, q0:q0 + 512], r=[b_G], w=[b_gt[a]])
                    cx.op("tensor", lambda e: e.matmul(po[:, cs_], lhsT=Vc[a][:, kt, :], rhs=a_sb[a][j][:, cs_],
                                                       start=False, stop=it["last"]),
                          r=[b_a[a][j], bV[a]], acc=[bpo])
                    if it["last"]:
                        cx.op("vector", lambda e: e.tensor_tensor(out=ow[a][:, :], in0=po[0:64, :], in1=gt[a][:, :], op=ALU.mult),
                              r=[bpo, b_gt[a]], w=[b_ow[a]])
                        cx.dma(WD_d[rr:rr + 64, q0:q0 + 512], ow[a][:, :], r=[b_ow[a]], w=[b_WD])

                stages = [s1, s2, s3, s4, s5]
                N = len(items)
                for n in range(N + 4):
                    for k_, st in enumerate(stages):
                        m = n - k_
                        if 0 <= m < N:
                            st(items[m])
                cx.barrier()

    def phase2d(l):
        with ExitStack() as ph:
            Qd = [sb("Qd%d" % i, [64, S], BF16, ph) for i in range(2)]
            Kd = [sb("Kd%d" % i, [64, S], BF16, ph) for i in range(2)]
            Vd = [sb("Vd%d" % i, [128, 32, 128], BF16, ph) for i in range(2)]
            bQ, bK, bV = [Buf(), Buf()], [Buf(), Buf()], [Buf(), Buf()]
            accs = [sb("dacc%d" % i, [128, S], F32, ph) for i in range(2)]
            b_acc = [Buf(), Buf()]
            rzl = sb("drzl", [64, S], F32, ph)
            b_rzl = Buf()
            gtd = sb("dgt", [64, S], F32, ph)
            b_gtd = Buf()
            owd = sb("dow", [64, S], BF16, ph)
            b_owd = Buf()
            NP_ = 5
            Pt = [sb("dP%d" % i, [128, 256], BF16, ph) for i in range(NP_)]
            b_P = [Buf() for _ in range(NP_)]
            it = 0
            bi = 0
            pi = 0
            for h in range(debug.get("p2d_heads", 4)):
                acc, bacc = accs[h % 2], b_acc[h % 2]
                for g in range(3):
                    dil = (1, 4, 16)[g]
                    hg = g * 4 + h
                    i = bi % 2
                    bi += 1
                    Q, K, V = Qd[i], Kd[i], Vd[i]
                    cx.dma(Q[:, :], QK[R_QD + 64 * hg:R_QD + 64 * (hg + 1), :], r=[b_QK], w=[bQ[i]])
                    cx.dma(K[:, :], QK[R_KD + 64 * hg:R_KD + 64 * (hg + 1), :], r=[b_QK], w=[bK[i]])
                    cx.dma(V[:, :, :], VD_d[g][:, h, :].rearrange("(kt p) c -> p kt c", p=128), r=[b_VD], w=[bV[i]])
                    nt = (S // dil) // 128
                    ditems = []
                    for c in range(dil):
                        for k in range(nt):
                            N = 256 if k + 1 < nt else 128
                            base = c + dil * 128 * k
                            ditems.append(dict(c=c, k=k, N=N, ti=c * nt + k,
                                               ktok=slice(base, base + dil * 127 + 1, dil),
                                               qtok=slice(base, base + dil * (N - 1) + 1, dil)))
                    ND = len(ditems)
                    it0 = it

                    def dbuf(n):
                        return PS[(it0 + n) % 3], bPS[(it0 + n) % 3], Pt[(it0 + n) % NP_], b_P[(it0 + n) % NP_]

                    for n in range(ND + 2):
                        if n < ND:
                            d_ = ditems[n]
                            pS, bpS, P, bP = dbuf(n)
                            cx.op("tensor", lambda e: e.matmul(pS[:, 0:d_["N"]], lhsT=K[:, d_["ktok"]], rhs=Q[:, d_["qtok"]],
                                                               start=True, stop=True), r=[bQ[i], bK[i]], w=[bpS])
                        if 0 <= n - 1 < ND:
                            d_ = ditems[n - 1]
                            pS, bpS, P, bP = dbuf(n - 1)
                            N = d_["N"]
                            cx.op("scalar", lambda e: e.activation(out=P[:, 0:N], in_=pS[:, 0:N], func=AF.Exp, scale=0.125),
                                  r=[bpS], w=[bP])
                            cx.op("gpsimd", lambda e: e.tensor_tensor(out=P[:, 0:N], in0=P[:, 0:N], in1=band[:, 0:N], op=ALU.mult),
                                  r=[bP], w=[bP])
                        if 0 <= n - 2 < ND:
                            m = n - 2
                            d_ = ditems[m]
                            pS, bpS, P, bP = dbuf(m)
                            po, bpo = PS[3 + pi % 2], bPS[3 + pi % 2]
                            pi += 1
                            ti = d_["ti"]
                            if d_["k"] > 0:
                                _, _, Pp, bPp = dbuf(m - 1)
                                cx.op("tensor", lambda e: e.matmul(po[:, 0:128], lhsT=V[:, ti - 1, :], rhs=Pp[:, 128:256],
                                                                   start=True, stop=False), r=[bPp, bV[i]], w=[bpo])
                                cx.op("tensor", lambda e: e.matmul(po[:, 0:128], lhsT=V[:, ti, :], rhs=P[:, 0:128],
                                                                   start=False, stop=True), r=[bP], acc=[bpo])
                            else:
                                cx.op("tensor", lambda e: e.matmul(po[:, 0:128], lhsT=V[:, ti, :], rhs=P[:, 0:128],
                                                                   start=True, stop=True), r=[bP, bV[i]], w=[bpo])
                            av = acc[:, d_["ktok"]]
                            if g == 0:
                                cx.op("scalar", lambda e: e.activation(out=av, in_=po[:, 0:128], func=AF.Copy),
                                      r=[bpo], w=[], acc=[bacc])
                            else:
                                cx.op("vector", lambda e: e.tensor_tensor(out=av, in0=po[:, 0:128], in1=av, op=ALU.add),
                                      r=[bpo, bacc] if m == 0 else [bpo], w=[], acc=[bacc])
                    it += ND
                rr = 768 + 64 * h
                cx.dma(gtd[:, :], G_d[rr:rr + 64, :], r=[b_G], w=[b_gtd])
                cx.op("vector", lambda e: e.reciprocal(out=acc[64:128, :], in_=acc[64:128, :]), r=[bacc], w=[bacc])
                cx.dma(rzl[:, :], acc[64:128, :], r=[bacc], w=[b_rzl])
                cx.op("vector", lambda e: e.tensor_tensor(out=acc[0:64, :], in0=acc[0:64, :], in1=rzl[:, :], op=ALU.mult),
                      r=[b_rzl], w=[bacc])
                cx.op("gpsimd", lambda e: e.tensor_tensor(out=owd[:, :], in0=acc[0:64, :], in1=gtd[:, :], op=ALU.mult),
                      r=[bacc, b_gtd], w=[b_owd])
                cx.dma(WD_d[rr:rr + 64, :], owd[:, :], r=[b_owd], w=[b_WD])
            cx.barrier()

    def phase2b(l):
        with ExitStack() as ph:
            KCm = sb("KCm", [64, 256], BF16, ph)
            VCa = [sb("VCa%d" % i, [128, 128], BF16, ph) for i in range(2)]
            OVb = sb("OVb", [128, 256], BF16, ph)
            Sel = sb("Sel", [12, 12 * 128], F32, ph)
            NGs = sb("NGs", [12, S], F32, ph)
            b_KCm, b_VCa, b_cst, b_NGs = Buf(), Buf(), Buf(), Buf()
            cx.dma(OVb[:], ovb_c[:, :], w=[b_cst])
            cx.dma(Sel[:], sel_c[:, :], w=[b_cst])
            cx.dma(NGs[:], NG_d[:, :], r=[b_NG], w=[b_NGs])
            with ExitStack() as p1_:
                KV = sb("KVs", [128, S], F32, p1_)
                peT = sb("peT", [128, 32], F32, p1_)
                W1s = sb("W1s", [128, 32, 256], F32, p1_)
                W1 = sb("W1", [128, 32, 256], BF16, p1_)
                W2s = sb("W2s", [128, 4, 64], F32, p1_)
                W2 = sb("W2", [128, 4, 64], BF16, p1_)
                X = sb("Xc", [128, 32, 256], BF16, p1_)
                hid = [sb("hid%d" % i, [128, 256], BF16, p1_) for i in range(4)]
                x2 = sb("gx2", [128, 256], F32, p1_)
                u_ = sb("gu", [128, 256], F32, p1_)
                b_KV, b_pe, b_W1s, b_W1, b_W2s, b_W2, b_X, b_x2, b_u = [Buf() for _ in range(9)]
                b_hid = [Buf() for _ in range(4)]
                cx.dma(KV[:], KVC[:, :], r=[b_KVC], w=[b_KV])
                cx.dma(peT[:], cmp_peT[l], w=[b_pe])
                for kv in range(2):
                    cx.dma(W1s[kv * 64:(kv + 1) * 64, :, :], cmp_w1[l, kv].rearrange("(l d) j -> d l j", d=64), w=[b_W1s])
                    cx.dma(W2s[:, kv * 2:kv * 2 + 2, :], cmp_w2[l, kv].rearrange("(jc p) d -> p jc d", p=128), w=[b_W2s])
                cx.op("gpsimd", lambda e: e.tensor_copy(out=W1[:], in_=W1s[:]), r=[b_W1s], w=[b_W1])
                cx.op("gpsimd", lambda e: e.tensor_copy(out=W2[:], in_=W2s[:]), r=[b_W2s], w=[b_W2])
                cx.op("vector", lambda e: e.memset(X[:], 0.0), w=[b_X])
                for ll in range(32):
                    cx.op("vector", lambda e: e.tensor_scalar(out=X[:, ll, 0:255], in0=KV[:, ll:ll + 16 * 254 + 1:16],
                                                              scalar1=peT[:, ll:ll + 1], scalar2=None, op0=ALU.add),
                          r=[b_KV, b_pe], w=[], acc=[b_X])
                for i in range(4):
                    cx.op("gpsimd", lambda e: e.memset(hid[i][:], 0.0), w=[b_hid[i]])
                for i in range(2):
                    cx.op("gpsimd", lambda e: e.memset(VCa[i][:], 1.0), w=[b_VCa])
                cx.op("gpsimd", lambda e: e.memset(KCm[:], 0.0), w=[b_KCm])
                for kv in range(2):
                    for jc in range(2):
                        pp, bpp = PS[kv * 2 + jc], bPS[kv * 2 + jc]
                        for ll in range(32):
                            cx.op("tensor", lambda e: e.matmul(pp[:, 0:255], lhsT=W1[kv * 64:(kv + 1) * 64, ll, jc * 128:(jc + 1) * 128],
                                                               rhs=X[kv * 64:(kv + 1) * 64, ll, 0:255],
                                                               start=(ll == 0), stop=(ll == 31)),
                                  r=[b_W1, b_X] if ll == 0 else [], w=[bpp] if ll == 0 else [], acc=[] if ll == 0 else [bpp])
                        hh = hid[kv * 2 + jc]
                        bh = b_hid[kv * 2 + jc]
                        cx.op("scalar", lambda e: e.activation(out=x2[:, 0:255], in_=pp[:, 0:255], func=AF.Square), r=[bpp], w=[b_x2])
                        cx.op("vector", lambda e: e.tensor_scalar(out=x2[:, 0:255], in0=x2[:, 0:255], scalar1=0.044715, scalar2=1.0,
                                                                  op0=ALU.mult, op1=ALU.add), r=[b_x2], w=[b_x2])
                        cx.op("vector", lambda e: e.tensor_tensor(out=u_[:, 0:255], in0=pp[:, 0:255], in1=x2[:, 0:255], op=ALU.mult),
                              r=[bpp, b_x2], w=[b_u])
                        cx.op("scalar", lambda e: e.activation(out=u_[:, 0:255], in_=u_[:, 0:255], func=AF.Sigmoid,
                                                               scale=1.5957691216057308), r=[b_u], w=[b_u])
                        cx.op("vector", lambda e: e.tensor_tensor(out=hh[:, 0:255], in0=pp[:, 0:255], in1=u_[:, 0:255], op=ALU.mult),
                              r=[bpp, b_u], w=[bh])
                pk, bpk = PS[4], bPS[4]
                for jc in range(2):
                    cx.op("tensor", lambda e: e.matmul(pk[0:64, 0:255], lhsT=W2[:, jc, :], rhs=hid[jc][:, 0:255],
                                                       start=(jc == 0), stop=(jc == 1)),
                          r=[b_W2, b_hid[jc]], w=[bpk] if jc == 0 else [], acc=[] if jc == 0 else [bpk])
                cx.op("scalar", lambda e: e.activation(out=KCm[:, 0:255], in_=pk[0:64, 0:255], func=AF.Copy), r=[bpk], w=[b_KCm])
                for nt in range(2):
                    rows = 128 if nt == 0 else 127
                    pv, bpv = PS[5 + nt], bPS[5 + nt]
                    for jc in range(2):
                        cx.op("tensor", lambda e: e.matmul(pv[0:rows, 0:64], lhsT=hid[2 + jc][:, nt * 128:nt * 128 + rows],
                                                           rhs=W2[:, 2 + jc, :], start=(jc == 0), stop=(jc == 1)),
                              r=[b_W2, b_hid[2 + jc]], w=[bpv] if jc == 0 else [], acc=[] if jc == 0 else [bpv])
                    cx.op("scalar", lambda e: e.activation(out=VCa[nt][0:rows, 64:128], in_=pv[0:rows, 0:64], func=AF.Copy),
                          r=[bpv], w=[b_VCa])
                cx.barrier()
            if debug.get("p2b_level", 9) < 1:
                return
            impT = sb("impT", [64, S], F32, ph)
            b_imp = Buf()
            with ExitStack() as p2_:
                QB4 = sb("QB4", [64, 4, S], BF16, p2_)
                vis = sb("vis", [128, 2, S], BF16, p2_)
                b_QB4, b_vis = Buf(), Buf()
                for h in range(4):
                    cx.dma(QB4[:, h, :], QK[R_QB + 64 * h:R_QB + 64 * (h + 1), :], r=[b_QK], w=[b_QB4])
                for nt in range(2):
                    cx.dma(vis[:, nt, :], vis_c[nt], w=[b_vis])
                Pc = [sb("Pc%d" % i, [128, 512], BF16, p2_) for i in range(4)]
                b_Pc = [Buf() for _ in range(4)]
                rza = [sb("rza%d" % i, [128, 512], F32, p2_) for i in range(2)]
                ocm = [sb("ocm%d" % i, [128, 512], F32, p2_) for i in range(2)]
                imt = [sb("imt%d" % i, [64, 512], F32, p2_) for i in range(2)]
                b_rza, b_ocm, b_imt = [Buf(), Buf()], [Buf(), Buf()], [Buf(), Buf()]
                it = 0
                ei = 0
                for qb in range(8):
                    q0 = qb * 512
                    for h in range(4):
                        nts = [0] if qb < 4 else [0, 1]
                        Ps = []
                        for nt in nts:
                            pS, bpS = PS[it % 2], bPS[it % 2]
                            P, bP = Pc[it % 4], b_Pc[it % 4]
                            it += 1
                            cx.op("tensor", lambda e: e.matmul(pS[:, :], lhsT=KCm[:, nt * 128:(nt + 1) * 128], rhs=QB4[:, h, q0:q0 + 512],
                                                               start=True, stop=True), r=[b_KCm, b_QB4], w=[bpS])
                            cx.op("scalar", lambda e: e.activation(out=P[:, :], in_=pS[:, :], func=AF.Exp, scale=0.125), r=[bpS], w=[bP])
                            cx.op("gpsimd", lambda e: e.tensor_tensor(out=P[:, :], in0=P[:, :], in1=vis[:, nt, q0:q0 + 512], op=ALU.mult),
                                  r=[bP, b_vis], w=[bP])
                            Ps.append((nt, P, bP))
                        j = ei % 2
                        ei += 1
                        pa_, bpa = PS[2 + j], bPS[2 + j]
                        pb_, bpb = PS[4 + j], bPS[4 + j]
                        pg_, bpg = PS[6 + j], bPS[6 + j]
                        for k_, (nt, P, bP) in enumerate(Ps):
                            cx.op("tensor", lambda e: e.matmul(pa_[:, :], lhsT=VCa[nt][:, :], rhs=P[:, :], start=(k_ == 0), stop=(k_ == len(Ps) - 1)),
                                  r=[bP, b_VCa], w=[bpa] if k_ == 0 else [], acc=[] if k_ == 0 else [bpa])
                        for k_, (nt, P, bP) in enumerate(Ps):
                            cx.op("tensor", lambda e: e.matmul(pb_[:, :], lhsT=OVb[:, nt * 128:(nt + 1) * 128], rhs=P[:, :], start=(k_ == 0), stop=(k_ == len(Ps) - 1)),
                                  r=[bP], w=[bpb] if k_ == 0 else [], acc=[] if k_ == 0 else [bpb])
                        cx.op("tensor", lambda e: e.matmul(pg_[:, :], lhsT=Sel[:, (0 * 4 + h) * 128:(0 * 4 + h + 1) * 128], rhs=NGs[:, q0:q0 + 512],
                                                           start=True, stop=True), r=[b_NGs], w=[bpg])
                        RZ, bRZ = rza[j], b_rza[j]
                        cx.op("vector", lambda e: e.tensor_scalar(out=RZ[0:64, :], in0=pa_[0:64, :], scalar1=1e-30, scalar2=None, op0=ALU.max),
                              r=[bpa], w=[bRZ])
                        cx.op("vector", lambda e: e.reciprocal(out=RZ[0:64, :], in_=RZ[0:64, :]), r=[bRZ], w=[bRZ])
                        cx.op("vector", lambda e: e.tensor_scalar(out=RZ[64:128, :], in0=pb_[64:128, :], scalar1=1e-30, scalar2=None, op0=ALU.max),
                              r=[bpb], w=[bRZ])
                        cx.op("vector", lambda e: e.reciprocal(out=RZ[64:128, :], in_=RZ[64:128, :]), r=[bRZ], w=[bRZ])
                        OC, bOC = ocm[j], b_ocm[j]
                        cx.op("vector", lambda e: e.tensor_tensor(out=OC[64:128, :], in0=pa_[64:128, :], in1=RZ[64:128, :], op=ALU.mult),
                              r=[bpa, bRZ], w=[bOC])
                        cx.op("vector", lambda e: e.tensor_tensor(out=OC[64:128, :], in0=pg_[64:128, :], in1=OC[64:128, :], op=ALU.mult),
                              r=[bpg, bOC], w=[bOC])
                        cx.dma(OB_d[0][64 * h:64 * (h + 1), q0:q0 + 512], OC[64:128, :], r=[bOC], w=[b_OB[0]])
                        if h == 0:
                            cx.op("vector", lambda e: e.tensor_tensor(out=impT[:, q0:q0 + 512], in0=pb_[0:64, :], in1=RZ[0:64, :], op=ALU.mult),
                                  r=[bpb, bRZ], w=[b_imp])
                        else:
                            IT, bIT = imt[j], b_imt[j]
                            cx.op("vector", lambda e: e.tensor_tensor(out=IT[:, :], in0=pb_[0:64, :], in1=RZ[0:64, :], op=ALU.mult),
                                  r=[bpb, bRZ], w=[bIT])
                            cx.op("gpsimd", lambda e: e.tensor_tensor(out=impT[:, q0:q0 + 512], in0=impT[:, q0:q0 + 512], in1=IT[:, :], op=ALU.add),
                                  r=[bIT], w=[b_imp])
                cx.barrier()
            if debug.get("p2b_level", 9) < 2:
                return
            QS = [sb("QS%d" % i, [128, S], BF16, ph) for i in range(4)]
            b_QSq = [Buf() for _ in range(4)]
            b_QSb = [Buf() for _ in range(4)]
            for h in range(4):
                cx.dma(QS[h][0:64, :], QK[R_QBR + 64 * h:R_QBR + 64 * (h + 1), :], r=[b_QK], w=[b_QSq[h]])
            with ExitStack() as p3_:
                Fm = sb("Fm", [128, 32 * 64], F32, p3_)
                PENT = sb("PENT", [128, S], BF16, p3_)
                IM = sb("IM", [128, 64], F32, p3_)
                IM2 = sb("IM2", [128, 64], F32, p3_)
                m8a = sb("m8a", [128, 8], F32, p3_)
                m8b = sb("m8b", [128, 8], F32, p3_)
                PT = sb("PTs", [128, 128], F32, p3_)
                b_Fm, b_PENT, b_IM, b_IM2, b_m8a, b_m8b, b_PT = [Buf() for _ in range(7)]
                cx.dma(Fm[:], fm_c[:, :], w=[b_Fm])
                cx.op("vector", lambda e: e.memset(PT[:, 0:64], 0.0), w=[b_PT])
                for tt in range(32):
                    tk = slice(tt * 128, (tt + 1) * 128)
                    fm = Fm[:, tt * 64:(tt + 1) * 64]
                    if tt < 8:
                        cx.op("vector", lambda e: e.tensor_scalar(out=PT[:, 64:128], in0=fm, scalar1=-1e29, scalar2=NEG,
                                                                  op0=ALU.is_lt, op1=ALU.mult), r=[b_Fm], w=[b_PT])
                    else:
                        p1x, bp1x = PS[tt % 2], bPS[tt % 2]
                        cx.op("tensor", lambda e: e.transpose(p1x[0:128, 0:64], impT[0:64, tk], ident[0:64, 0:64]), r=[b_imp], w=[bp1x])
                        cx.op("vector", lambda e: e.tensor_tensor(out=IM[:, :], in0=p1x[:, 0:64], in1=fm, op=ALU.add), r=[bp1x, b_Fm], w=[b_IM])
                        cx.op("vector", lambda e: e.max(out=m8a[:, :], in_=IM[:, :]), r=[b_IM], w=[b_m8a])
                        cx.op("vector", lambda e: e.match_replace(out=IM2[:, :], in_to_replace=m8a[:, :], in_values=IM[:, :], imm_value=-3e30),
                              r=[b_IM, b_m8a], w=[b_IM2])
                        cx.op("vector", lambda e: e.max(out=m8b[:, :], in_=IM2[:, :]), r=[b_IM2], w=[b_m8b])
                        cx.op("vector", lambda e: e.tensor_scalar(out=PT[:, 64:128], in0=IM[:, :], scalar1=m8b[:, 7:8], scalar2=NEG,
                                                                  op0=ALU.is_lt, op1=ALU.mult), r=[b_IM, b_m8b], w=[b_PT])
                    p2x, bp2x = PS[2 + tt % 2], bPS[2 + tt % 2]
                    cx.op("tensor", lambda e: e.transpose(p2x[:, 0:128], PT[:, :], ident), r=[b_PT], w=[bp2x])
                    cx.op("scalar", lambda e: e.activation(out=PENT[64:128, tk], in_=p2x[64:128, 0:128], func=AF.Copy), r=[bp2x], w=[], acc=[b_PENT])
                for h in range(4):
                    cx.op("gpsimd", lambda e: e.tensor_copy(out=QS[h][64:128, :], in_=PENT[64:128, :]), r=[b_PENT], w=[b_QSb[h]])
                cx.barrier()
            if debug.get("p2b_level", 9) < 3:
                return
            with ExitStack() as p4_:
                KS = sb("KS", [128, S], BF16, p4_)
                VS = sb("VS", [128, 32, 128], BF16, p4_)
                KW = sb("KW", [64, S], BF16, p4_)
                VW = sb("VW", [128, 32, 128], BF16, p4_)
                b_KS, b_VS, b_KW, b_VW = Buf(), Buf(), Buf(), Buf()
                cx.dma(KS[0:64, :], QK[R_KSW:R_KSW + 64, :], r=[b_QK], w=[b_KS])
                cx.dma(KS[64:128, :], ind64[:, :], w=[b_KS])
                cx.dma(VS[:, :, :], VB_d[:, 0, :].rearrange("(kt p) c -> p kt c", p=128), r=[b_VB], w=[b_VS])
                cx.dma(KW[:, :], QK[R_KSW + 64:R_KSW + 128, :], r=[b_QK], w=[b_KW])
                cx.dma(VW[:, :, :], VB_d[:, 1, :].rearrange("(kt p) c -> p kt c", p=128), r=[b_VB], w=[b_VW])
                rz = [sb("brz%d" % i, [128, 512], F32, p4_) for i in range(2)]
                on = [sb("bon%d" % i, [64, 512], F32, p4_) for i in range(2)]
                b_rz, b_on = [Buf(), Buf()], [Buf(), Buf()]
                cnt = [0]

                def make_ep(branch, h):
                    def ep(qb, po, bpo):
                        i = cnt[0] % 2
                        cnt[0] += 1
                        q0 = qb * 512
                        pg_, bpg = PS[6 + i], bPS[6 + i]
                        cx.op("tensor", lambda e: e.matmul(pg_[:, :], lhsT=Sel[:, (branch * 4 + h) * 128:(branch * 4 + h + 1) * 128],
                                                           rhs=NGs[:, q0:q0 + 512], start=True, stop=True), r=[b_NGs], w=[bpg])
                        cx.op("vector", lambda e: e.reciprocal(out=rz[i][64:128, :], in_=po[64:128, :]), r=[bpo], w=[b_rz[i]])
                        cx.op("vector", lambda e: e.tensor_tensor(out=on[i][:, :], in0=po[0:64, :], in1=rz[i][64:128, :], op=ALU.mult),
                              r=[bpo, b_rz[i]], w=[b_on[i]])
                        cx.op("vector", lambda e: e.tensor_tensor(out=on[i][:, :], in0=pg_[0:64, :], in1=on[i][:, :], op=ALU.mult),
                              r=[bpg, b_on[i]], w=[b_on[i]])
                        cx.dma(OB_d[branch][64 * h:64 * (h + 1), q0:q0 + 512], on[i][:, :], r=[b_on[i]], w=[b_OB[branch]])
                    return ep
                ptp = make_pt(p4_, "b")
                if debug.get("p2b_level", 9) >= 3:
                    for h in range(debug.get("p2b_heads", 4)):
                        causal_attn(ptp, QS[h], [b_QSq[h], b_QSb[h]], KS, [b_KS], 128, VS, [b_VS], make_ep(1, h))
                if debug.get("p2b_level", 9) >= 4:
                    for h in range(debug.get("p2b_heads", 4)):
                        attn_items(ptp, QS[h], [b_QSq[h]], KW, [b_KW], 64, VW, [b_VW], window_items(), make_ep(2, h))
                cx.barrier()
            if debug.get("p2b_level", 9) < 5:
                return
            with ExitStack() as p5_:
                ta = [sb("cmb%d" % i, [64, S], F32, p5_) for i in range(4)]
                b_ta = [Buf() for _ in range(4)]
                oo = sb("cmbo", [64, S], BF16, p5_)
                b_oo = Buf()
                for h in range(4):
                    rr = 256 + 64 * h
                    for i in range(3):
                        cx.dma(ta[i][:, :], OB_d[i][64 * h:64 * (h + 1), :], r=[b_OB[i]], w=[b_ta[i]])
                    cx.dma(ta[3][:, :], G_d[rr:rr + 64, :], r=[b_G], w=[b_ta[3]])
                    cx.op("vector", lambda e: e.tensor_tensor(out=ta[0][:, :], in0=ta[0][:, :], in1=ta[1][:, :], op=ALU.add),
                          r=[b_ta[1]], w=[b_ta[0]])
                    cx.op("gpsimd", lambda e: e.tensor_tensor(out=ta[0][:, :], in0=ta[0][:, :], in1=ta[2][:, :], op=ALU.add),
                          r=[b_ta[2]], w=[b_ta[0]])
                    cx.op("vector", lambda e: e.tensor_tensor(out=oo[:, :], in0=ta[0][:, :], in1=ta[3][:, :], op=ALU.mult),
                          r=[b_ta[0], b_ta[3]], w=[b_oo])
                    cx.dma(WD_d[rr:rr + 64, :], oo[:, :], r=[b_oo], w=[b_WD])
                cx.barrier()

    def phase3(l, x_src, b_x_src, x_dst, b_x_dst, last):
        with ExitStack() as ph:
            Wm = sb("Wm", [128, 8, 4096], BF16, ph)
            Wu = sb("Wu", [128, 8, 1024], BF16, ph)
            Wo = sb("Wo", [128, 8, 1024], BF16, ph)
            b_W = Buf()
            stg = ExitStack()
            wst = [sb("w3st%d" % i, [128, 8, 512], F32, stg) for i in range(2)]
            b_wst = [Buf(), Buf()]
            k = 0
            for j in range(8):
                cx.dma(wst[k % 2][:], w_in[l][:, C_MG + 512 * j:C_MG + 512 * (j + 1)].rearrange("(kc p) c -> p kc c", p=128),
                       w=[b_wst[k % 2]])
                if j % 2 == 0:
                    cx.op("gpsimd", lambda e: e.tensor_copy(out=Wm[:, :, 512 * j:512 * (j + 1)], in_=wst[k % 2][:]),
                          r=[b_wst[k % 2]], acc=[b_W])
                else:
                    cx.op("vector", lambda e: e.tensor_copy(out=Wm[:, :, 512 * j:512 * (j + 1)], in_=wst[k % 2][:]),
                          r=[b_wst[k % 2]], acc=[b_W])
                k += 1
            for j in range(2):
                cx.dma(wst[k % 2][:], w_up[l].rearrange("b (kc p) d -> p (b kc) d", p=128)[:, :, 512 * j:512 * (j + 1)],
                       w=[b_wst[k % 2]])
                cx.op("gpsimd", lambda e: e.tensor_copy(out=Wu[:, :, 512 * j:512 * (j + 1)], in_=wst[k % 2][:]),
                      r=[b_wst[k % 2]], acc=[b_W])
                k += 1
            for j in range(2):
                cx.dma(wst[k % 2][:], w_out[l][:, 512 * j:512 * (j + 1)].rearrange("(kc p) c -> p kc c", p=128),
                       w=[b_wst[k % 2]])
                cx.op("gpsimd", lambda e: e.tensor_copy(out=Wo[:, :, 512 * j:512 * (j + 1)], in_=wst[k % 2][:]),
                      r=[b_wst[k % 2]], acc=[b_W])
                k += 1
            cx.barrier()
            stg.close()
            hb = [sb("hb%d" % i, [128, 8, 512], BF16, ph) for i in range(2)]
            wb = [sb("wb%d" % i, [128, 8, 512], BF16, ph) for i in range(2)]
            xb1 = sb("xb", [128, 8, 512], F32, ph)
            xb = [xb1, xb1]
            b_xb1 = Buf()
            b_hb, b_wb, b_xb = [Buf(), Buf()], [Buf(), Buf()], [b_xb1, b_xb1]
            yac = sb("yac", [128, 512], F32, ph)
            b_yac = Buf()
            ybf = sb("ybf", [128, 8, 512], BF16, ph)
            b_ybf = Buf()
            sg = [sb("sg%d" % i, [128, 512], F32, ph) for i in range(2)]
            tm = [sb("tm%d" % i, [128, 512], F32, ph) for i in range(2)]
            b_sg, b_tm = [Buf(), Buf()], [Buf(), Buf()]
            xn = sb("xn", [128, 8, 512], F32, ph)
            b_xn = Buf()
            sq = sb("sq3", [128, 8, 512], F32, ph)
            b_sq = Buf()
            rstd = sb("rstd3", [128, 512], F32, ph)
            b_rstd = Buf()
            pi = 0
            ci = 0
            for tb in range(8):
                t0 = tb * 512
                i = tb % 2
                cx.dma(hb[i][:], hT_d[:, t0:t0 + 512].rearrange("(kc p) t -> p kc t", p=128), r=[b_hT_d], w=[b_hb[i]])
                cx.dma(wb[i][:], WD_d[:, t0:t0 + 512].rearrange("(kc p) t -> p kc t", p=128), r=[b_WD], w=[b_wb[i]])
                cx.dma(xb[i][:], x_src[:, t0:t0 + 512].rearrange("(kc p) t -> p kc t", p=128), r=[b_x_src], w=[b_xb[i]])
                for dc in range(8):
                    for br in range(4):
                        pm, bpm = PS[pi % 8], bPS[pi % 8]
                        pu, bpu = PS[(pi + 1) % 8], bPS[(pi + 1) % 8]
                        pi += 2
                        c0 = br * 1024 + dc * 128
                        for kc in range(8):
                            cx.op("tensor", lambda e: e.matmul(pm[:, :], lhsT=Wm[:, kc, c0:c0 + 128], rhs=hb[i][:, kc, :],
                                                               start=(kc == 0), stop=(kc == 7)),
                                  r=[b_W, b_hb[i]] if kc == 0 else [], w=[bpm] if kc == 0 else [],
                                  acc=[] if kc == 0 else [bpm])
                        for kc in range(2):
                            cx.op("tensor", lambda e: e.matmul(pu[:, :], lhsT=Wu[:, br * 2 + kc, dc * 128:(dc + 1) * 128],
                                                               rhs=wb[i][:, br * 2 + kc, :],
                                                               start=(kc == 0), stop=(kc == 1)),
                                  r=[b_W, b_wb[i]] if kc == 0 else [], w=[bpu] if kc == 0 else [],
                                  acc=[] if kc == 0 else [bpu])
                        j = ci % 2
                        ci += 1
                        cx.op("scalar", lambda e: e.activation(out=sg[j][:], in_=pm[:, :], func=AF.Sigmoid),
                              r=[bpm], w=[b_sg[j]])
                        if br == 0:
                            cx.op("vector", lambda e: e.tensor_tensor(out=yac[:], in0=pu[:, :], in1=sg[j][:], op=ALU.mult),
                                  r=[bpu, b_sg[j]], w=[b_yac])
                        else:
                            cx.op("vector", lambda e: e.tensor_tensor(out=tm[j][:], in0=pu[:, :], in1=sg[j][:], op=ALU.mult),
                                  r=[bpu, b_sg[j]], w=[b_tm[j]])
                            if br < 3:
                                cx.op("gpsimd", lambda e: e.tensor_tensor(out=yac[:], in0=yac[:], in1=tm[j][:], op=ALU.add),
                                      r=[b_tm[j]], w=[b_yac])
                            else:
                                cx.op("gpsimd", lambda e: e.tensor_tensor(out=ybf[:, dc, :], in0=yac[:], in1=tm[j][:], op=ALU.add),
                                      r=[b_tm[j], b_yac], w=[b_ybf])
                for dp in range(8):
                    po, bpo = PS[pi % 8], bPS[pi % 8]
                    pi += 1
                    for dc in range(8):
                        cx.op("tensor", lambda e: e.matmul(po[:, :], lhsT=Wo[:, dc, dp * 128:(dp + 1) * 128], rhs=ybf[:, dc, :],
                                                           start=(dc == 0), stop=(dc == 7)),
                              r=[b_W, b_ybf] if dc == 0 else [], w=[bpo] if dc == 0 else [],
                              acc=[] if dc == 0 else [bpo])
                    cx.op("vector", lambda e: e.tensor_tensor(out=xn[:, dp, :], in0=po[:, :], in1=xb[i][:, dp, :], op=ALU.add),
                          r=[bpo, b_xb[i]], w=[b_xn])
                if not last:
                    cx.dma(x_dst[:, t0:t0 + 512].rearrange("(kc p) t -> p kc t", p=128), xn[:], r=[b_xn], w=[b_x_dst])
                else:
                    cx.op("scalar", lambda e: e.activation(out=sq[:], in_=xn[:], func=AF.Square), r=[b_xn], w=[b_sq])
                    pm, bpm = PS[pi % 8], bPS[pi % 8]
                    pi += 1
                    for kc in range(8):
                        cx.op("tensor", lambda e: e.matmul(pm[:, :], lhsT=avg, rhs=sq[:, kc, :], start=(kc == 0), stop=(kc == 7)),
                              r=[b_sq] if kc == 0 else [], w=[bpm] if kc == 0 else [], acc=[] if kc == 0 else [bpm])
                    cx.op("vector", lambda e: e.tensor_scalar(out=rstd[:], in0=pm[:, :], scalar1=1e-6, scalar2=None, op0=ALU.add),
                          r=[bpm], w=[b_rstd])
                    cx.op("scalar", lambda e: e.activation(out=rstd[:], in_=rstd[:], func=AF.Sqrt), r=[b_rstd], w=[b_rstd])
                    cx.op("vector", lambda e: e.reciprocal(out=rstd[:], in_=rstd[:]), r=[b_rstd], w=[b_rstd])
                    for kc in range(8):
                        cx.op("vector", lambda e: e.scalar_tensor_tensor(out=sq[:, kc, :], in0=xn[:, kc, :],
                                                                         scalar=fnw_sb[:, kc:kc + 1], in1=rstd[:],
                                                                         op0=ALU.mult, op1=ALU.mult),
                              r=[b_xn, b_rstd], w=[b_sq])
                    cx.dma(x_dst[:, t0:t0 + 512].rearrange("(kc p) t -> p kc t", p=128), sq[:], r=[b_sq], w=[b_x_dst])
            cx.barrier()

    x_cur, b_x_cur = xT_in, Buf("xT_in")
    layers = debug.get("layers", list(range(DEPTH)))
    phases = debug.get("phases", ["p1", "p1c", "p2a", "p2b", "p2c", "p2d", "p3"])
    b_out = Buf("outT")
    for l in layers:
        if "p1" in phases:
            phase1(l, x_cur, b_x_cur)
        if "p2a" in phases:
            phase2a(l)
        if "p2b" in phases:
            phase2b(l)
            cx.barrier()
        if "p2c" in phases:
            phase2c(l)
        if "p2d" in phases:
            phase2d(l)
        if "p3" in phases:
            last = (l == DEPTH - 1)
            if last:
                phase3(l, x_cur, b_x_cur, outT, b_out, True)
            else:
                phase3(l, x_cur, b_x_cur, xr[l], b_xr[l], False)
                x_cur, b_x_cur = xr[l], b_xr[l]

    cx.barrier()
    cx.n_total = cx.n_inst
    return nc, cx


def host_constants():
    half = 32
    inv = 10000.0 ** (-np.arange(half, dtype=np.float32) / half)
    ang = np.arange(S, dtype=np.float32)[:, None] * inv[None, :]
    cos = np.cos(ang).astype(np.float32).T
    sin = np.sin(ang).astype(np.float32).T
    cos64 = np.concatenate([cos, cos], 0)
    sin64 = np.concatenate([-sin, sin], 0)
    cs2 = np.stack([np.concatenate([cos64, cos64], 0), np.concatenate([sin64, sin64], 0)]).astype(np.float32)
    c_f32 = np.zeros((128, 512), np.float32)
    c_f32[:, 0:128] = np.eye(128, dtype=np.float32)
    c_f32[:, 128:256] = 1.0 / 1024.0
    import ml_dtypes
    bf = ml_dtypes.bfloat16
    sI = np.arange(128)[:, None]
    tI = np.arange(128)[None, :]
    c_bf = np.zeros((128, 1024), np.float32)
    c_bf[:, 0:128] = (sI <= tI)
    c_bf[:, 128:256] = (sI < tI)
    c_bf[:, 256:384] = (sI > tI)
    t2 = np.arange(256)[None, :]
    c_bf[:, 384:640] = (sI <= t2) & (t2 <= sI + 128)
    c_bf[:, 768:896] = -(sI >= tI).astype(np.float32)
    c_bf[:, 896:1024] = -1.0
    ind16 = (np.arange(S)[None, :] // 256 == np.arange(16)[:, None]).astype(np.float32)
    ind64 = (np.arange(S)[None, :] // 64 == np.arange(64)[:, None]).astype(np.float32)
    n_all = np.arange(256)
    vis = ((np.arange(S)[None, :] >= 16 * n_all[:, None] + 31) & (n_all[:, None] < 255)).astype(np.float32)
    vis = vis.reshape(2, 128, S)
    starts = n_all * 16
    jj = np.arange(64)
    ov = ((starts[:, None] < (jj[None, :] + 1) * 64) & (starts[:, None] + 32 > jj[None, :] * 64)).astype(np.float32)
    ov[255] = 0.0
    ovb = np.zeros((128, 256), np.float32)
    for nt in range(2):
        ovb[:, nt * 128:nt * 128 + 64] = ov[nt * 128:(nt + 1) * 128]
        ovb[:, nt * 128 + 64:nt * 128 + 128] = 1.0
    fm = np.zeros((128, 32, 64), np.float32)
    for tt in range(32):
        cur = (tt * 128 + np.arange(128)) // 64
        f = np.zeros((128, 64), np.float32)
        f[jj[None, :] > cur[:, None]] = -1e30
        for p in range(128):
            c = cur[p]
            if c - 1 >= 0:
                f[p, c - 1] = 1e30
            f[p, c] = 2e30
            f[p, 0] = 3e30 if c != 0 else 2e30
        fm[:, tt, :] = f
    sel = np.zeros((12, 12, 128), np.float32)
    for g in range(12):
        sel[g, g, :] = 1.0
    return {"cs2": cs2, "c_f32": c_f32, "c_bf": c_bf.astype(bf), "ind16": ind16.astype(bf),
            "ind64": ind64.astype(bf), "vis_c": vis.astype(bf), "ovb_c": ovb.astype(bf),
            "fm_c": np.ascontiguousarray(fm.reshape(128, 32 * 64)), "sel_c": np.ascontiguousarray(sel.reshape(12, 12 * 128))}


def make_in_maps(x, norm_w, w_in, nsa_cmp_pos, nsa_cmp_w1, nsa_cmp_w2, w_up, w_out, final_norm_w):
    consts = host_constants()
    norm_wT = np.ascontiguousarray(norm_w.reshape(DEPTH, 8, 128).transpose(2, 0, 1).reshape(128, DEPTH * 8))
    fnorm_wT = np.ascontiguousarray(final_norm_w.reshape(8, 128).T)
    cmp_peT = np.ascontiguousarray(nsa_cmp_pos.transpose(0, 1, 3, 2).reshape(DEPTH, 128, 32))
    shared = {"norm_wT": norm_wT, "fnorm_wT": fnorm_wT, "w_in": np.ascontiguousarray(w_in),
              "cmp_peT": cmp_peT, "cmp_w1": np.ascontiguousarray(nsa_cmp_w1),
              "cmp_w2": np.ascontiguousarray(nsa_cmp_w2), "w_up": np.ascontiguousarray(w_up),
              "w_out": np.ascontiguousarray(w_out)}
    shared.update(consts)
    maps = []
    for c in range(x.shape[0]):
        m = dict(shared)
        m["xT"] = np.ascontiguousarray(x[c].T)
        maps.append(m)
    return maps


def kernel(x, norm_w, w_in, nsa_cmp_pos, nsa_cmp_w1, nsa_cmp_w2, w_up, w_out, final_norm_w):
    x = np.asarray(x, np.float32)
    in_maps = make_in_maps(x, np.asarray(norm_w, np.float32), np.asarray(w_in, np.float32),
                           np.asarray(nsa_cmp_pos, np.float32), np.asarray(nsa_cmp_w1, np.float32),
                           np.asarray(nsa_cmp_w2, np.float32), np.asarray(w_up, np.float32),
                           np.asarray(w_out, np.float32), np.asarray(final_norm_w, np.float32))
    nc, cx = build_program()
    res = run_bass_kernel_spmd(nc, in_maps, core_ids=list(range(NCORES)))
    out = np.stack([np.ascontiguousarray(r["outT"].T) for r in res.results], 0)
    return out.astype(np.float32)
```
